# Optimizing a Trainium2 kernel written in Bass

```python
import math
import jax, jax.numpy as jnp
from jax import lax
import numpy as np

D_MODEL = 1024
BATCH = 4
SEQ = 8192
DEPTH = 1

D_LRU = D_MODEL
LRU_BLOCKS = 16
LRU_BW = D_LRU // LRU_BLOCKS
CONV_W = 4
LRU_C = 8.0
N_HEADS = 8
HEAD_DIM = 128
D_ATTN = N_HEADS * HEAD_DIM
BLOCK_Q = 128
D_FF = 4 * D_MODEL
N_BRANCH = 2
RMS_EPS = 1e-6
IN_SPLITS = (D_LRU, D_LRU, D_ATTN, D_ATTN, D_ATTN, N_BRANCH * D_MODEL, N_HEADS)
D_IN = sum(IN_SPLITS)

kernel_name = "hybrid_rglru_fox_gated_block"


def rms_norm(x, g):
    xf = x.astype(jnp.float32)
    y = xf * lax.rsqrt(jnp.mean(xf * xf, axis=-1, keepdims=True) + RMS_EPS)
    return (y * g.astype(jnp.float32)).astype(x.dtype)


def causal_depthwise_conv(x, w, b):
    S = x.shape[1]
    xp = jnp.pad(x, ((0, 0), (CONV_W - 1, 0), (0, 0)))
    out = b
    for k in range(CONV_W):
        out = out + xp[:, k:k + S, :] * w[k]
    return out


def block_diag_linear(x, w, b):
    B, S, _ = x.shape
    xb = x.reshape(B, S, LRU_BLOCKS, LRU_BW)
    y = jnp.einsum('bsnc,ncd->bsnd', xb, w).reshape(B, S, D_LRU)
    return y + b


def rg_lru(x, wa, ba, wx, bx, lam):
    r = jax.nn.sigmoid(block_diag_linear(x, wa, ba).astype(jnp.float32))
    i = jax.nn.sigmoid(block_diag_linear(x, wx, bx).astype(jnp.float32))
    log_a = -LRU_C * r * jax.nn.softplus(-lam.astype(jnp.float32))
    a = jnp.exp(log_a)
    mult = jnp.sqrt(-jnp.expm1(2.0 * log_a))
    bterm = mult * (i * x.astype(jnp.float32))

    def combine(left, right):
        a1, b1 = left
        a2, b2 = right
        return a1 * a2, a2 * b1 + b2

    _, h = lax.associative_scan(combine, (a, bterm), axis=1)
    return h.astype(x.dtype)


def fox_attention(q, k, v, log_f):
    S = q.shape[1]
    scale = 1.0 / math.sqrt(HEAD_DIM)
    F = jnp.cumsum(log_f.astype(jnp.float32), axis=1)
    F = jnp.transpose(F, (0, 2, 1))
    outs = []
    for blk in range(S // BLOCK_Q):
        q0, q1 = blk * BLOCK_Q, (blk + 1) * BLOCK_Q
        qb = q[:, q0:q1]
        kb = k[:, :q1]
        vb = v[:, :q1]
        s = jnp.einsum('bqhd,bkhd->bhqk', qb, kb).astype(jnp.float32) * scale
        s = s + F[:, :, q0:q1, None] - F[:, :, None, :q1]
        q_pos = jnp.arange(q0, q1)
        k_pos = jnp.arange(q1)
        mask = q_pos[:, None] >= k_pos[None, :]
        s = jnp.where(mask[None, None], s, -jnp.inf)
        p = jax.nn.softmax(s, axis=-1).astype(v.dtype)
        outs.append(jnp.einsum('bhqk,bkhd->bqhd', p, vb))
    return jnp.concatenate(outs, axis=1)


def setup_inputs(seed: int = 0) -> dict:
    key = jax.random.key(seed)
    ks = jax.random.split(key, 20)
    f32 = jnp.float32
    n = lambda k, shape, s: jax.random.normal(k, shape, f32) * s
    x = jax.random.normal(ks[0], (BATCH, SEQ, D_MODEL), f32)
    norm_mix_g = 1.0 + n(ks[1], (D_MODEL,), 0.02)
    w_in = n(ks[2], (D_MODEL, D_IN), D_MODEL ** -0.5)
    conv_w = n(ks[3], (CONV_W, D_LRU), CONV_W ** -0.5)
    conv_b = n(ks[4], (D_LRU,), 0.02)
    lru_wa = n(ks[5], (LRU_BLOCKS, LRU_BW, LRU_BW), LRU_BW ** -0.5)
    lru_ba = n(ks[6], (D_LRU,), 0.02)
    lru_wx = n(ks[7], (LRU_BLOCKS, LRU_BW, LRU_BW), LRU_BW ** -0.5)
    lru_bx = n(ks[8], (D_LRU,), 0.02)
    a0 = jax.random.uniform(ks[9], (D_LRU,), f32, 0.9, 0.999)
    s0 = a0 ** (1.0 / LRU_C)
    lru_lambda = jnp.log(s0) - jnp.log1p(-s0)
    forget_b = 2.0 + n(ks[10], (N_HEADS,), 0.5)
    w_branch_a = n(ks[11], (D_LRU, D_MODEL), D_LRU ** -0.5)
    w_branch_b = n(ks[12], (D_ATTN, D_MODEL), D_ATTN ** -0.5)
    w_out = n(ks[13], (D_MODEL, D_MODEL), D_MODEL ** -0.5)
    norm_mlp_g = 1.0 + n(ks[14], (D_MODEL,), 0.02)
    w_up = n(ks[15], (D_MODEL, D_FF), D_MODEL ** -0.5)
    w_down = n(ks[16], (D_FF, D_MODEL), D_FF ** -0.5)
    norm_final_g = 1.0 + n(ks[17], (D_MODEL,), 0.02)
    return {"x": x, "norm_mix_g": norm_mix_g, "w_in": w_in, "conv_w": conv_w,
            "conv_b": conv_b, "lru_wa": lru_wa, "lru_ba": lru_ba, "lru_wx": lru_wx,
            "lru_bx": lru_bx, "lru_lambda": lru_lambda, "forget_b": forget_b,
            "w_branch_a": w_branch_a, "w_branch_b": w_branch_b, "w_out": w_out,
            "norm_mlp_g": norm_mlp_g, "w_up": w_up, "w_down": w_down,
            "norm_final_g": norm_final_g}


def reference(x, norm_mix_g, w_in, conv_w, conv_b, lru_wa, lru_ba, lru_wx, lru_bx,
              lru_lambda, forget_b, w_branch_a, w_branch_b, w_out, norm_mlp_g,
              w_up, w_down, norm_final_g):
    B, S, _ = x.shape
    for _layer in range(DEPTH):
        u = rms_norm(x, norm_mix_g)
        proj = u @ w_in
        cuts = list(np.cumsum(IN_SPLITS)[:-1])
        x_lru, g_lru, q, k, v, gates, f_logit = jnp.split(proj, cuts, axis=-1)

        xa = causal_depthwise_conv(x_lru, conv_w, conv_b)
        ha = rg_lru(xa, lru_wa, lru_ba, lru_wx, lru_bx, lru_lambda)
        ya = (jax.nn.gelu(g_lru) * ha) @ w_branch_a

        log_f = jax.nn.log_sigmoid((f_logit + forget_b).astype(jnp.float32))
        qh = q.reshape(B, S, N_HEADS, HEAD_DIM)
        kh = k.reshape(B, S, N_HEADS, HEAD_DIM)
        vh = v.reshape(B, S, N_HEADS, HEAD_DIM)
        ob = fox_attention(qh, kh, vh, log_f).reshape(B, S, D_ATTN)
        yb = ob @ w_branch_b

        g_a, g_b = jnp.split(jax.nn.sigmoid(gates), N_BRANCH, axis=-1)
        x = x + (g_a * ya + g_b * yb) @ w_out

        m = rms_norm(x, norm_mlp_g)
        h = jnp.square(jax.nn.relu(m @ w_up))
        x = x + h @ w_down
    return rms_norm(x, norm_final_g)
```

```python
import math
from contextlib import ExitStack

import numpy as np
import concourse.bass as bass
import concourse.mybir as mybir
from concourse.bass_utils import run_bass_kernel_spmd

F32 = mybir.dt.float32
BF16 = mybir.dt.bfloat16
AF = mybir.ActivationFunctionType
ALU = mybir.AluOpType
AX = mybir.AxisListType


class _Op:
    __slots__ = ("eng", "fn", "reads", "writes", "sig", "dsem", "key", "cnt", "waits", "clock")

    def __init__(self, eng, fn, reads, writes, sig, dsem):
        self.eng, self.fn, self.reads, self.writes, self.sig, self.dsem = eng, fn, reads, writes, sig, dsem


class Sched:
    ENGS = ("pe", "act", "dve", "pool", "sp")
    SAME_ENGINE_SYNC = True

    def __init__(self, nc):
        self.nc = nc
        self.ops = []
        self.sems = {}
        self.count = {}
        self.last_w = {}
        self.readers = {}
        self.known = {e: {} for e in self.ENGS}
        self.stack = ExitStack()
        self.n_wait = 0
        self.n_ops = 0

    def __enter__(self):
        self.stack.__enter__()
        return self

    def __exit__(self, *a):
        return self.stack.__exit__(*a)

    def _sem(self, key):
        if key not in self.sems:
            self.sems[key] = self.stack.enter_context(self.nc.semaphore("s_" + str(key)))
            self.count[key] = 0
        return self.sems[key]

    def op(self, eng, fn, reads=(), writes=(), sig=True):
        if eng != "pe":
            sig = True
        self.ops.append(_Op(eng, fn, tuple(reads), tuple(writes), sig, None))

    def dma(self, eng, fn, reads=(), writes=(), sem=None):
        assert sem is not None
        self.ops.append(_Op(eng, fn, tuple(reads), tuple(writes), True, sem))

    def capture(self, fn, *a):
        saved, self.ops = self.ops, []
        fn(*a)
        out, self.ops = self.ops, saved
        return out

    def interleave(self, *lists, spans=None):
        if spans is None:
            spans = [1.0] * len(lists)
        pairs = [(l, sp) for l, sp in zip(lists, spans) if l]
        lists = [p[0] for p in pairs]
        spans = [p[1] for p in pairs]
        pos = [0] * len(lists)
        for _ in range(sum(len(l) for l in lists)):
            j = min((spans[j] * pos[j] / len(lists[j]), j) for j in range(len(lists)) if pos[j] < len(lists[j]))[1]
            self.ops.append(lists[j][pos[j]])
            pos[j] += 1

    def run(self, name):
        ops, self.ops = self.ops, []
        self.last_w = {}
        self.readers = {}
        last = {}
        for o in ops:
            if o.dsem is None:
                last[o.eng] = o
        for o in last.values():
            o.sig = True

        def producers(o, last_w, readers):
            ps = []
            for t in o.reads:
                if t in last_w:
                    ps.append(last_w[t])
            for t in o.writes:
                if t in last_w:
                    ps.append(last_w[t])
                ps.extend(readers.get(t, ()))
            return ps

        def update(o, last_w, readers):
            for t in o.writes:
                last_w[t] = o
                readers[t] = []
            for t in o.reads:
                if t not in o.writes:
                    readers.setdefault(t, []).append(o)

        lw, rd = {}, {}
        for o in ops:
            for p in producers(o, lw, rd):
                if p.eng == "pe" and p.dsem is None and not p.sig and not (o.eng == "pe" and o.dsem is None):
                    p.sig = True
            update(o, lw, rd)
        pending = {e: [] for e in self.ENGS}
        for o in ops:
            if o.dsem is not None:
                self._sem(o.dsem)
                o.key = o.dsem
                self.count[o.dsem] += 16 * getattr(o.fn, "n", 1)
                o.cnt = self.count[o.dsem]
            else:
                self._sem(o.eng)
                o.key = o.eng
                pending[o.eng].append(o)
                if o.sig:
                    self.count[o.eng] += 1
                    for p in pending[o.eng]:
                        p.cnt = self.count[o.eng]
                    pending[o.eng] = []
        per_eng = {e: [] for e in self.ENGS}
        for o in ops:
            kn = self.known[o.eng]
            need = {}
            for p in producers(o, self.last_w, self.readers):
                k, c = p.key, p.cnt
                if k == "pe" and o.eng == "pe" and o.dsem is None:
                    continue
                if (not self.SAME_ENGINE_SYNC) and k == o.eng and o.dsem is None and p.dsem is None:
                    continue
                if kn.get(k, 0) >= c:
                    continue
                if need.get(k, (0, None))[0] < c:
                    need[k] = (c, p.clock)
            o.waits = []
            for k, (c, clk) in need.items():
                if kn.get(k, 0) >= c:
                    continue
                o.waits.append((k, c))
                for kk, cc in clk.items():
                    if kn.get(kk, 0) < cc:
                        kn[kk] = cc
            clock = dict(kn)
            clock[o.key] = max(clock.get(o.key, 0), o.cnt)
            o.clock = clock
            update(o, self.last_w, self.readers)
            per_eng[o.eng].append(o)
            self.n_wait += len(o.waits)
        self.n_ops += len(ops)
        final = dict(self.count)

        def emit(engname, e):
            for o in per_eng[engname]:
                for k, c in o.waits:
                    e.wait_ge(self.sems[k], c)
                r = o.fn(e)
                if o.dsem is not None:
                    assert len(r) == getattr(o.fn, "n", 1), (len(r), getattr(o.fn, "n", 1))
                    for ins in r:
                        ins.then_inc(self.sems[o.dsem], 16)
                elif o.sig:
                    r.then_inc(self.sems[o.eng], 1)
            kn = self.known[engname]
            for k, c in final.items():
                if c > 0 and kn.get(k, 0) < c:
                    e.wait_ge(self.sems[k], c)

        with self.nc.Block() as blk:
            blk.tensor(lambda e: emit("pe", e))
            blk.scalar(lambda e: emit("act", e))
            blk.vector(lambda e: emit("dve", e))
            blk.gpsimd(lambda e: emit("pool", e))
            blk.sync(lambda e: emit("sp", e))
        for e in self.ENGS:
            for k, c in final.items():
                if self.known[e].get(k, 0) < c:
                    self.known[e][k] = c


def _n(fn, n):
    fn.n = n
    return fn


SEQ = 8192
D = 1024
KC = 8
NB = SEQ // 128
NT = SEQ // 512
NOWN = NT // 2
OWN = NOWN * 512
DFF = 4096
HC = DFF // 128
NH = 8
D_IN = 7176
C_XL, C_GL, C_Q, C_K, C_V, C_GA, C_GB, C_F = 0, 1024, 2048, 3072, 4096, 5120, 6144, 7168
QSCALE = 1.0 / math.sqrt(128.0)
EPS = 1e-6
MASKV = -30000.0
V_CW, V_CB, V_BA, V_BX, V_LAM, V_G1, V_G2, V_G3 = 0, 4, 5, 6, 7, 8, 9, 10


def build_nc(debug=False, stop_after="C"):
    nc = bass.Bass("TRN2", target_bir_lowering=False)
    dk = "ExternalOutput" if debug else "Internal"

    def din(name, shape, dt=F32):
        return nc.dram_tensor(name, list(shape), dt, kind="ExternalInput").ap()

    def dscr(name, shape, dt=BF16):
        return nc.dram_tensor(name, list(shape), dt, kind=dk).ap()

    x_d = din("x", [SEQ, D])
    xo_d = din("xo", [OWN, D])
    win_d = din("w_in", [D, D_IN])
    vecs_d = din("vecs", [16, D])
    g3_d = din("g3row", [1, D])
    lwa_d = din("lru_wa", [16, 64, 64])
    lwx_d = din("lru_wx", [16, 64, 64])
    fb_d = din("forget_b", [1, NH])
    wa_d = din("w_branch_a", [D, D])
    wb_d = din("w_branch_b", [D, D])
    wo_d = din("w_out", [D, D])
    wup_d = din("w_up", [D, DFF])
    wdn_d = din("w_down", [DFF, D])
    ident_d = din("ident", [128, 128])
    tri_d = din("tri", [128, 128])
    sel_d = din("sel", [128, 2])
    masks_d = din("masks", [8, 128, 512])
    csel_d = din("csel", [16, NH, 128])
    out_d = nc.dram_tensor("out", [OWN, D], F32, kind="ExternalOutput").ap()

    kt_s = dscr("kt_s", [128, NH, SEQ])
    v_s = dscr("v_s", [NH, 128, NB, 128])
    q_s = dscr("q_s", [128, NH, OWN])
    ut_s = dscr("ut_s", [NOWN, 128, KC, 512])
    hs_s = dscr("hs_s", [NOWN, 128, KC, 512])
    ob_s = dscr("ob_s", [NOWN, 128, NH, 512])
    wq_s = dscr("wq_s", [D, D])
    wgl_s = dscr("wgl_s", [4, 128, KC, 256])
    wgg_s = dscr("wgg_s", [8, 128, KC, 256])
    wa_s = dscr("wa_s", [4, 128, KC, 256])
    wb_s = dscr("wb_s", [4, 128, KC, 256])
    wo_s = dscr("wo_s", [4, 128, KC, 256])
    wup_s = dscr("wup_s", [8, 128, KC, 512])
    wdn_s = dscr("wdn_s", [8, 128, KC, 512])
    dbg_d = nc.dram_tensor("dbg", [128, 4096], F32, kind="ExternalOutput").ap() if debug else None

    S = Sched(nc)
    es = ExitStack()

    def sb(name, shape, dt, stack=None):
        return (stack or es).enter_context(nc.sbuf_tensor(name, list(shape), dt))

    def ps_(name, shape, dt, stack=None):
        return (stack or es).enter_context(nc.psum_tensor(name, list(shape), dt))

    with S, es:
        identf = sb("identf", [128, 128], F32)
        identb = sb("identb", [128, 128], BF16)
        onesf = sb("onesf", [128, 128], F32)
        onesb = sb("onesb", [128, 128], BF16)
        trif = sb("trif", [128, 128], F32)
        selt = sb("selt", [128, 2], F32)
        selB = [sb("selB%d" % i, [128, 512], F32) for i in range(2)]
        vecT = sb("vecT", [128, KC, 16], F32)
        dv = sb("dv", [128, KC, 4], F32)
        fbb = sb("fbb", [128, NH], F32)
        E = sb("E", [128, NB, NH], F32)

        with ExitStack() as sA:
            Wx = sb("Wx", [128, KC, D], BF16, sA)
            Wk = sb("Wk", [128, KC, D], BF16, sA)
            Wv = sb("Wv", [128, KC, D], BF16, sA)
            Wf = sb("Wf", [128, KC, NH], BF16, sA)
            diag = sb("diag", [128, 32, 128], BF16, sA)
            BDa = sb("BDa", [128, KC, 128], BF16, sA)
            BDx = sb("BDx", [128, KC, 128], BF16, sA)
            with ExitStack() as s0:
                vecs16 = sb("vecs16", [16, D], F32, s0)
                fbrow = sb("fbrow", [1, NH], F32, s0)
                stg = [sb("stg%d" % i, [128, KC, 512], F32, s0) for i in range(2)]
                gB1 = sb("gB1", [128, KC, 512], F32, s0)
                stgf = sb("stgf", [128, KC, NH], F32, s0)
                tmp8 = sb("tmp8", [128, KC], F32, s0)
                psA = ps_("p0a", [128, 512], F32, s0)
                psB = ps_("p0b", [128, 512], F32, s0)

                S.dma("sp", _n(lambda e: [
                    e.dma_start(out=identf[:], in_=ident_d),
                    e.dma_start(out=trif[:], in_=tri_d),
                    e.dma_start(out=selt[:], in_=sel_d),
                    e.dma_start(out=vecs16[:], in_=vecs_d),
                    e.dma_start(out=fbrow[:], in_=fb_d),
                ], 5), writes=["identf", "trif", "selt", "vecs16", "fbrow"], sem="d_const")
                S.op("pool", lambda e: e.memset(onesf[:], 1.0), writes=["onesf"])
                S.op("pool", lambda e: e.memset(onesb[:], 1.0), writes=["onesb"])
                S.op("pool", lambda e: e.memset(BDa[:], 0.0), writes=["BDa"])
                S.op("pool", lambda e: e.memset(BDx[:], 0.0), writes=["BDx"])
                S.dma("pool", _n(lambda e: [
                    e.dma_start(out=BD[half * 64:(half + 1) * 64, :, half * 64:(half + 1) * 64],
                                in_=src.rearrange("(oc two) c d -> two c oc d", two=2)[half])
                    for BD, src in ((BDa, lwa_d), (BDx, lwx_d)) for half in range(2)
                ], 4), reads=[], writes=["BDa", "BDx"], sem="d_const2")
                S.op("dve", lambda e: e.tensor_copy(out=identb[:], in_=identf[:]), reads=["identf"], writes=["identb"])
                for c in range(KC):
                    S.op("pe", lambda e, c=c: e.transpose(out=psA[:, c * 16:(c + 1) * 16], in_=vecs16[:, c * 128:(c + 1) * 128],
                                                           identity=identf[0:16, 0:16]),
                         reads=["vecs16", "identf"], writes=["psA"], sig=(c == KC - 1))
                S.op("dve", lambda e: e.tensor_copy(out=vecT[:].rearrange("p c r -> p (c r)"), in_=psA[:, 0:128]),
                     reads=["psA"], writes=["vecT"])
                S.op("dve", lambda e: e.tensor_scalar(out=dv[:, :, 0], in0=vecT[:, :, V_BA], scalar1=0.5, scalar2=None, op0=ALU.mult),
                     reads=["vecT"], writes=["dv0"])
                S.op("dve", lambda e: e.tensor_scalar(out=dv[:, :, 1], in0=vecT[:, :, V_BX], scalar1=0.5, scalar2=None, op0=ALU.mult),
                     reads=["vecT"], writes=["dv1"])
                S.op("act", lambda e: e.activation(out=tmp8[:], in_=vecT[:, :, V_LAM], func=AF.Exp, scale=-1.0), reads=["vecT"], writes=["tmp8"])
                S.op("act", lambda e: e.activation(out=tmp8[:], in_=tmp8[:], func=AF.Ln, bias=1.0, scale=1.0), reads=["tmp8"], writes=["tmp8"])
                S.op("dve", lambda e: e.tensor_scalar(out=dv[:, :, 2], in0=tmp8[:], scalar1=-4.0, scalar2=None, op0=ALU.mult),
                     reads=["tmp8"], writes=["dv2"])
                S.op("pe", lambda e: e.matmul(psB[:, 0:NH], lhsT=onesf[0:1, :], rhs=fbrow[0:1, :], start=True, stop=True),
                     reads=["onesf", "fbrow"], writes=["psB"])
                S.op("dve", lambda e: e.tensor_copy(out=fbb[:], in_=psB[:, 0:NH]), reads=["psB"], writes=["fbb"])
                for oc in range(KC):
                    for k in range(4):
                        S.op("dve", lambda e, oc=oc, k=k: e.tensor_scalar(out=diag[:, oc * 4 + k, :], in0=identf[:], scalar1=vecT[:, oc, V_CW + k:V_CW + k + 1],
                                                                       scalar2=None, op0=ALU.mult),
                             reads=["identf", "vecT"], writes=["diag"])

                S.op("pool", lambda e: e.memset(gB1[:], 1.0), writes=["gB1"])
                for kc in range(KC):
                    S.op("dve", lambda e, kc=kc: e.tensor_scalar(out=gB1[:, kc, :], in0=gB1[:, kc, :], scalar1=vecT[:, kc, V_G1:V_G1 + 1], scalar2=None, op0=ALU.mult),
                         reads=["gB1", "vecT"], writes=["gB1"])
                for j in range(2):
                    S.op("pool", lambda e, j=j: e.memset(selB[j][:], 1.0), writes=[("selB", j)])
                    S.op("dve", lambda e, j=j: e.tensor_scalar(out=selB[j][:], in0=selB[j][:], scalar1=selt[:, j:j + 1], scalar2=None, op0=ALU.mult),
                         reads=[("selB", j), "selt"], writes=[("selB", j)])
                fwc = [0]

                def fold_weight(dst, src_d, c0, ncols, gidx, tag, q="sp"):
                    for hf in range((ncols + 511) // 512):
                        w = min(512, ncols - hf * 512)
                        sl = fwc[0] % 2
                        fwc[0] += 1
                        S.dma(q, lambda e, sl=sl, hf=hf, w=w: [e.dma_start(
                            out=stg[sl][:, :, 0:w], in_=src_d[:, c0 + hf * 512:c0 + hf * 512 + w].rearrange("(kc p) n -> p kc n", p=128))],
                            writes=[("stg", sl)], sem="d_stg%d" % sl)
                        eng = "dve" if sl == 0 else "pool"
                        S.op(eng, lambda e, sl=sl, hf=hf, w=w: e.tensor_tensor(out=dst[:, :, hf * 512:hf * 512 + w], in0=stg[sl][:, :, 0:w], in1=gB1[:, :, 0:w], op=ALU.mult),
                             reads=[("stg", sl), "gB1"], writes=[tag])

                fold_weight(Wk, win_d, C_K, D, V_G1, "Wk")
                fold_weight(Wv, win_d, C_V, D, V_G1, "Wv")
                fold_weight(Wx, win_d, C_XL, D, V_G1, "Wx")
                fold_weight(Wf, win_d, C_F, NH, V_G1, "Wf")
                S.run("P0")

            with ExitStack() as s1:
                xt = [sb("xt%d" % i, [128, D], F32, s1) for i in range(4)]
                xn = [sb("xn%d" % i, [128, D], BF16, s1) for i in range(4)]
                junk = sb("junk", [128, D], BF16, s1)
                ss = sb("ss", [128, 2, 4], F32, s1)
                rstd = sb("rstd", [128, 2, 4], F32, s1)
                uT = [sb("uT%d" % i, [128, KC, 512], BF16, s1) for i in range(2)]
                kst = sb("kst", [128, NH, 512], BF16, s1)
                vst = sb("vst", [128, 4, D], BF16, s1)
                xl = [sb("xl%d" % i, [128, KC, 515], BF16, s1) for i in range(2)]
                xa = [sb("xa%d" % i, [128, 512], BF16, s1) for i in range(KC)]
                Abuf = sb("Abuf", [128, KC, 512], F32, s1)
                A2buf = sb("A2buf", [128, KC, 512], F32, s1)
                Tbuf = sb("Tbuf", [128, KC, 512], BF16, s1)
                TH = [sb("TH%d" % i, [128, 512], F32, s1) for i in range(2)]
                Hf = [sb("Hf%d" % i, [128, 512], F32, s1) for i in range(2)]
                hst = sb("hst", [128, KC], F32, s1)
                Hsel = sb("Hsel", [128, KC, 512], BF16, s1)
                zt = sb("zt", [128, NH], F32, s1)
                pT = [ps_("pT%d" % i, [128, KC, 128], BF16, s1) for i in range(2)]
                pM = [ps_("pM%d" % i, [128, 512], F32, s1) for i in range(5)]
                pF = ps_("pF", [128, NH], F32, s1)
                pmc = [0, 0]

                def next_pm():
                    k = pmc[0] % 3
                    pmc[0] += 1
                    return k

                def next_pl():
                    k = 3 + pmc[1] % 2
                    pmc[1] += 1
                    return k

                S.op("pool", lambda e: e.memset(xl[0][:, :, 0:3], 0.0), writes=[("xl", 0, oc) for oc in range(KC)])

                def front(t):
                    p = t % 2
                    for tb in range(4):
                        blk = 4 * t + tb
                        S.dma("sp", lambda e, tb=tb, blk=blk: [e.dma_start(out=xt[tb][:], in_=x_d[blk * 128:(blk + 1) * 128, :])],
                              writes=[("xt", tb)], sem="d_xt%d" % tb)
                        S.op("act", lambda e, tb=tb, p=p: e.activation(out=junk[:], in_=xt[tb][:], func=AF.Square, accum_out=ss[:, p, tb:tb + 1]),
                             reads=[("xt", tb)], writes=["junk", ("ss", p, tb)])
                    S.op("dve", lambda e, p=p: e.tensor_scalar(out=rstd[:, p, :], in0=ss[:, p, :], scalar1=1.0 / D, scalar2=EPS, op0=ALU.mult, op1=ALU.add),
                         reads=[("ss", p, tb) for tb in range(4)], writes=[("rstd", p)])
                    S.op("act", lambda e, p=p: e.activation(out=rstd[:, p, :], in_=rstd[:, p, :], func=AF.Sqrt), reads=[("rstd", p)], writes=[("rstd", p)])
                    S.op("dve", lambda e, p=p: e.reciprocal(out=rstd[:, p, :], in_=rstd[:, p, :]), reads=[("rstd", p)], writes=[("rstd", p)])
                    for tb in range(4):
                        S.op("act", lambda e, tb=tb, p=p: e.mul(out=xn[tb][:], in_=xt[tb][:], mul=rstd[:, p, tb:tb + 1]),
                             reads=[("xt", tb), ("rstd", p)], writes=[("xn", tb)])
                    for tb in range(4):
                        q = tb % 2
                        for c in range(KC):
                            S.op("pe", lambda e, q=q, c=c, tb=tb: e.transpose(out=pT[q][:, c, :], in_=xn[tb][:, c * 128:(c + 1) * 128], identity=identb[:]),
                                 reads=[("xn", tb), "identb"], writes=[("pT", q)], sig=(c == KC - 1))
                        S.op("dve", lambda e, q=q, p=p, tb=tb: e.tensor_copy(out=uT[p][:, :, tb * 128:(tb + 1) * 128], in_=pT[q][:]),
                             reads=[("pT", q)], writes=[("uT", p, tb)])

                def mid(t):
                    p = t % 2
                    uTr = [("uT", p, tb) for tb in range(4)]
                    for h in range(NH):
                        k = next_pm()
                        for kc in range(KC):
                            S.op("pe", lambda e, k=k, kc=kc, h=h, p=p: e.matmul(pM[k][:], lhsT=Wk[:, kc, h * 128:(h + 1) * 128], rhs=uT[p][:, kc, :],
                                                                           start=(kc == 0), stop=(kc == KC - 1)),
                                 reads=uTr + ["Wk"], writes=[("pM", k)], sig=(kc == KC - 1))
                        if h % 2 == 0:
                            S.op("act", lambda e, k=k, h=h: e.copy(out=kst[:, h, :], in_=pM[k][:]), reads=[("pM", k)], writes=[("kst", h)])
                        else:
                            S.op("dve", lambda e, k=k, h=h: e.tensor_copy(out=kst[:, h, :], in_=pM[k][:]), reads=[("pM", k)], writes=[("kst", h)])
                    S.dma("sp", lambda e, t=t: [e.dma_start(out=kt_s[:, :, t * 512:(t + 1) * 512], in_=kst[:])],
                          reads=[("kst", h) for h in range(NH)], writes=["kt_s"], sem="d_kst")
                    for tb in range(4):
                        for hf in range(2):
                            k = next_pm()
                            for kc in range(KC):
                                S.op("pe", lambda e, k=k, kc=kc, tb=tb, hf=hf, p=p: e.matmul(pM[k][:], lhsT=uT[p][:, kc, tb * 128:(tb + 1) * 128],
                                                                                        rhs=Wv[:, kc, hf * 512:(hf + 1) * 512], start=(kc == 0), stop=(kc == KC - 1)),
                                     reads=[("uT", p, tb), "Wv"], writes=[("pM", k)], sig=(kc == KC - 1))
                            if hf == 0:
                                S.op("dve", lambda e, k=k, tb=tb, hf=hf: e.tensor_copy(out=vst[:, tb, hf * 512:(hf + 1) * 512], in_=pM[k][:]),
                                     reads=[("pM", k)], writes=[("vst", tb)])
                            else:
                                S.op("act", lambda e, k=k, tb=tb, hf=hf: e.copy(out=vst[:, tb, hf * 512:(hf + 1) * 512], in_=pM[k][:]),
                                     reads=[("pM", k)], writes=[("vst", tb)])
                        for kc in range(KC):
                            S.op("pe", lambda e, kc=kc, tb=tb, p=p: e.matmul(pF[:], lhsT=uT[p][:, kc, tb * 128:(tb + 1) * 128], rhs=Wf[:, kc, :],
                                                                        start=(kc == 0), stop=(kc == KC - 1)),
                                 reads=[("uT", p, tb), "Wf"], writes=["pF"], sig=(kc == KC - 1))
                        S.op("dve", lambda e: e.tensor_tensor(out=zt[:], in0=pF[:], in1=fbb[:], op=ALU.add), reads=["pF", "fbb"], writes=["zt"])
                        S.op("act", lambda e, blk=4 * t + tb: e.activation(out=E[:, blk, :], in_=zt[:], func=AF.Exp, scale=-1.0), reads=["zt"], writes=["E"])
                    S.dma("sp", _n(lambda e, t=t: [e.dma_start(out=v_s[:, :, 4 * t + tb, :].rearrange("h p d -> p h d"),
                                                               in_=vst[:, tb, :].rearrange("p (h d) -> p h d", h=NH)) for tb in range(4)], 4),
                          reads=[("vst", tb) for tb in range(4)], writes=["v_s"], sem="d_vst")
                    for oc in range(KC):
                        k = next_pm()
                        for kc in range(KC):
                            S.op("pe", lambda e, k=k, kc=kc, oc=oc, p=p: e.matmul(pM[k][:], lhsT=Wx[:, kc, oc * 128:(oc + 1) * 128], rhs=uT[p][:, kc, :],
                                                                            start=(kc == 0), stop=(kc == KC - 1)),
                                 reads=uTr + ["Wx"], writes=[("pM", k)], sig=(kc == KC - 1))
                        eng = "act" if oc % 2 == 0 else "dve"
                        if eng == "act":
                            S.op("act", lambda e, k=k, oc=oc, p=p: e.copy(out=xl[p][:, oc, 3:515], in_=pM[k][:]), reads=[("pM", k)], writes=[("xl", p, oc)])
                        else:
                            S.op("dve", lambda e, k=k, oc=oc, p=p: e.tensor_copy(out=xl[p][:, oc, 3:515], in_=pM[k][:]), reads=[("pM", k)], writes=[("xl", p, oc)])

                def lru1(t):
                    p = t % 2
                    for oc in range(KC):
                        k = next_pl()
                        for j in range(4):
                            S.op("pe", lambda e, k=k, j=j, oc=oc, p=p: e.matmul(pM[k][:], lhsT=diag[:, oc * 4 + j, :], rhs=xl[p][:, oc, j:j + 512], start=(j == 0), stop=(j == 3)),
                                 reads=[("xl", p, oc), "diag"], writes=[("pM", k)], sig=(j == 3))
                        S.op("act", lambda e, k=k, oc=oc: e.activation(out=xa[oc][:], in_=pM[k][:], func=AF.Identity, bias=vecT[:, oc, V_CB:V_CB + 1], scale=1.0),
                             reads=[("pM", k), "vecT"], writes=[("xa", oc)])
                        S.op("pool", lambda e, oc=oc, p=p: e.tensor_copy(out=xl[1 - p][:, oc, 0:3], in_=xl[p][:, oc, 512:515]), reads=[("xl", p, oc)], writes=[("xl", 1 - p, oc)])
                    for oc in range(KC):
                        q = oc % 2
                        kr = next_pl()
                        S.op("pe", lambda e, kr=kr, oc=oc: e.matmul(pM[kr][:], lhsT=BDa[:, oc, :], rhs=xa[oc][:], start=True, stop=True),
                             reads=[("xa", oc), "BDa"], writes=[("pM", kr)])
                        ki = next_pl()
                        S.op("pe", lambda e, ki=ki, oc=oc: e.matmul(pM[ki][:], lhsT=BDx[:, oc, :], rhs=xa[oc][:], start=True, stop=True),
                             reads=[("xa", oc), "BDx"], writes=[("pM", ki)])
                        S.op("act", lambda e, kr=kr, oc=oc: e.activation(out=Abuf[:, oc, :], in_=pM[kr][:], func=AF.Tanh, bias=dv[:, oc, 0:1], scale=0.5),
                             reads=[("pM", kr), "dv0"], writes=[("A", oc)])
                        S.op("act", lambda e, ki=ki, oc=oc, q=q: e.activation(out=TH[q][:], in_=pM[ki][:], func=AF.Tanh, bias=dv[:, oc, 1:2], scale=0.5),
                             reads=[("pM", ki), "dv1"], writes=[("TH", q)])
                        S.op("act", lambda e, oc=oc: e.activation(out=Abuf[:, oc, :], in_=Abuf[:, oc, :], func=AF.Exp, bias=dv[:, oc, 2:3], scale=dv[:, oc, 2:3]),
                             reads=[("A", oc), "dv2"], writes=[("A", oc)])
                        S.op("pool", lambda e, oc=oc: e.tensor_tensor(out=A2buf[:, oc, :], in0=Abuf[:, oc, :], in1=Abuf[:, oc, :], op=ALU.mult),
                             reads=[("A", oc)], writes=[("A2", oc)])
                        S.op("dve", lambda e, oc=oc, q=q: e.scalar_tensor_tensor(out=Tbuf[:, oc, :], in0=TH[q][:], scalar=1.0, in1=xa[oc][:], op0=ALU.add, op1=ALU.mult),
                             reads=[("TH", q), ("xa", oc)], writes=[("T", oc)])

                def lru2(t):
                    p = t % 2
                    i = t // 2
                    for oc in range(KC):
                        S.op("act", lambda e, oc=oc: e.activation(out=A2buf[:, oc, :], in_=A2buf[:, oc, :], func=AF.Sqrt, bias=0.25, scale=-0.25),
                             reads=[("A2", oc)], writes=[("A2", oc)])
                    for oc in range(KC):
                        S.op("dve", lambda e, oc=oc: e.tensor_tensor(out=A2buf[:, oc, :], in0=A2buf[:, oc, :], in1=Tbuf[:, oc, :], op=ALU.mult),
                             reads=[("A2", oc), ("T", oc)], writes=[("A2", oc)])
                    for oc in range(KC):
                        q = oc % 2
                        init = 0.0 if t == 0 else hst[:, oc:oc + 1]
                        S.op("dve", lambda e, oc=oc, q=q, init=init: e.tensor_tensor_scan(out=Hf[q][:], data0=Abuf[:, oc, :], data1=A2buf[:, oc, :], initial=init,
                                                                                      op0=ALU.mult, op1=ALU.add),
                             reads=[("A", oc), ("A2", oc), ("hst", oc)], writes=[("Hf", q)])
                        S.op("act", lambda e, oc=oc, q=q: e.copy(out=hst[:, oc:oc + 1], in_=Hf[q][:, 511:512]), reads=[("Hf", q)], writes=[("hst", oc)])
                        if p == 0:
                            S.op("pool", lambda e, oc=oc, q=q: e.tensor_tensor(out=Hsel[:, oc, :], in0=Hf[q][:], in1=selB[0][:], op=ALU.mult),
                                 reads=[("Hf", q), ("selB", 0)], writes=[("Hsel", oc)])
                        else:
                            S.op("pool", lambda e, oc=oc, q=q: e.tensor_tensor(out=Hf[q][:], in0=Hf[q][:], in1=selB[1][:], op=ALU.mult),
                                 reads=[("Hf", q), ("selB", 1)], writes=[("Hf", q)])
                            S.op("pool", lambda e, oc=oc, q=q: e.tensor_tensor(out=Hsel[:, oc, :], in0=Hsel[:, oc, :], in1=Hf[q][:], op=ALU.add),
                                 reads=[("Hf", q), ("Hsel", oc)], writes=[("Hsel", oc)])
                    if p == 1:
                        S.dma("sp", lambda e, i=i: [e.dma_start(out=hs_s[i], in_=Hsel[:])],
                              reads=[("Hsel", oc) for oc in range(KC)], writes=["hs_s"], sem="d_hsel")

                nt_run = NT if stop_after != "A1x" else 2
                def lru(t):
                    lru1(t)
                    lru2(t)

                front(0)
                mid(0)
                front(1)
                for t in range(1, nt_run):
                    S.interleave(S.capture(mid, t), S.capture(front, t + 1) if t + 1 < nt_run else [], S.capture(lru, t - 1), spans=[1.0, 0.55, 1.0])
                lru(nt_run - 1)
                S.run("A1")
                if stop_after in ("A1", "A1x"):
                    S.dma("sp", lambda e: [e.dma_start(out=dbg_d[:, 0:512], in_=E[:].rearrange("p b h -> p (b h)"))], reads=["E"], writes=["dbg"], sem="d_dbg")
                    S.run("dbgA1")
                    return nc

            with ExitStack() as s2:
                Wq = sb("Wq", [128, KC, D], BF16, s2)
                ut2 = [sb("ut2_%d" % i, [128, KC, 512], BF16, s2) for i in range(2)]
                qst = [sb("qst%d" % i, [128, NH, 512], BF16, s2) for i in range(2)]
                pQ = [ps_("pQ%d" % i, [128, 512], F32, s2) for i in range(4)]
                S.dma("pool", _n(lambda e: [e.dma_start(out=Wq[:, :, hf * 512:(hf + 1) * 512],
                                                        in_=win_d[:, C_Q + hf * 512:C_Q + (hf + 1) * 512].rearrange("(kc p) n -> p kc n", p=128)) for hf in range(2)], 2),
                      writes=["Wq"], sem="d_wq")
                xt2 = sb("xtA2", [128, 4, D], F32, s2)
                xnA = [sb("xnA%d" % i, [128, D], BF16, s2) for i in range(2)]
                junkA = sb("junkA", [128, D], BF16, s2)
                ssA = sb("ssA", [128, 4], F32, s2)
                rsA = sb("rsA", [128, 4], F32, s2)
                pTA = [ps_("pTA%d" % i, [128, KC, 128], BF16, s2) for i in range(2)]
                cqc = [0]

                def a2_front(i):
                    sl = i % 2
                    S.dma("sp", lambda e, i=i: [e.dma_start(out=xt2[:], in_=xo_d[i * 512:(i + 1) * 512, :].rearrange("(tb p) c -> p tb c", p=128))],
                          writes=[("xt2", tb) for tb in range(4)], sem="d_xt2")
                    for tb in range(4):
                        S.op("act", lambda e, tb=tb: e.activation(out=junkA[:], in_=xt2[:, tb, :], func=AF.Square, accum_out=ssA[:, tb:tb + 1]),
                             reads=[("xt2", tb)], writes=["junkA", ("ssA", tb)])
                    S.op("dve", lambda e: e.tensor_scalar(out=rsA[:], in0=ssA[:], scalar1=1.0 / D, scalar2=EPS, op0=ALU.mult, op1=ALU.add),
                         reads=[("ssA", tb) for tb in range(4)], writes=["rsA"])
                    S.op("act", lambda e: e.activation(out=rsA[:], in_=rsA[:], func=AF.Sqrt), reads=["rsA"], writes=["rsA"])
                    S.op("dve", lambda e: e.reciprocal(out=rsA[:], in_=rsA[:]), reads=["rsA"], writes=["rsA"])
                    for tb in range(4):
                        q = tb % 2
                        S.op("act", lambda e, tb=tb, q=q: e.mul(out=xnA[q][:], in_=xt2[:, tb, :], mul=rsA[:, tb:tb + 1]),
                             reads=[("xt2", tb), "rsA"], writes=[("xnA", q)])
                        for c in range(KC):
                            S.op("pe", lambda e, q=q, c=c: e.transpose(out=pTA[q][:, c, :], in_=xnA[q][:, c * 128:(c + 1) * 128], identity=identb[:]),
                                 reads=[("xnA", q), "identb"], writes=[("pTA", q)], sig=(c == KC - 1))
                        for c in range(KC):
                            S.op("dve", lambda e, q=q, sl=sl, tb=tb, c=c: e.tensor_scalar(out=ut2[sl][:, c, tb * 128:(tb + 1) * 128], in0=pTA[q][:, c, :],
                                                                                       scalar1=vecT[:, c, V_G1:V_G1 + 1], scalar2=None, op0=ALU.mult),
                                 reads=[("pTA", q), "vecT"], writes=[("ut2", sl, tb, c)])
                    S.dma("sp", lambda e, i=i, sl=sl: [e.dma_start(out=ut_s[i], in_=ut2[sl][:])],
                          reads=[("ut2", sl, tb, c) for tb in range(4) for c in range(KC)], writes=["ut_s"], sem="d_uts%d" % sl)

                def a2_q(i):
                    sl = i % 2
                    for h in range(NH):
                        k = cqc[0] % 4
                        cqc[0] += 1
                        for kc in range(KC):
                            S.op("pe", lambda e, k=k, kc=kc, h=h, sl=sl: e.matmul(pQ[k][:], lhsT=Wq[:, kc, h * 128:(h + 1) * 128], rhs=ut2[sl][:, kc, :],
                                                                            start=(kc == 0), stop=(kc == KC - 1)),
                                 reads=[("ut2", sl, tb, kc) for tb in range(4)] + ["Wq"], writes=[("pQ", k)], sig=(kc == KC - 1))
                        if h % 2 == 0:
                            S.op("act", lambda e, k=k, h=h, sl=sl: e.mul(out=qst[sl][:, h, :], in_=pQ[k][:], mul=QSCALE), reads=[("pQ", k)], writes=[("qst", sl, h)])
                        else:
                            S.op("dve", lambda e, k=k, h=h, sl=sl: e.tensor_scalar(out=qst[sl][:, h, :], in0=pQ[k][:], scalar1=QSCALE, scalar2=None, op0=ALU.mult),
                                 reads=[("pQ", k)], writes=[("qst", sl, h)])
                    S.dma("sp", lambda e, i=i, sl=sl: [e.dma_start(out=q_s[:, :, i * 512:(i + 1) * 512], in_=qst[sl][:])],
                          reads=[("qst", sl, h) for h in range(NH)], writes=["q_s"], sem="d_qst%d" % sl)

                a2_front(0)
                for i in range(NOWN):
                    S.interleave(S.capture(a2_q, i), S.capture(a2_front, i + 1) if i + 1 < NOWN else [])
                S.run("A2")
                if stop_after == "A2":
                    return nc

        with ExitStack() as sB:
            negF = sb("negF", [128, NB, NH], F32, sB)
            Fb = sb("Fb", [128, NOWN, 4, 2, NH], BF16, sB)
            cT = sb("cT", [128, NOWN, 512], BF16, sB)
            maskb = sb("maskb", [128, 8, 512], BF16, sB)
            cselb = sb("cselb", [128, NH, 128], BF16, sB)
            with ExitStack() as sb0:
                tot = sb("tot", [128, NB, NH], F32, sb0)
                inc = sb("inc", [128, NB, NH], F32, sb0)
                ones64 = sb("ones64", [128, NB], F32, sb0)
                pc1 = ps_("pc1", [128, 512], F32, sb0)
                pc2 = ps_("pc2", [128, 512], F32, sb0)
                pX = ps_("pX", [128, 512], BF16, sb0)
                Efl = E[:].rearrange("p b h -> p (b h)")
                S.op("pool", lambda e: e.memset(cT[:], 0.0), writes=["cT"])
                S.op("pool", lambda e: e.memset(cselb[:], 0.0), writes=["cselb"])
                S.op("pool", lambda e: e.memset(ones64[:], 1.0), writes=["ones64"])
                S.dma("pool", _n(lambda e: [
                    e.dma_start(out=maskb[:], in_=masks_d.rearrange("m p q -> p m q")),
                    e.dma_start(out=cselb[0:16, :, :], in_=csel_d),
                ], 2), reads=[], writes=["maskb", "cselb"], sem="d_constB")
                S.op("act", lambda e: e.activation(out=Efl, in_=Efl, func=AF.Ln, bias=1.0, scale=1.0), reads=["E"], writes=["E"])
                S.op("pe", lambda e: e.matmul(pc1[:], lhsT=trif[:], rhs=Efl, start=True, stop=True), reads=["E", "trif"], writes=["pc1"])
                S.op("pe", lambda e: e.matmul(pc2[:], lhsT=onesf[:], rhs=Efl, start=True, stop=True), reads=["E", "onesf"], writes=["pc2"])
                S.op("dve", lambda e: e.tensor_copy(out=tot[:].rearrange("p b h -> p (b h)"), in_=pc2[:]), reads=["pc2"], writes=["tot"])
                for h in range(NH):
                    S.op("dve", lambda e, h=h: e.tensor_tensor_scan(out=inc[:, :, h], data0=ones64[:], data1=tot[:, :, h], initial=0.0, op0=ALU.mult, op1=ALU.add),
                         reads=["tot", "ones64"], writes=[("inc", h)])
                S.op("dve", lambda e: e.tensor_tensor(out=inc[:], in0=inc[:], in1=tot[:], op=ALU.subtract),
                     reads=[("inc", h) for h in range(NH)] + ["tot"], writes=["incx"])
                S.op("dve", lambda e: e.tensor_tensor(out=negF[:].rearrange("p b h -> p (b h)"), in0=pc1[:], in1=inc[:].rearrange("p b h -> p (b h)"), op=ALU.add),
                     reads=["pc1", "incx"], writes=["negF"])
                for i in range(NOWN):
                    S.op("dve", lambda e, i=i: e.tensor_scalar(out=Fb[:, i], in0=negF[:, 8 * i:8 * i + 8, :].rearrange("p (p2 tb) h -> p tb p2 h", p2=2),
                                                            scalar1=-1.0, scalar2=None, op0=ALU.mult),
                         reads=["negF"], writes=[("Fb", i)])
                    for tb in range(4):
                        S.op("pe", lambda e, i=i, tb=tb: e.transpose(out=pX[0:16, tb * 128:(tb + 1) * 128], in_=Fb[:, i, tb].rearrange("p a h -> p (a h)"), identity=identb[:]),
                             reads=[("Fb", i), "identb"], writes=["pX"], sig=(tb == 3))
                    S.op("dve", lambda e, i=i: e.tensor_copy(out=cT[0:16, i, :], in_=pX[0:16, :]), reads=["pX"], writes=["cT"])
                if debug:
                    S.dma("sp", _n(lambda e: [e.dma_start(out=dbg_d[:, 512:1024], in_=negF[:].rearrange("p b h -> p (b h)")),
                                              e.dma_start(out=dbg_d[0:16, 1024:1536], in_=cT[0:16, 0, :].bitcast(F32) if False else negF[0:16, :, :].rearrange("p b h -> p (b h)"))], 2),
                          reads=["negF", "cT"], writes=["dbg"], sem="d_dbg")
                S.run("B0")

            with ExitStack() as sb1:
                KT = [sb("KT%d" % i, [128, SEQ], BF16, sb1) for i in range(2)]
                Vh = [sb("Vh%d" % i, [128, NB, 128], BF16, sb1) for i in range(2)]
                Qh = [sb("Qh%d" % i, [128, OWN], BF16, sb1) for i in range(2)]
                pt = [sb("pt%d" % i, [128, 512], BF16, sb1) for i in range(3)]
                rl = [sb("rl%d" % i, [128, 512], F32, sb1) for i in range(2)]
                acc = [sb("acc%d" % i, [128, 512], F32, sb1) for i in range(2)]
                obst = [sb("obst%d" % i, [128, 512], BF16, sb1) for i in range(2)]
                pS = [ps_("pS%d" % i, [128, 512], F32, sb1) for i in range(4)]
                pO = [ps_("pO%d" % i, [128, 512], F32, sb1) for i in range(2)]
                pL = [ps_("pL%d" % i, [128, 512], F32, sb1) for i in range(2)]

                heads = range(NH) if stop_after != "Bx" else range(2)
                blocks = []
                for h in heads:
                    for i in range(NOWN):
                        for kb in range(8 * i + 8):
                            blocks.append((h, i, kb))
                LA = 2

                def load_head(h):
                    sl = h % 2
                    S.dma("sp", lambda e, h=h, sl=sl: [e.dma_start(out=KT[sl][:], in_=kt_s[:, h, :])], reads=["kt_s"], writes=[("KT", sl)], sem="d_kt%d" % sl)
                    S.dma("sp", lambda e, h=h, sl=sl: [e.dma_start(out=Vh[sl][:], in_=v_s[h])], reads=["v_s"], writes=[("Vh", sl)], sem="d_vh%d" % sl)
                    S.dma("sp", lambda e, h=h, sl=sl: [e.dma_start(out=Qh[sl][:], in_=q_s[:, h, :])], reads=["q_s"], writes=[("Qh", sl)], sem="d_qh%d" % sl)

                def emit_S(n):
                    h, i, kb = blocks[n]
                    sl = h % 2
                    s_ = n % 4
                    pslot = n % 3
                    diag_blk = kb >= 8 * i
                    S.op("pe", lambda e: e.matmul(pS[s_][:], lhsT=KT[sl][:, kb * 128:(kb + 1) * 128], rhs=Qh[sl][:, i * 512:(i + 1) * 512], start=True, stop=False),
                         reads=[("KT", sl), ("Qh", sl)], writes=[("pS", s_)], sig=False)
                    S.op("pe", lambda e: e.matmul(pS[s_][:], lhsT=cselb[:, h, :], rhs=cT[:, i, :], start=False, stop=not diag_blk),
                         reads=["cselb", "cT"], writes=[("pS", s_)], sig=not diag_blk)
                    if diag_blk:
                        S.op("pe", lambda e: e.matmul(pS[s_][:], lhsT=identb[:], rhs=maskb[:, kb - 8 * i, :], start=False, stop=True),
                             reads=["identb", "maskb"], writes=[("pS", s_)], sig=True)
                    S.op("act", lambda e: e.activation(out=pt[pslot][:], in_=pS[s_][:], func=AF.Exp, bias=negF[:, kb, h:h + 1], scale=1.0),
                         reads=[("pS", s_), "negF"], writes=[("pt", pslot)])

                def emit_PV(n):
                    h, i, kb = blocks[n]
                    sl = h % 2
                    pslot = n % 3
                    nkb = 8 * i + 8
                    o = (h * NOWN + i) % 2
                    last = kb == nkb - 1
                    S.op("pe", lambda e: e.matmul(pO[o][:], lhsT=Vh[sl][:, kb, :], rhs=pt[pslot][:], start=(kb == 0), stop=last),
                         reads=[("Vh", sl), ("pt", pslot)], writes=[("pO", o)], sig=True)
                    if kb == 0:
                        S.op("dve", lambda e: e.tensor_copy(out=acc[o][:], in_=pt[pslot][:]), reads=[("pt", pslot)], writes=[("acc", o)])
                    elif kb % 4 != 3:
                        S.op("dve", lambda e: e.tensor_tensor(out=acc[o][:], in0=acc[o][:], in1=pt[pslot][:], op=ALU.add), reads=[("pt", pslot), ("acc", o)], writes=[("acc", o)])
                    else:
                        S.op("pe", lambda e: e.matmul(pL[o][:], lhsT=onesb[:], rhs=pt[pslot][:], start=(kb == 3), stop=False),
                             reads=["onesb", ("pt", pslot)], writes=[("pL", o)], sig=True)
                    if last:
                        S.op("pe", lambda e: e.matmul(pL[o][:], lhsT=onesf[:], rhs=acc[o][:], start=False, stop=True), reads=["onesf", ("acc", o)], writes=[("pL", o)])
                        S.op("dve", lambda e: e.reciprocal(out=rl[o][:], in_=pL[o][:]), reads=[("pL", o)], writes=[("rl", o)])
                        S.op("dve", lambda e: e.tensor_tensor(out=obst[o][:], in0=pO[o][:], in1=rl[o][:], op=ALU.mult), reads=[("pO", o), ("rl", o)], writes=[("obst", o)])
                        S.dma("sp", lambda e: [e.dma_start(out=ob_s[i, :, h, :], in_=obst[o][:])], reads=[("obst", o)], writes=["ob_s"], sem="d_ob%d" % o)

                cvtC = [sb("cvtC%d" % i, [128, KC, 512], BF16, sb1) for i in range(3)]
                wcnt = [0]

                def prep_weight(src_d, r0, c0, dsts):
                    sl = wcnt[0] % 3
                    wcnt[0] += 1
                    S.dma("pool", lambda e: [e.dma_start(out=cvtC[sl][:], in_=src_d[r0:r0 + 1024, c0:c0 + 512].rearrange("(kc p) n -> p kc n", p=128))],
                          writes=[("cvtC", sl)], sem="d_cvl%d" % sl)
                    S.dma("pool", _n(lambda e: [e.dma_start(out=d_ap, in_=cvtC[sl][:, :, lo:hi]) for (d_ap, lo, hi) in dsts], len(dsts)),
                          reads=[("cvtC", sl)], writes=["wscr"], sem="d_cvs%d" % sl)

                def halves(scr, j):
                    return [(scr[2 * j], 0, 256), (scr[2 * j + 1], 256, 512)]

                if stop_after not in ("B", "Bx"):
                    for hf in range(2):
                        prep_weight(win_d, 0, C_GL + hf * 512, halves(wgl_s, hf))
                    for hf in range(4):
                        prep_weight(win_d, 0, C_GA + hf * 512, halves(wgg_s, hf))
                    for (dst, src) in ((wa_s, wa_d), (wb_s, wb_d), (wo_s, wo_d)):
                        for hf in range(2):
                            prep_weight(src, 0, hf * 512, halves(dst, hf))
                    for hf in range(8):
                        prep_weight(wup_d, 0, hf * 512, [(wup_s[hf], 0, 512)])
                    for g in range(4):
                        for hf in range(2):
                            prep_weight(wdn_d, g * 1024, hf * 512, [(wdn_s[hf * 4 + g], 0, 512)])

                hl = list(heads)
                load_head(hl[0])
                for n in range(len(blocks) + LA):
                    if n < len(blocks):
                        emit_S(n)
                    if n >= LA:
                        emit_PV(n - LA)
                        h, i, kb = blocks[n - LA]
                        if i == 0 and kb == 0 and hl.index(h) + 1 < len(hl):
                            load_head(hl[hl.index(h) + 1])
                S.run("B")
                if stop_after in ("B", "Bx", "Bp"):
                    return nc

        with ExitStack() as sC:
            NW = 3
            NWA = 8
            wst = [sb("wst%d" % i, [128, KC, 512], BF16, sC) for i in range(NW)]
            wsa = [sb("wsa%d" % i, [128, KC, 256], BF16, sC) for i in range(NWA)]
            g3b = sb("g3b", [128, D], F32, sC)
            xtC = [sb("xtC%d" % i, [128, 4, D], F32, sC) for i in range(2)]
            utC = sb("utC", [128, KC, 512], BF16, sC)
            hsC = sb("hsC", [128, KC, 512], BF16, sC)
            obC = sb("obC", [128, NH, 512], BF16, sC)
            gatC = sb("gatC", [128, KC, 512], BF16, sC)
            mT = sb("mT", [128, KC, 512], BF16, sC)
            m2T = sb("m2T", [128, KC, 512], BF16, sC)
            hT = sb("hT", [128, HC, 512], BF16, sC)
            xn2 = [sb("xn2_%d" % i, [128, D], BF16, sC) for i in range(2)]
            junk2 = sb("junk2", [128, D], BF16, sC)
            ggt = [sb("ggt%d" % i, [128, 512], BF16, sC) for i in range(2)]
            sg = [sb("sg%d" % i, [128, 512], F32, sC) for i in range(4)]
            rr = [sb("rr%d" % i, [128, 512], BF16, sC) for i in range(2)]
            ssC = sb("ssC", [128, 2, 2, 4], F32, sC)
            rsC = sb("rsC", [128, 2, 2, 4], F32, sC)
            pC = [ps_("pC%d" % i, [128, 512], F32, sC) for i in range(7)]
            pT2 = ps_("pT2", [128, KC, 128], BF16, sC)
            pcc = [0]
            wsc = [0]

            def next_pc():
                k = pcc[0] % 3
                pcc[0] += 1
                return k

            pcc2 = [0]

            def next_pc2():
                k = 3 + pcc2[0] % 4
                pcc2[0] += 1
                return k

            def load_w(src_ap):
                sl = wsc[0] % NW
                wsc[0] += 1
                S.dma("sp", lambda e: [e.dma_start(out=wst[sl][:], in_=src_ap)], reads=["wscr"], writes=[("wst", sl)], sem="d_wst%d" % sl)
                return sl

            wsca = [0]

            def load_wa(scr, c0):
                sl = wsca[0] % NWA
                wsca[0] += 1
                S.dma("sp", lambda e: [e.dma_start(out=wsa[sl][:], in_=scr[c0 // 256])],
                      reads=["wscr"], writes=[("wsa", sl)], sem="d_wsa%d" % sl)
                return sl

            def kcview(scr, r0, c0):
                return scr[r0:r0 + 1024, c0:c0 + 512].rearrange("(kc p) n -> p kc n", p=128)

            S.dma("sp", lambda e: [e.dma_start(out=xtC[1][0:1, 0, :], in_=g3_d)], writes=[("xtC", 1, 0)], sem="d_constC")
            for hf in range(2):
                k = next_pc()
                S.op("pe", lambda e, hf=hf, k=k: e.matmul(pC[k][:], lhsT=onesf[0:1, :], rhs=xtC[1][0:1, 0, hf * 512:(hf + 1) * 512], start=True, stop=True),
                     reads=["onesf", ("xtC", 1, 0)], writes=[("pC", k)])
                S.op("dve", lambda e, hf=hf, k=k: e.tensor_copy(out=g3b[:, hf * 512:(hf + 1) * 512], in_=pC[k][:]), reads=[("pC", k)], writes=["g3b"])

            def rms_stats(xs, slot):
                for tb in range(4):
                    S.op("act", lambda e, tb=tb: e.activation(out=junk2[:], in_=xtC[xs][:, tb, :], func=AF.Square, accum_out=ssC[:, xs, slot, tb:tb + 1]),
                         reads=[("xtC", xs, tb)], writes=["junk2", ("ssC", xs, slot, tb)])
                S.op("dve", lambda e: e.tensor_scalar(out=rsC[:, xs, slot, :], in0=ssC[:, xs, slot, :], scalar1=1.0 / D, scalar2=EPS, op0=ALU.mult, op1=ALU.add),
                     reads=[("ssC", xs, slot, tb) for tb in range(4)], writes=[("rsC", xs, slot)])
                S.op("act", lambda e: e.activation(out=rsC[:, xs, slot, :], in_=rsC[:, xs, slot, :], func=AF.Sqrt), reads=[("rsC", xs, slot)], writes=[("rsC", xs, slot)])
                S.op("dve", lambda e: e.reciprocal(out=rsC[:, xs, slot, :], in_=rsC[:, xs, slot, :]), reads=[("rsC", xs, slot)], writes=[("rsC", xs, slot)])

            def stage1(i):
                xs = i % 2
                X = xtC[xs]
                tsl = slice(i * 512, (i + 1) * 512)
                S.dma("sp", _n(lambda e: [
                    e.dma_start(out=utC[:], in_=ut_s[i]),
                    e.dma_start(out=hsC[:], in_=hs_s[i]),
                ], 2), reads=["ut_s", "hs_s"], writes=["utC", "hsC"], sem="d_actC")
                for oc in range(KC):
                    if oc % 2 == 0:
                        wsl = load_wa(wgl_s, (oc // 2) * 256)
                    k = next_pc()
                    for kc in range(KC):
                        S.op("pe", lambda e, k=k, kc=kc, oc=oc, wsl=wsl: e.matmul(pC[k][:], lhsT=wsa[wsl][:, kc, (oc % 2) * 128:(oc % 2 + 1) * 128], rhs=utC[:, kc, :],
                                                                            start=(kc == 0), stop=(kc == KC - 1)),
                             reads=[("wsa", wsl), "utC"], writes=[("pC", k)], sig=(kc == KC - 1))
                    q = oc % 2
                    S.op("act", lambda e, k=k, q=q: e.activation(out=ggt[q][:], in_=pC[k][:], func=AF.Gelu), reads=[("pC", k)], writes=[("ggt", q)])
                    S.op("pool", lambda e, oc=oc, q=q: e.tensor_tensor(out=gatC[:, oc, :], in0=ggt[q][:], in1=hsC[:, oc, :], op=ALU.mult),
                         reads=[("ggt", q), "hsC"], writes=[("gatC", oc)])
                S.dma("sp", lambda e: [e.dma_start(out=obC[:], in_=ob_s[i])], reads=["ob_s"], writes=["obC"], sem="d_actC2")
                gatR = [("gatC", oc) for oc in range(KC)]
                for oc in range(KC):
                    if oc % 2 == 0:
                        wa_ = load_wa(wa_s, (oc // 2) * 256)
                        wb_ = load_wa(wb_s, (oc // 2) * 256)
                        wga = load_wa(wgg_s, (oc // 2) * 256)
                        wgb = load_wa(wgg_s, 1024 + (oc // 2) * 256)
                    csl = slice((oc % 2) * 128, (oc % 2 + 1) * 128)
                    qa, qb = (2 * oc) % 4, (2 * oc + 1) % 4
                    for (wy, rhs_y, rtok, wg, qq) in ((wa_, gatC, gatR, wga, qa), (wb_, obC, ["obC"], wgb, qb)):
                        ky, kg = next_pc(), next_pc()
                        for kc in range(KC):
                            S.op("pe", lambda e, kc=kc, ky=ky, wy=wy, rhs_y=rhs_y, csl=csl: e.matmul(pC[ky][:], lhsT=wsa[wy][:, kc, csl], rhs=rhs_y[:, kc, :],
                                                                                             start=(kc == 0), stop=(kc == KC - 1)),
                                 reads=list(rtok) + [("wsa", wy)], writes=[("pC", ky)], sig=(kc == KC - 1))
                        for kc in range(KC):
                            S.op("pe", lambda e, kc=kc, kg=kg, wg=wg, csl=csl: e.matmul(pC[kg][:], lhsT=wsa[wg][:, kc, csl], rhs=utC[:, kc, :],
                                                                                  start=(kc == 0), stop=(kc == KC - 1)),
                                 reads=["utC", ("wsa", wg)], writes=[("pC", kg)], sig=(kc == KC - 1))
                        S.op("act", lambda e, kg=kg, qq=qq: e.activation(out=sg[qq][:], in_=pC[kg][:], func=AF.Sigmoid), reads=[("pC", kg)], writes=[("sg", qq)])
                        S.op("dve", lambda e, ky=ky, qq=qq: e.tensor_tensor(out=sg[qq][:], in0=pC[ky][:], in1=sg[qq][:], op=ALU.mult), reads=[("pC", ky), ("sg", qq)], writes=[("sg", qq)])
                    S.op("pool", lambda e, oc=oc, qa=qa, qb=qb: e.tensor_tensor(out=mT[:, oc, :], in0=sg[qa][:], in1=sg[qb][:], op=ALU.add),
                         reads=[("sg", qa), ("sg", qb)], writes=[("mT", oc)])
                S.dma("sp", lambda e: [e.dma_start(out=X[:], in_=xo_d[i * 512:(i + 1) * 512, :].rearrange("(tb p) c -> p tb c", p=128))],
                      writes=[("xtC", xs, tb) for tb in range(4)], sem="d_actC3_%d" % xs)
                mR = [("mT", oc) for oc in range(KC)]
                for qt in range(4):
                    wo_ = load_wa(wo_s, qt * 256)
                    for tb in range(4):
                        k = next_pc()
                        for kc in range(KC):
                            S.op("pe", lambda e, k=k, kc=kc, tb=tb, wo_=wo_: e.matmul(pC[k][:, 0:256], lhsT=mT[:, kc, tb * 128:(tb + 1) * 128], rhs=wsa[wo_][:, kc, :],
                                                                                start=(kc == 0), stop=(kc == KC - 1)),
                                 reads=mR + [("wsa", wo_)], writes=[("pC", k)], sig=(kc == KC - 1))
                        S.op("dve", lambda e, k=k, tb=tb, qt=qt: e.tensor_tensor(out=X[:, tb, qt * 256:(qt + 1) * 256], in0=X[:, tb, qt * 256:(qt + 1) * 256], in1=pC[k][:, 0:256], op=ALU.add),
                             reads=[("pC", k), ("xtC", xs, tb)], writes=[("xtC", xs, tb)])

            def stage2(i):
                xs = i % 2
                X = xtC[xs]
                chunks = [wup_s[c] for c in range(8)]
                chunks += [wdn_s[hf * 4 + g] for hf in range(2) for g in range(4)]
                slot_of = {}

                def use_chunk(n):
                    for m in (n, n + 1, n + 2):
                        if m < len(chunks) and m not in slot_of:
                            slot_of[m] = load_w(chunks[m])
                    return slot_of[n]

                for m in (0, 1):
                    slot_of[m] = load_w(chunks[m])
                rms_stats(xs, 0)
                for tb in range(4):
                    q = tb % 2
                    S.op("act", lambda e, tb=tb, q=q: e.mul(out=xn2[q][:], in_=X[:, tb, :], mul=rsC[:, xs, 0, tb:tb + 1]),
                         reads=[("xtC", xs, tb), ("rsC", xs, 0)], writes=[("xn2", q)])
                    for c in range(KC):
                        S.op("pe", lambda e, q=q, c=c: e.transpose(out=pT2[:, c, :], in_=xn2[q][:, c * 128:(c + 1) * 128], identity=identb[:]),
                             reads=[("xn2", q), "identb"], writes=["pT2"], sig=(c == KC - 1))
                    for c in range(KC):
                        S.op("dve", lambda e, tb=tb, c=c: e.tensor_scalar(out=m2T[:, c, tb * 128:(tb + 1) * 128], in0=pT2[:, c, :], scalar1=vecT[:, c, V_G2:V_G2 + 1],
                                                                        scalar2=None, op0=ALU.mult),
                             reads=["pT2", "vecT"], writes=[("m2T", tb, c)])
                m2R = [("m2T", tb, c) for tb in range(4) for c in range(KC)]
                for hc in range(HC):
                    if hc % 4 == 0:
                        wsl = use_chunk(hc // 4)
                    k = next_pc2()
                    for kc in range(KC):
                        S.op("pe", lambda e, k=k, kc=kc, hc=hc, wsl=wsl: e.matmul(pC[k][:], lhsT=wst[wsl][:, kc, (hc % 4) * 128:(hc % 4 + 1) * 128], rhs=m2T[:, kc, :],
                                                                            start=(kc == 0), stop=(kc == KC - 1)),
                             reads=m2R + [("wst", wsl)], writes=[("pC", k)], sig=(kc == KC - 1))
                    q = hc % 2
                    S.op("act", lambda e, k=k, q=q: e.activation(out=rr[q][:], in_=pC[k][:], func=AF.Relu), reads=[("pC", k)], writes=[("rr", q)])
                    S.op("pool", lambda e, hc=hc, q=q: e.tensor_tensor(out=hT[:, hc, :], in0=rr[q][:], in1=rr[q][:], op=ALU.mult), reads=[("rr", q)], writes=[("hT", hc)])
                for hf in range(2):
                    kd = [next_pc2() for _ in range(4)]
                    for g in range(4):
                        wsl = use_chunk(8 + hf * 4 + g)
                        for h8 in range(8):
                            hc = g * 8 + h8
                            for tb in range(4):
                                last = (g == 3 and h8 == 7)
                                S.op("pe", lambda e, hc=hc, h8=h8, tb=tb, wsl=wsl, k=kd[tb], first=(g == 0 and h8 == 0), last=last: e.matmul(
                                    pC[k][:], lhsT=hT[:, hc, tb * 128:(tb + 1) * 128], rhs=wst[wsl][:, h8, :], start=first, stop=last),
                                    reads=[("hT", hc), ("wst", wsl)], writes=[("pC", kd[tb])], sig=last)
                    for tb in range(4):
                        S.op("dve", lambda e, k=kd[tb], tb=tb, hf=hf: e.tensor_tensor(out=X[:, tb, hf * 512:(hf + 1) * 512], in0=X[:, tb, hf * 512:(hf + 1) * 512], in1=pC[k][:], op=ALU.add),
                             reads=[("pC", kd[tb]), ("xtC", xs, tb)], writes=[("xtC", xs, tb)])
                rms_stats(xs, 1)
                for tb in range(4):
                    S.op("dve", lambda e, tb=tb: e.scalar_tensor_tensor(out=X[:, tb, :], in0=X[:, tb, :], scalar=rsC[:, xs, 1, tb:tb + 1], in1=g3b[:], op0=ALU.mult, op1=ALU.mult),
                         reads=[("xtC", xs, tb), ("rsC", xs, 1), "g3b"], writes=[("xtC", xs, tb)])
                S.dma("pool", lambda e: [e.dma_start(out=out_d[i * 512:(i + 1) * 512, :].rearrange("(tb p) c -> p tb c", p=128), in_=X[:])],
                      reads=[("xtC", xs, tb) for tb in range(4)], writes=["out"], sem="d_out%d" % xs)

            n_own = NOWN if stop_after != "Cx" else 1
            stage1(0)
            for i in range(n_own):
                S.interleave(S.capture(stage2, i), S.capture(stage1, i + 1) if i + 1 < n_own else [])
            S.run("C")
    return nc


_NC_CACHE = {}


def _prep_inputs(inp, ncores=8):
    f = lambda a: np.ascontiguousarray(np.asarray(a, dtype=np.float32))
    x = f(inp["x"])
    vecs = np.zeros((16, D), np.float32)
    vecs[0:4] = f(inp["conv_w"])
    vecs[4] = f(inp["conv_b"])
    vecs[5] = f(inp["lru_ba"])
    vecs[6] = f(inp["lru_bx"])
    vecs[7] = f(inp["lru_lambda"])
    vecs[8] = f(inp["norm_mix_g"])
    vecs[9] = f(inp["norm_mlp_g"])
    vecs[10] = f(inp["norm_final_g"])
    ident = np.eye(128, dtype=np.float32)
    tri = np.triu(np.ones((128, 128), np.float32))
    common = {
        "w_in": f(inp["w_in"]), "vecs": vecs, "g3row": f(inp["norm_final_g"])[None, :],
        "lru_wa": f(inp["lru_wa"]), "lru_wx": f(inp["lru_wx"]), "forget_b": f(inp["forget_b"])[None, :],
        "w_branch_a": f(inp["w_branch_a"]), "w_branch_b": f(inp["w_branch_b"]), "w_out": f(inp["w_out"]),
        "w_up": f(inp["w_up"]), "w_down": f(inp["w_down"]), "ident": ident, "tri": tri,
    }
    maps = []
    for c in range(ncores):
        b, j = c // 2, c % 2
        xb = x[b]
        xo = np.ascontiguousarray(xb.reshape(NT, 512, D)[j::2].reshape(OWN, D))
        sel = np.zeros((128, 2), np.float32)
        sel[:, j] = 1.0
        kpos = np.arange(8)[:, None, None] * 128 + np.arange(128)[None, :, None]
        qpos = j * 512 + np.arange(512)[None, None, :]
        masks = np.where(kpos <= qpos, 0.0, MASKV).astype(np.float32)
        csel = np.zeros((16, NH, 128), np.float32)
        for h in range(NH):
            csel[j * 8 + h, h, :] = 1.0
        m = dict(common)
        m.update({"x": np.ascontiguousarray(xb), "xo": xo, "sel": sel, "masks": masks, "csel": csel})
        maps.append(m)
    return maps


def kernel(**inputs):
    if "nc" not in _NC_CACHE:
        _NC_CACHE["nc"] = build_nc()
    nc = _NC_CACHE["nc"]
    maps = _prep_inputs(inputs, 8)
    res = run_bass_kernel_spmd(nc, maps, core_ids=list(range(8)))
    B = inputs["x"].shape[0]
    out = np.empty((B, SEQ, D), np.float32)
    ov = out.reshape(B, NT, 512, D)
    for c in range(8):
        b, j = c // 2, c % 2
        ov[b, j::2] = np.asarray(res.results[c]["out"], dtype=np.float32).reshape(NOWN, 512, D)
    return out
```

```python
import math
from contextlib import ExitStack

import numpy as np
import concourse.bass as bass
import concourse.mybir as mybir
from concourse.bass_utils import run_bass_kernel_spmd

F32 = mybir.dt.float32
BF16 = mybir.dt.bfloat16
AF = mybir.ActivationFunctionType
ALU = mybir.AluOpType
AX = mybir.AxisListType


class _Op:
    __slots__ = ("eng", "fn", "reads", "writes", "sig", "dsem", "key", "cnt", "waits", "clock")

    def __init__(self, eng, fn, reads, writes, sig, dsem):
        self.eng, self.fn, self.reads, self.writes, self.sig, self.dsem = eng, fn, reads, writes, sig, dsem


class Sched:
    ENGS = ("pe", "act", "dve", "pool", "sp")
    SAME_ENGINE_SYNC = True

    def __init__(self, nc):
        self.nc = nc
        self.ops = []
        self.sems = {}
        self.count = {}
        self.last_w = {}
        self.readers = {}
        self.known = {e: {} for e in self.ENGS}
        self.stack = ExitStack()
        self.n_wait = 0
        self.n_ops = 0

    def __enter__(self):
        self.stack.__enter__()
        return self

    def __exit__(self, *a):
        return self.stack.__exit__(*a)

    def _sem(self, key):
        if key not in self.sems:
            self.sems[key] = self.stack.enter_context(self.nc.semaphore("s_" + str(key)))
            self.count[key] = 0
        return self.sems[key]

    def op(self, eng, fn, reads=(), writes=(), sig=True):
        if eng != "pe":
            sig = True
        self.ops.append(_Op(eng, fn, tuple(reads), tuple(writes), sig, None))

    def dma(self, eng, fn, reads=(), writes=(), sem=None):
        assert sem is not None
        self.ops.append(_Op(eng, fn, tuple(reads), tuple(writes), True, sem))

    def capture(self, fn, *a):
        saved, self.ops = self.ops, []
        fn(*a)
        out, self.ops = self.ops, saved
        return out

    def interleave(self, *lists, spans=None):
        if spans is None:
            spans = [1.0] * len(lists)
        pairs = [(l, sp) for l, sp in zip(lists, spans) if l]
        lists = [p[0] for p in pairs]
        spans = [p[1] for p in pairs]
        pos = [0] * len(lists)
        for _ in range(sum(len(l) for l in lists)):
            j = min((spans[j] * pos[j] / len(lists[j]), j) for j in range(len(lists)) if pos[j] < len(lists[j]))[1]
            self.ops.append(lists[j][pos[j]])
            pos[j] += 1

    def run(self, name):
        ops, self.ops = self.ops, []
        self.last_w = {}
        self.readers = {}
        last = {}
        for o in ops:
            if o.dsem is None:
                last[o.eng] = o
        for o in last.values():
            o.sig = True

        def producers(o, last_w, readers):
            ps = []
            for t in o.reads:
                if t in last_w:
                    ps.append(last_w[t])
            for t in o.writes:
                if t in last_w:
                    ps.append(last_w[t])
                ps.extend(readers.get(t, ()))
            return ps

        def update(o, last_w, readers):
            for t in o.writes:
                last_w[t] = o
                readers[t] = []
            for t in o.reads:
                if t not in o.writes:
                    readers.setdefault(t, []).append(o)

        lw, rd = {}, {}
        for o in ops:
            for p in producers(o, lw, rd):
                if p.eng == "pe" and p.dsem is None and not p.sig and not (o.eng == "pe" and o.dsem is None):
                    p.sig = True
            update(o, lw, rd)
        pending = {e: [] for e in self.ENGS}
        for o in ops:
            if o.dsem is not None:
                self._sem(o.dsem)
                o.key = o.dsem
                self.count[o.dsem] += 16 * getattr(o.fn, "n", 1)
                o.cnt = self.count[o.dsem]
            else:
                self._sem(o.eng)
                o.key = o.eng
                pending[o.eng].append(o)
                if o.sig:
                    self.count[o.eng] += 1
                    for p in pending[o.eng]:
                        p.cnt = self.count[o.eng]
                    pending[o.eng] = []
        per_eng = {e: [] for e in self.ENGS}
        for o in ops:
            kn = self.known[o.eng]
            need = {}
            for p in producers(o, self.last_w, self.readers):
                k, c = p.key, p.cnt
                if k == "pe" and o.eng == "pe" and o.dsem is None:
                    continue
                if (not self.SAME_ENGINE_SYNC) and k == o.eng and o.dsem is None and p.dsem is None:
                    continue
                if kn.get(k, 0) >= c:
                    continue
                if need.get(k, (0, None))[0] < c:
                    need[k] = (c, p.clock)
            o.waits = []
            for k, (c, clk) in need.items():
                if kn.get(k, 0) >= c:
                    continue
                o.waits.append((k, c))
                for kk, cc in clk.items():
                    if kn.get(kk, 0) < cc:
                        kn[kk] = cc
            clock = dict(kn)
            clock[o.key] = max(clock.get(o.key, 0), o.cnt)
            o.clock = clock
            update(o, self.last_w, self.readers)
            per_eng[o.eng].append(o)
            self.n_wait += len(o.waits)
        self.n_ops += len(ops)
        final = dict(self.count)

        def emit(engname, e):
            for o in per_eng[engname]:
                for k, c in o.waits:
                    e.wait_ge(self.sems[k], c)
                r = o.fn(e)
                if o.dsem is not None:
                    assert len(r) == getattr(o.fn, "n", 1), (len(r), getattr(o.fn, "n", 1))
                    for ins in r:
                        ins.then_inc(self.sems[o.dsem], 16)
                elif o.sig:
                    r.then_inc(self.sems[o.eng], 1)
            kn = self.known[engname]
            for k, c in final.items():
                if c > 0 and kn.get(k, 0) < c:
                    e.wait_ge(self.sems[k], c)

        with self.nc.Block() as blk:
            blk.tensor(lambda e: emit("pe", e))
            blk.scalar(lambda e: emit("act", e))
            blk.vector(lambda e: emit("dve", e))
            blk.gpsimd(lambda e: emit("pool", e))
            blk.sync(lambda e: emit("sp", e))
        for e in self.ENGS:
            for k, c in final.items():
                if self.known[e].get(k, 0) < c:
                    self.known[e][k] = c


def _n(fn, n):
    fn.n = n
    return fn


SEQ = 8192
D = 1024
KC = 8
NB = SEQ // 128
NT = SEQ // 512
NOWN = NT // 2
OWN = NOWN * 512
DFF = 4096
HC = DFF // 128
NH = 8
D_IN = 7176
C_XL, C_GL, C_Q, C_K, C_V, C_GA, C_GB, C_F = 0, 1024, 2048, 3072, 4096, 5120, 6144, 7168
QSCALE = 1.0 / math.sqrt(128.0)
EPS = 1e-6
MASKV = -30000.0
V_CW, V_CB, V_BA, V_BX, V_LAM, V_G1, V_G2, V_G3 = 0, 4, 5, 6, 7, 8, 9, 10


def build_nc(debug=False, stop_after="C"):
    nc = bass.Bass("TRN2", target_bir_lowering=False)
    dk = "ExternalOutput" if debug else "Internal"

    def din(name, shape, dt=F32):
        return nc.dram_tensor(name, list(shape), dt, kind="ExternalInput").ap()

    def dscr(name, shape, dt=BF16):
        return nc.dram_tensor(name, list(shape), dt, kind=dk).ap()

    x_d = din("x", [SEQ, D])
    xo_d = din("xo", [OWN, D])
    win_d = din("w_in", [D, D_IN])
    vecs_d = din("vecs", [16, D])
    g3_d = din("g3row", [1, D])
    lwa_d = din("lru_wa", [16, 64, 64])
    lwx_d = din("lru_wx", [16, 64, 64])
    fb_d = din("forget_b", [1, NH])
    wa_d = din("w_branch_a", [D, D])
    wb_d = din("w_branch_b", [D, D])
    wo_d = din("w_out", [D, D])
    wup_d = din("w_up", [D, DFF])
    wdn_d = din("w_down", [DFF, D])
    ident_d = din("ident", [128, 128])
    tri_d = din("tri", [128, 128])
    sel_d = din("sel", [128, 2])
    masks_d = din("masks", [8, 128, 512])
    csel_d = din("csel", [16, NH, 128])
    out_d = nc.dram_tensor("out", [OWN, D], F32, kind="ExternalOutput").ap()

    kt_s = dscr("kt_s", [128, NH, SEQ])
    v_s = dscr("v_s", [NH, 128, NB, 128])
    q_s = dscr("q_s", [128, NH, OWN])
    ut_s = dscr("ut_s", [NOWN, 128, KC, 512])
    hs_s = dscr("hs_s", [NOWN, 128, KC, 512])
    ob_s = dscr("ob_s", [NOWN, 128, NH, 512])
    wq_s = dscr("wq_s", [D, D])
    wgl_s = dscr("wgl_s", [4, 128, KC, 256])
    wgg_s = dscr("wgg_s", [8, 128, KC, 256])
    wa_s = dscr("wa_s", [4, 128, KC, 256])
    wb_s = dscr("wb_s", [4, 128, KC, 256])
    wo_s = dscr("wo_s", [4, 128, KC, 256])
    wup_s = dscr("wup_s", [8, 128, KC, 512])
    wdn_s = dscr("wdn_s", [8, 128, KC, 512])
    dbg_d = nc.dram_tensor("dbg", [128, 4096], F32, kind="ExternalOutput").ap() if debug else None

    S = Sched(nc)
    es = ExitStack()

    def sb(name, shape, dt, stack=None):
        return (stack or es).enter_context(nc.sbuf_tensor(name, list(shape), dt))

    def ps_(name, shape, dt, stack=None):
        return (stack or es).enter_context(nc.psum_tensor(name, list(shape), dt))

    with S, es:
        identf = sb("identf", [128, 128], F32)
        identb = sb("identb", [128, 128], BF16)
        onesf = sb("onesf", [128, 128], F32)
        onesb = sb("onesb", [128, 128], BF16)
        trif = sb("trif", [128, 128], F32)
        selt = sb("selt", [128, 2], F32)
        selB = [sb("selB%d" % i, [128, 512], F32) for i in range(2)]
        vecT = sb("vecT", [128, KC, 16], F32)
        dv = sb("dv", [128, KC, 4], F32)
        fbb = sb("fbb", [128, NH], F32)
        E = sb("E", [128, NB, NH], F32)

        with ExitStack() as sA:
            Wx = sb("Wx", [128, KC, D], BF16, sA)
            Wk = sb("Wk", [128, KC, D], BF16, sA)
            Wv = sb("Wv", [128, KC, D], BF16, sA)
            Wf = sb("Wf", [128, KC, NH], BF16, sA)
            diag = sb("diag", [128, 32, 128], BF16, sA)
            BDa = sb("BDa", [128, KC, 128], BF16, sA)
            BDx = sb("BDx", [128, KC, 128], BF16, sA)
            with ExitStack() as s0:
                vecs16 = sb("vecs16", [16, D], F32, s0)
                fbrow = sb("fbrow", [1, NH], F32, s0)
                stg = [sb("stg%d" % i, [128, KC, 512], F32, s0) for i in range(2)]
                gB1 = sb("gB1", [128, KC, 512], F32, s0)
                stgf = sb("stgf", [128, KC, NH], F32, s0)
                tmp8 = sb("tmp8", [128, KC], F32, s0)
                psA = ps_("p0a", [128, 512], F32, s0)
                psB = ps_("p0b", [128, 512], F32, s0)

                S.dma("sp", _n(lambda e: [
                    e.dma_start(out=identf[:], in_=ident_d),
                    e.dma_start(out=trif[:], in_=tri_d),
                    e.dma_start(out=selt[:], in_=sel_d),
                    e.dma_start(out=vecs16[:], in_=vecs_d),
                    e.dma_start(out=fbrow[:], in_=fb_d),
                ], 5), writes=["identf", "trif", "selt", "vecs16", "fbrow"], sem="d_const")
                S.op("pool", lambda e: e.memset(onesf[:], 1.0), writes=["onesf"])
                S.op("pool", lambda e: e.memset(onesb[:], 1.0), writes=["onesb"])
                S.op("pool", lambda e: e.memset(BDa[:], 0.0), writes=["BDa"])
                S.op("pool", lambda e: e.memset(BDx[:], 0.0), writes=["BDx"])
                S.dma("pool", _n(lambda e: [
                    e.dma_start(out=BD[half * 64:(half + 1) * 64, :, half * 64:(half + 1) * 64],
                                in_=src.rearrange("(oc two) c d -> two c oc d", two=2)[half])
                    for BD, src in ((BDa, lwa_d), (BDx, lwx_d)) for half in range(2)
                ], 4), reads=[], writes=["BDa", "BDx"], sem="d_const2")
                S.op("dve", lambda e: e.tensor_copy(out=identb[:], in_=identf[:]), reads=["identf"], writes=["identb"])
                for c in range(KC):
                    S.op("pe", lambda e, c=c: e.transpose(out=psA[:, c * 16:(c + 1) * 16], in_=vecs16[:, c * 128:(c + 1) * 128],
                                                           identity=identf[0:16, 0:16]),
                         reads=["vecs16", "identf"], writes=["psA"], sig=(c == KC - 1))
                S.op("dve", lambda e: e.tensor_copy(out=vecT[:].rearrange("p c r -> p (c r)"), in_=psA[:, 0:128]),
                     reads=["psA"], writes=["vecT"])
                S.op("dve", lambda e: e.tensor_scalar(out=dv[:, :, 0], in0=vecT[:, :, V_BA], scalar1=0.5, scalar2=None, op0=ALU.mult),
                     reads=["vecT"], writes=["dv0"])
                S.op("dve", lambda e: e.tensor_scalar(out=dv[:, :, 1], in0=vecT[:, :, V_BX], scalar1=0.5, scalar2=None, op0=ALU.mult),
                     reads=["vecT"], writes=["dv1"])
                S.op("act", lambda e: e.activation(out=tmp8[:], in_=vecT[:, :, V_LAM], func=AF.Exp, scale=-1.0), reads=["vecT"], writes=["tmp8"])
                S.op("act", lambda e: e.activation(out=tmp8[:], in_=tmp8[:], func=AF.Ln, bias=1.0, scale=1.0), reads=["tmp8"], writes=["tmp8"])
                S.op("dve", lambda e: e.tensor_scalar(out=dv[:, :, 2], in0=tmp8[:], scalar1=-4.0, scalar2=None, op0=ALU.mult),
                     reads=["tmp8"], writes=["dv2"])
                S.op("pe", lambda e: e.matmul(psB[:, 0:NH], lhsT=onesf[0:1, :], rhs=fbrow[0:1, :], start=True, stop=True),
                     reads=["onesf", "fbrow"], writes=["psB"])
                S.op("dve", lambda e: e.tensor_copy(out=fbb[:], in_=psB[:, 0:NH]), reads=["psB"], writes=["fbb"])
                for oc in range(KC):
                    for k in range(4):
                        S.op("dve", lambda e, oc=oc, k=k: e.tensor_scalar(out=diag[:, oc * 4 + k, :], in0=identf[:], scalar1=vecT[:, oc, V_CW + k:V_CW + k + 1],
                                                                       scalar2=None, op0=ALU.mult),
                             reads=["identf", "vecT"], writes=["diag"])

                S.op("pool", lambda e: e.memset(gB1[:], 1.0), writes=["gB1"])
                for kc in range(KC):
                    S.op("dve", lambda e, kc=kc: e.tensor_scalar(out=gB1[:, kc, :], in0=gB1[:, kc, :], scalar1=vecT[:, kc, V_G1:V_G1 + 1], scalar2=None, op0=ALU.mult),
                         reads=["gB1", "vecT"], writes=["gB1"])
                for j in range(2):
                    S.op("pool", lambda e, j=j: e.memset(selB[j][:], 1.0), writes=[("selB", j)])
                    S.op("dve", lambda e, j=j: e.tensor_scalar(out=selB[j][:], in0=selB[j][:], scalar1=selt[:, j:j + 1], scalar2=None, op0=ALU.mult),
                         reads=[("selB", j), "selt"], writes=[("selB", j)])
                fwc = [0]

                def fold_weight(dst, src_d, c0, ncols, gidx, tag, q="sp"):
                    for hf in range((ncols + 511) // 512):
                        w = min(512, ncols - hf * 512)
                        sl = fwc[0] % 2
                        fwc[0] += 1
                        S.dma(q, lambda e, sl=sl, hf=hf, w=w: [e.dma_start(
                            out=stg[sl][:, :, 0:w], in_=src_d[:, c0 + hf * 512:c0 + hf * 512 + w].rearrange("(kc p) n -> p kc n", p=128))],
                            writes=[("stg", sl)], sem="d_stg%d" % sl)
                        eng = "dve" if sl == 0 else "pool"
                        S.op(eng, lambda e, sl=sl, hf=hf, w=w: e.tensor_tensor(out=dst[:, :, hf * 512:hf * 512 + w], in0=stg[sl][:, :, 0:w], in1=gB1[:, :, 0:w], op=ALU.mult),
                             reads=[("stg", sl), "gB1"], writes=[tag])

                fold_weight(Wk, win_d, C_K, D, V_G1, "Wk")
                fold_weight(Wv, win_d, C_V, D, V_G1, "Wv")
                fold_weight(Wx, win_d, C_XL, D, V_G1, "Wx")
                fold_weight(Wf, win_d, C_F, NH, V_G1, "Wf")
                S.run("P0")

            with ExitStack() as s1:
                xt = [sb("xt%d" % i, [128, D], F32, s1) for i in range(4)]
                xn = [sb("xn%d" % i, [128, D], BF16, s1) for i in range(4)]
                junk = sb("junk", [128, D], BF16, s1)
                ss = sb("ss", [128, 2, 4], F32, s1)
                rstd = sb("rstd", [128, 2, 4], F32, s1)
                uT = [sb("uT%d" % i, [128, KC, 512], BF16, s1) for i in range(2)]
                kst = sb("kst", [128, NH, 512], BF16, s1)
                vst = sb("vst", [128, 4, D], BF16, s1)
                xl = [sb("xl%d" % i, [128, KC, 515], BF16, s1) for i in range(2)]
                xa = [sb("xa%d" % i, [128, 512], BF16, s1) for i in range(KC)]
                Abuf = sb("Abuf", [128, KC, 512], F32, s1)
                A2buf = sb("A2buf", [128, KC, 512], F32, s1)
                Tbuf = sb("Tbuf", [128, KC, 512], BF16, s1)
                TH = [sb("TH%d" % i, [128, 512], F32, s1) for i in range(2)]
                Hf = [sb("Hf%d" % i, [128, 512], F32, s1) for i in range(2)]
                hst = sb("hst", [128, KC], F32, s1)
                Hsel = sb("Hsel", [128, KC, 512], BF16, s1)
                zt = sb("zt", [128, NH], F32, s1)
                pT = [ps_("pT%d" % i, [128, KC, 128], BF16, s1) for i in range(2)]
                pM = [ps_("pM%d" % i, [128, 512], F32, s1) for i in range(5)]
                pF = ps_("pF", [128, NH], F32, s1)
                pmc = [0, 0]

                def next_pm():
                    k = pmc[0] % 3
                    pmc[0] += 1
                    return k

                def next_pl():
                    k = 3 + pmc[1] % 2
                    pmc[1] += 1
                    return k

                S.op("pool", lambda e: e.memset(xl[0][:, :, 0:3], 0.0), writes=[("xl", 0, oc) for oc in range(KC)])

                def front(t):
                    p = t % 2
                    for tb in range(4):
                        blk = 4 * t + tb
                        S.dma("sp", lambda e, tb=tb, blk=blk: [e.dma_start(out=xt[tb][:], in_=x_d[blk * 128:(blk + 1) * 128, :])],
                              writes=[("xt", tb)], sem="d_xt%d" % tb)
                        S.op("act", lambda e, tb=tb, p=p: e.activation(out=junk[:], in_=xt[tb][:], func=AF.Square, accum_out=ss[:, p, tb:tb + 1]),
                             reads=[("xt", tb)], writes=["junk", ("ss", p, tb)])
                    S.op("dve", lambda e, p=p: e.tensor_scalar(out=rstd[:, p, :], in0=ss[:, p, :], scalar1=1.0 / D, scalar2=EPS, op0=ALU.mult, op1=ALU.add),
                         reads=[("ss", p, tb) for tb in range(4)], writes=[("rstd", p)])
                    S.op("act", lambda e, p=p: e.activation(out=rstd[:, p, :], in_=rstd[:, p, :], func=AF.Sqrt), reads=[("rstd", p)], writes=[("rstd", p)])
                    S.op("dve", lambda e, p=p: e.reciprocal(out=rstd[:, p, :], in_=rstd[:, p, :]), reads=[("rstd", p)], writes=[("rstd", p)])
                    for tb in range(4):
                        S.op("act", lambda e, tb=tb, p=p: e.mul(out=xn[tb][:], in_=xt[tb][:], mul=rstd[:, p, tb:tb + 1]),
                             reads=[("xt", tb), ("rstd", p)], writes=[("xn", tb)])
                    for tb in range(4):
                        q = tb % 2
                        for c in range(KC):
                            S.op("pe", lambda e, q=q, c=c, tb=tb: e.transpose(out=pT[q][:, c, :], in_=xn[tb][:, c * 128:(c + 1) * 128], identity=identb[:]),
                                 reads=[("xn", tb), "identb"], writes=[("pT", q)], sig=(c == KC - 1))
                        S.op("dve", lambda e, q=q, p=p, tb=tb: e.tensor_copy(out=uT[p][:, :, tb * 128:(tb + 1) * 128], in_=pT[q][:]),
                             reads=[("pT", q)], writes=[("uT", p, tb)])

                def mid(t):
                    p = t % 2
                    uTr = [("uT", p, tb) for tb in range(4)]
                    for h in range(NH):
                        k = next_pm()
                        for kc in range(KC):
                            S.op("pe", lambda e, k=k, kc=kc, h=h, p=p: e.matmul(pM[k][:], lhsT=Wk[:, kc, h * 128:(h + 1) * 128], rhs=uT[p][:, kc, :],
                                                                           start=(kc == 0), stop=(kc == KC - 1)),
                                 reads=uTr + ["Wk"], writes=[("pM", k)], sig=(kc == KC - 1))
                        if h % 2 == 0:
                            S.op("act", lambda e, k=k, h=h: e.copy(out=kst[:, h, :], in_=pM[k][:]), reads=[("pM", k)], writes=[("kst", h)])
                        else:
                            S.op("dve", lambda e, k=k, h=h: e.tensor_copy(out=kst[:, h, :], in_=pM[k][:]), reads=[("pM", k)], writes=[("kst", h)])
                    S.dma("sp", lambda e, t=t: [e.dma_start(out=kt_s[:, :, t * 512:(t + 1) * 512], in_=kst[:])],
                          reads=[("kst", h) for h in range(NH)], writes=["kt_s"], sem="d_kst")
                    for tb in range(4):
                        for hf in range(2):
                            k = next_pm()
                            for kc in range(KC):
                                S.op("pe", lambda e, k=k, kc=kc, tb=tb, hf=hf, p=p: e.matmul(pM[k][:], lhsT=uT[p][:, kc, tb * 128:(tb + 1) * 128],
                                                                                        rhs=Wv[:, kc, hf * 512:(hf + 1) * 512], start=(kc == 0), stop=(kc == KC - 1)),
                                     reads=[("uT", p, tb), "Wv"], writes=[("pM", k)], sig=(kc == KC - 1))
                            if hf == 0:
                                S.op("dve", lambda e, k=k, tb=tb, hf=hf: e.tensor_copy(out=vst[:, tb, hf * 512:(hf + 1) * 512], in_=pM[k][:]),
                                     reads=[("pM", k)], writes=[("vst", tb)])
                            else:
                                S.op("act", lambda e, k=k, tb=tb, hf=hf: e.copy(out=vst[:, tb, hf * 512:(hf + 1) * 512], in_=pM[k][:]),
                                     reads=[("pM", k)], writes=[("vst", tb)])
                        for kc in range(KC):
                            S.op("pe", lambda e, kc=kc, tb=tb, p=p: e.matmul(pF[:], lhsT=uT[p][:, kc, tb * 128:(tb + 1) * 128], rhs=Wf[:, kc, :],
                                                                        start=(kc == 0), stop=(kc == KC - 1)),
                                 reads=[("uT", p, tb), "Wf"], writes=["pF"], sig=(kc == KC - 1))
                        S.op("dve", lambda e: e.tensor_tensor(out=zt[:], in0=pF[:], in1=fbb[:], op=ALU.add), reads=["pF", "fbb"], writes=["zt"])
                        S.op("act", lambda e, blk=4 * t + tb: e.activation(out=E[:, blk, :], in_=zt[:], func=AF.Exp, scale=-1.0), reads=["zt"], writes=["E"])
                    S.dma("sp", _n(lambda e, t=t: [e.dma_start(out=v_s[:, :, 4 * t + tb, :].rearrange("h p d -> p h d"),
                                                               in_=vst[:, tb, :].rearrange("p (h d) -> p h d", h=NH)) for tb in range(4)], 4),
                          reads=[("vst", tb) for tb in range(4)], writes=["v_s"], sem="d_vst")
                    for oc in range(KC):
                        k = next_pm()
                        for kc in range(KC):
                            S.op("pe", lambda e, k=k, kc=kc, oc=oc, p=p: e.matmul(pM[k][:], lhsT=Wx[:, kc, oc * 128:(oc + 1) * 128], rhs=uT[p][:, kc, :],
                                                                            start=(kc == 0), stop=(kc == KC - 1)),
                                 reads=uTr + ["Wx"], writes=[("pM", k)], sig=(kc == KC - 1))
                        eng = "act" if oc % 2 == 0 else "dve"
                        if eng == "act":
                            S.op("act", lambda e, k=k, oc=oc, p=p: e.copy(out=xl[p][:, oc, 3:515], in_=pM[k][:]), reads=[("pM", k)], writes=[("xl", p, oc)])
                        else:
                            S.op("dve", lambda e, k=k, oc=oc, p=p: e.tensor_copy(out=xl[p][:, oc, 3:515], in_=pM[k][:]), reads=[("pM", k)], writes=[("xl", p, oc)])

                def lru1(t):
                    p = t % 2
                    for oc in range(KC):
                        k = next_pl()
                        for j in range(4):
                            S.op("pe", lambda e, k=k, j=j, oc=oc, p=p: e.matmul(pM[k][:], lhsT=diag[:, oc * 4 + j, :], rhs=xl[p][:, oc, j:j + 512], start=(j == 0), stop=(j == 3)),
                                 reads=[("xl", p, oc), "diag"], writes=[("pM", k)], sig=(j == 3))
                        S.op("act", lambda e, k=k, oc=oc: e.activation(out=xa[oc][:], in_=pM[k][:], func=AF.Identity, bias=vecT[:, oc, V_CB:V_CB + 1], scale=1.0),
                             reads=[("pM", k), "vecT"], writes=[("xa", oc)])
                        S.op("pool", lambda e, oc=oc, p=p: e.tensor_copy(out=xl[1 - p][:, oc, 0:3], in_=xl[p][:, oc, 512:515]), reads=[("xl", p, oc)], writes=[("xl", 1 - p, oc)])
                    for oc in range(KC):
                        q = oc % 2
                        kr = next_pl()
                        S.op("pe", lambda e, kr=kr, oc=oc: e.matmul(pM[kr][:], lhsT=BDa[:, oc, :], rhs=xa[oc][:], start=True, stop=True),
                             reads=[("xa", oc), "BDa"], writes=[("pM", kr)])
                        ki = next_pl()
                        S.op("pe", lambda e, ki=ki, oc=oc: e.matmul(pM[ki][:], lhsT=BDx[:, oc, :], rhs=xa[oc][:], start=True, stop=True),
                             reads=[("xa", oc), "BDx"], writes=[("pM", ki)])
                        S.op("act", lambda e, kr=kr, oc=oc: e.activation(out=Abuf[:, oc, :], in_=pM[kr][:], func=AF.Tanh, bias=dv[:, oc, 0:1], scale=0.5),
                             reads=[("pM", kr), "dv0"], writes=[("A", oc)])
                        S.op("act", lambda e, ki=ki, oc=oc, q=q: e.activation(out=TH[q][:], in_=pM[ki][:], func=AF.Tanh, bias=dv[:, oc, 1:2], scale=0.5),
                             reads=[("pM", ki), "dv1"], writes=[("TH", q)])
                        S.op("act", lambda e, oc=oc: e.activation(out=Abuf[:, oc, :], in_=Abuf[:, oc, :], func=AF.Exp, bias=dv[:, oc, 2:3], scale=dv[:, oc, 2:3]),
                             reads=[("A", oc), "dv2"], writes=[("A", oc)])
                        S.op("pool", lambda e, oc=oc: e.tensor_tensor(out=A2buf[:, oc, :], in0=Abuf[:, oc, :], in1=Abuf[:, oc, :], op=ALU.mult),
                             reads=[("A", oc)], writes=[("A2", oc)])
                        S.op("dve", lambda e, oc=oc, q=q: e.scalar_tensor_tensor(out=Tbuf[:, oc, :], in0=TH[q][:], scalar=1.0, in1=xa[oc][:], op0=ALU.add, op1=ALU.mult),
                             reads=[("TH", q), ("xa", oc)], writes=[("T", oc)])

                def lru2(t):
                    p = t % 2
                    i = t // 2
                    for oc in range(KC):
                        S.op("act", lambda e, oc=oc: e.activation(out=A2buf[:, oc, :], in_=A2buf[:, oc, :], func=AF.Sqrt, bias=0.25, scale=-0.25),
                             reads=[("A2", oc)], writes=[("A2", oc)])
                    for oc in range(KC):
                        S.op("dve", lambda e, oc=oc: e.tensor_tensor(out=A2buf[:, oc, :], in0=A2buf[:, oc, :], in1=Tbuf[:, oc, :], op=ALU.mult),
                             reads=[("A2", oc), ("T", oc)], writes=[("A2", oc)])
                    for oc in range(KC):
                        q = oc % 2
                        init = 0.0 if t == 0 else hst[:, oc:oc + 1]
                        S.op("dve", lambda e, oc=oc, q=q, init=init: e.tensor_tensor_scan(out=Hf[q][:], data0=Abuf[:, oc, :], data1=A2buf[:, oc, :], initial=init,
                                                                                      op0=ALU.mult, op1=ALU.add),
                             reads=[("A", oc), ("A2", oc), ("hst", oc)], writes=[("Hf", q)])
                        S.op("act", lambda e, oc=oc, q=q: e.copy(out=hst[:, oc:oc + 1], in_=Hf[q][:, 511:512]), reads=[("Hf", q)], writes=[("hst", oc)])
                        if p == 0:
                            S.op("pool", lambda e, oc=oc, q=q: e.tensor_tensor(out=Hsel[:, oc, :], in0=Hf[q][:], in1=selB[0][:], op=ALU.mult),
                                 reads=[("Hf", q), ("selB", 0)], writes=[("Hsel", oc)])
                        else:
                            S.op("pool", lambda e, oc=oc, q=q: e.tensor_tensor(out=Hf[q][:], in0=Hf[q][:], in1=selB[1][:], op=ALU.mult),
                                 reads=[("Hf", q), ("selB", 1)], writes=[("Hf", q)])
                            S.op("pool", lambda e, oc=oc, q=q: e.tensor_tensor(out=Hsel[:, oc, :], in0=Hsel[:, oc, :], in1=Hf[q][:], op=ALU.add),
                                 reads=[("Hf", q), ("Hsel", oc)], writes=[("Hsel", oc)])
                    if p == 1:
                        S.dma("sp", lambda e, i=i: [e.dma_start(out=hs_s[i], in_=Hsel[:])],
                              reads=[("Hsel", oc) for oc in range(KC)], writes=["hs_s"], sem="d_hsel")

                nt_run = NT if stop_after != "A1x" else 2
                def lru(t):
                    lru1(t)
                    lru2(t)

                front(0)
                mid(0)
                front(1)
                for t in range(1, nt_run):
                    S.interleave(S.capture(mid, t), S.capture(front, t + 1) if t + 1 < nt_run else [], S.capture(lru, t - 1), spans=[1.0, 0.55, 1.0])
                lru(nt_run - 1)
                S.run("A1")
                if stop_after in ("A1", "A1x"):
                    S.dma("sp", lambda e: [e.dma_start(out=dbg_d[:, 0:512], in_=E[:].rearrange("p b h -> p (b h)"))], reads=["E"], writes=["dbg"], sem="d_dbg")
                    S.run("dbgA1")
                    return nc

            with ExitStack() as s2:
                Wq = sb("Wq", [128, KC, D], BF16, s2)
                ut2 = [sb("ut2_%d" % i, [128, KC, 512], BF16, s2) for i in range(2)]
                qst = [sb("qst%d" % i, [128, NH, 512], BF16, s2) for i in range(2)]
                pQ = [ps_("pQ%d" % i, [128, 512], F32, s2) for i in range(4)]
                S.dma("pool", _n(lambda e: [e.dma_start(out=Wq[:, :, hf * 512:(hf + 1) * 512],
                                                        in_=win_d[:, C_Q + hf * 512:C_Q + (hf + 1) * 512].rearrange("(kc p) n -> p kc n", p=128)) for hf in range(2)], 2),
                      writes=["Wq"], sem="d_wq")
                xt2 = sb("xtA2", [128, 4, D], F32, s2)
                xnA = [sb("xnA%d" % i, [128, D], BF16, s2) for i in range(2)]
                junkA = sb("junkA", [128, D], BF16, s2)
                ssA = sb("ssA", [128, 4], F32, s2)
                rsA = sb("rsA", [128, 4], F32, s2)
                pTA = [ps_("pTA%d" % i, [128, KC, 128], BF16, s2) for i in range(2)]
                cqc = [0]

                def a2_front(i):
                    sl = i % 2
                    S.dma("sp", lambda e, i=i: [e.dma_start(out=xt2[:], in_=xo_d[i * 512:(i + 1) * 512, :].rearrange("(tb p) c -> p tb c", p=128))],
                          writes=[("xt2", tb) for tb in range(4)], sem="d_xt2")
                    for tb in range(4):
                        S.op("act", lambda e, tb=tb: e.activation(out=junkA[:], in_=xt2[:, tb, :], func=AF.Square, accum_out=ssA[:, tb:tb + 1]),
                             reads=[("xt2", tb)], writes=["junkA", ("ssA", tb)])
                    S.op("dve", lambda e: e.tensor_scalar(out=rsA[:], in0=ssA[:], scalar1=1.0 / D, scalar2=EPS, op0=ALU.mult, op1=ALU.add),
                         reads=[("ssA", tb) for tb in range(4)], writes=["rsA"])
                    S.op("act", lambda e: e.activation(out=rsA[:], in_=rsA[:], func=AF.Sqrt), reads=["rsA"], writes=["rsA"])
                    S.op("dve", lambda e: e.reciprocal(out=rsA[:], in_=rsA[:]), reads=["rsA"], writes=["rsA"])
                    for tb in range(4):
                        q = tb % 2
                        S.op("act", lambda e, tb=tb, q=q: e.mul(out=xnA[q][:], in_=xt2[:, tb, :], mul=rsA[:, tb:tb + 1]),
                             reads=[("xt2", tb), "rsA"], writes=[("xnA", q)])
                        for c in range(KC):
                            S.op("pe", lambda e, q=q, c=c: e.transpose(out=pTA[q][:, c, :], in_=xnA[q][:, c * 128:(c + 1) * 128], identity=identb[:]),
                                 reads=[("xnA", q), "identb"], writes=[("pTA", q)], sig=(c == KC - 1))
                        for c in range(KC):
                            S.op("dve", lambda e, q=q, sl=sl, tb=tb, c=c: e.tensor_scalar(out=ut2[sl][:, c, tb * 128:(tb + 1) * 128], in0=pTA[q][:, c, :],
                                                                                       scalar1=vecT[:, c, V_G1:V_G1 + 1], scalar2=None, op0=ALU.mult),
                                 reads=[("pTA", q), "vecT"], writes=[("ut2", sl, tb, c)])
                    S.dma("sp", lambda e, i=i, sl=sl: [e.dma_start(out=ut_s[i], in_=ut2[sl][:])],
                          reads=[("ut2", sl, tb, c) for tb in range(4) for c in range(KC)], writes=["ut_s"], sem="d_uts%d" % sl)

                def a2_q(i):
                    sl = i % 2
                    for h in range(NH):
                        k = cqc[0] % 4
                        cqc[0] += 1
                        for kc in range(KC):
                            S.op("pe", lambda e, k=k, kc=kc, h=h, sl=sl: e.matmul(pQ[k][:], lhsT=Wq[:, kc, h * 128:(h + 1) * 128], rhs=ut2[sl][:, kc, :],
                                                                            start=(kc == 0), stop=(kc == KC - 1)),
                                 reads=[("ut2", sl, tb, kc) for tb in range(4)] + ["Wq"], writes=[("pQ", k)], sig=(kc == KC - 1))
                        if h % 2 == 0:
                            S.op("act", lambda e, k=k, h=h, sl=sl: e.mul(out=qst[sl][:, h, :], in_=pQ[k][:], mul=QSCALE), reads=[("pQ", k)], writes=[("qst", sl, h)])
                        else:
                            S.op("dve", lambda e, k=k, h=h, sl=sl: e.tensor_scalar(out=qst[sl][:, h, :], in0=pQ[k][:], scalar1=QSCALE, scalar2=None, op0=ALU.mult),
                                 reads=[("pQ", k)], writes=[("qst", sl, h)])
                    S.dma("sp", lambda e, i=i, sl=sl: [e.dma_start(out=q_s[:, :, i * 512:(i + 1) * 512], in_=qst[sl][:])],
                          reads=[("qst", sl, h) for h in range(NH)], writes=["q_s"], sem="d_qst%d" % sl)

                a2_front(0)
                for i in range(NOWN):
                    S.interleave(S.capture(a2_q, i), S.capture(a2_front, i + 1) if i + 1 < NOWN else [])
                S.run("A2")
                if stop_after == "A2":
                    return nc

        with ExitStack() as sB:
            negF = sb("negF", [128, NB, NH], F32, sB)
            Fb = sb("Fb", [128, NOWN, 4, 2, NH], BF16, sB)
            cT = sb("cT", [128, NOWN, 512], BF16, sB)
            maskb = sb("maskb", [128, 8, 512], BF16, sB)
            cselb = sb("cselb", [128, NH, 128], BF16, sB)
            with ExitStack() as sb0:
                tot = sb("tot", [128, NB, NH], F32, sb0)
                inc = sb("inc", [128, NB, NH], F32, sb0)
                ones64 = sb("ones64", [128, NB], F32, sb0)
                pc1 = ps_("pc1", [128, 512], F32, sb0)
                pc2 = ps_("pc2", [128, 512], F32, sb0)
                pX = ps_("pX", [128, 512], BF16, sb0)
                Efl = E[:].rearrange("p b h -> p (b h)")
                S.op("pool", lambda e: e.memset(cT[:], 0.0), writes=["cT"])
                S.op("pool", lambda e: e.memset(cselb[:], 0.0), writes=["cselb"])
                S.op("pool", lambda e: e.memset(ones64[:], 1.0), writes=["ones64"])
                S.dma("pool", _n(lambda e: [
                    e.dma_start(out=maskb[:], in_=masks_d.rearrange("m p q -> p m q")),
                    e.dma_start(out=cselb[0:16, :, :], in_=csel_d),
                ], 2), reads=[], writes=["maskb", "cselb"], sem="d_constB")
                S.op("act", lambda e: e.activation(out=Efl, in_=Efl, func=AF.Ln, bias=1.0, scale=1.0), reads=["E"], writes=["E"])
                S.op("pe", lambda e: e.matmul(pc1[:], lhsT=trif[:], rhs=Efl, start=True, stop=True), reads=["E", "trif"], writes=["pc1"])
                S.op("pe", lambda e: e.matmul(pc2[:], lhsT=onesf[:], rhs=Efl, start=True, stop=True), reads=["E", "onesf"], writes=["pc2"])
                S.op("dve", lambda e: e.tensor_copy(out=tot[:].rearrange("p b h -> p (b h)"), in_=pc2[:]), reads=["pc2"], writes=["tot"])
                for h in range(NH):
                    S.op("dve", lambda e, h=h: e.tensor_tensor_scan(out=inc[:, :, h], data0=ones64[:], data1=tot[:, :, h], initial=0.0, op0=ALU.mult, op1=ALU.add),
                         reads=["tot", "ones64"], writes=[("inc", h)])
                S.op("dve", lambda e: e.tensor_tensor(out=inc[:], in0=inc[:], in1=tot[:], op=ALU.subtract),
                     reads=[("inc", h) for h in range(NH)] + ["tot"], writes=["incx"])
                S.op("dve", lambda e: e.tensor_tensor(out=negF[:].rearrange("p b h -> p (b h)"), in0=pc1[:], in1=inc[:].rearrange("p b h -> p (b h)"), op=ALU.add),
                     reads=["pc1", "incx"], writes=["negF"])
                for i in range(NOWN):
                    S.op("dve", lambda e, i=i: e.tensor_scalar(out=Fb[:, i], in0=negF[:, 8 * i:8 * i + 8, :].rearrange("p (p2 tb) h -> p tb p2 h", p2=2),
                                                            scalar1=-1.0, scalar2=None, op0=ALU.mult),
                         reads=["negF"], writes=[("Fb", i)])
                    for tb in range(4):
                        S.op("pe", lambda e, i=i, tb=tb: e.transpose(out=pX[0:16, tb * 128:(tb + 1) * 128], in_=Fb[:, i, tb].rearrange("p a h -> p (a h)"), identity=identb[:]),
                             reads=[("Fb", i), "identb"], writes=["pX"], sig=(tb == 3))
                    S.op("dve", lambda e, i=i: e.tensor_copy(out=cT[0:16, i, :], in_=pX[0:16, :]), reads=["pX"], writes=["cT"])
                if debug:
                    S.dma("sp", _n(lambda e: [e.dma_start(out=dbg_d[:, 512:1024], in_=negF[:].rearrange("p b h -> p (b h)")),
                                              e.dma_start(out=dbg_d[0:16, 1024:1536], in_=cT[0:16, 0, :].bitcast(F32) if False else negF[0:16, :, :].rearrange("p b h -> p (b h)"))], 2),
                          reads=["negF", "cT"], writes=["dbg"], sem="d_dbg")
                S.run("B0")

            with ExitStack() as sb1:
                KT = [sb("KT%d" % i, [128, SEQ], BF16, sb1) for i in range(2)]
                Vh = [sb("Vh%d" % i, [128, NB, 128], BF16, sb1) for i in range(2)]
                Qh = [sb("Qh%d" % i, [128, OWN], BF16, sb1) for i in range(2)]
                pt = [sb("pt%d" % i, [128, 512], BF16, sb1) for i in range(3)]
                rl = [sb("rl%d" % i, [128, 512], F32, sb1) for i in range(2)]
                acc = [sb("acc%d" % i, [128, 512], F32, sb1) for i in range(2)]
                obst = [sb("obst%d" % i, [128, 512], BF16, sb1) for i in range(2)]
                pS = [ps_("pS%d" % i, [128, 512], F32, sb1) for i in range(4)]
                pO = [ps_("pO%d" % i, [128, 512], F32, sb1) for i in range(2)]
                pL = [ps_("pL%d" % i, [128, 512], F32, sb1) for i in range(2)]

                heads = range(NH) if stop_after != "Bx" else range(2)
                blocks = []
                for h in heads:
                    for i in range(NOWN):
                        for kb in range(8 * i + 8):
                            blocks.append((h, i, kb))
                LA = 2

                def load_head(h):
                    sl = h % 2
                    S.dma("sp", lambda e, h=h, sl=sl: [e.dma_start(out=KT[sl][:], in_=kt_s[:, h, :])], reads=["kt_s"], writes=[("KT", sl)], sem="d_kt%d" % sl)
                    S.dma("sp", lambda e, h=h, sl=sl: [e.dma_start(out=Vh[sl][:], in_=v_s[h])], reads=["v_s"], writes=[("Vh", sl)], sem="d_vh%d" % sl)
                    S.dma("sp", lambda e, h=h, sl=sl: [e.dma_start(out=Qh[sl][:], in_=q_s[:, h, :])], reads=["q_s"], writes=[("Qh", sl)], sem="d_qh%d" % sl)

                def emit_S(n):
                    h, i, kb = blocks[n]
                    sl = h % 2
                    s_ = n % 4
                    pslot = n % 3
                    diag_blk = kb >= 8 * i
                    S.op("pe", lambda e: e.matmul(pS[s_][:], lhsT=KT[sl][:, kb * 128:(kb + 1) * 128], rhs=Qh[sl][:, i * 512:(i + 1) * 512], start=True, stop=False),
                         reads=[("KT", sl), ("Qh", sl)], writes=[("pS", s_)], sig=False)
                    S.op("pe", lambda e: e.matmul(pS[s_][:], lhsT=cselb[:, h, :], rhs=cT[:, i, :], start=False, stop=not diag_blk),
                         reads=["cselb", "cT"], writes=[("pS", s_)], sig=not diag_blk)
                    if diag_blk:
                        S.op("pe", lambda e: e.matmul(pS[s_][:], lhsT=identb[:], rhs=maskb[:, kb - 8 * i, :], start=False, stop=True),
                             reads=["identb", "maskb"], writes=[("pS", s_)], sig=True)
                    S.op("act", lambda e: e.activation(out=pt[pslot][:], in_=pS[s_][:], func=AF.Exp, bias=negF[:, kb, h:h + 1], scale=1.0),
                         reads=[("pS", s_), "negF"], writes=[("pt", pslot)])

                def emit_PV(n):
                    h, i, kb = blocks[n]
                    sl = h % 2
                    pslot = n % 3
                    nkb = 8 * i + 8
                    o = (h * NOWN + i) % 2
                    last = kb == nkb - 1
                    S.op("pe", lambda e: e.matmul(pO[o][:], lhsT=Vh[sl][:, kb, :], rhs=pt[pslot][:], start=(kb == 0), stop=last),
                         reads=[("Vh", sl), ("pt", pslot)], writes=[("pO", o)], sig=True)
                    if kb == 0:
                        S.op("dve", lambda e: e.tensor_copy(out=acc[o][:], in_=pt[pslot][:]), reads=[("pt", pslot)], writes=[("acc", o)])
                    elif kb % 4 != 3:
                        S.op("dve", lambda e: e.tensor_tensor(out=acc[o][:], in0=acc[o][:], in1=pt[pslot][:], op=ALU.add), reads=[("pt", pslot), ("acc", o)], writes=[("acc", o)])
                    else:
                        S.op("pe", lambda e: e.matmul(pL[o][:], lhsT=onesb[:], rhs=pt[pslot][:], start=(kb == 3), stop=False),
                             reads=["onesb", ("pt", pslot)], writes=[("pL", o)], sig=True)
                    if last:
                        S.op("pe", lambda e: e.matmul(pL[o][:], lhsT=onesf[:], rhs=acc[o][:], start=False, stop=True), reads=["onesf", ("acc", o)], writes=[("pL", o)])
                        S.op("dve", lambda e: e.reciprocal(out=rl[o][:], in_=pL[o][:]), reads=[("pL", o)], writes=[("rl", o)])
                        S.op("dve", lambda e: e.tensor_tensor(out=obst[o][:], in0=pO[o][:], in1=rl[o][:], op=ALU.mult), reads=[("pO", o), ("rl", o)], writes=[("obst", o)])
                        S.dma("sp", lambda e: [e.dma_start(out=ob_s[i, :, h, :], in_=obst[o][:])], reads=[("obst", o)], writes=["ob_s"], sem="d_ob%d" % o)

                cvtC = [sb("cvtC%d" % i, [128, KC, 512], BF16, sb1) for i in range(3)]
                wcnt = [0]

                def prep_weight(src_d, r0, c0, dsts):
                    sl = wcnt[0] % 3
                    wcnt[0] += 1
                    S.dma("pool", lambda e: [e.dma_start(out=cvtC[sl][:], in_=src_d[r0:r0 + 1024, c0:c0 + 512].rearrange("(kc p) n -> p kc n", p=128))],
                          writes=[("cvtC", sl)], sem="d_cvl%d" % sl)
                    S.dma("pool", _n(lambda e: [e.dma_start(out=d_ap, in_=cvtC[sl][:, :, lo:hi]) for (d_ap, lo, hi) in dsts], len(dsts)),
                          reads=[("cvtC", sl)], writes=["wscr"], sem="d_cvs%d" % sl)

                def halves(scr, j):
                    return [(scr[2 * j], 0, 256), (scr[2 * j + 1], 256, 512)]

                if stop_after not in ("B", "Bx"):
                    for hf in range(2):
                        prep_weight(win_d, 0, C_GL + hf * 512, halves(wgl_s, hf))
                    for hf in range(4):
                        prep_weight(win_d, 0, C_GA + hf * 512, halves(wgg_s, hf))
                    for (dst, src) in ((wa_s, wa_d), (wb_s, wb_d), (wo_s, wo_d)):
                        for hf in range(2):
                            prep_weight(src, 0, hf * 512, halves(dst, hf))
                    for hf in range(8):
                        prep_weight(wup_d, 0, hf * 512, [(wup_s[hf], 0, 512)])
                    for g in range(4):
                        for hf in range(2):
                            prep_weight(wdn_d, g * 1024, hf * 512, [(wdn_s[hf * 4 + g], 0, 512)])

                hl = list(heads)
                load_head(hl[0])
                for n in range(len(blocks) + LA):
                    if n < len(blocks):
                        emit_S(n)
                    if n >= LA:
                        emit_PV(n - LA)
                        h, i, kb = blocks[n - LA]
                        if i == 0 and kb == 0 and hl.index(h) + 1 < len(hl):
                            load_head(hl[hl.index(h) + 1])
                S.run("B")
                if stop_after in ("B", "Bx", "Bp"):
                    return nc

        with ExitStack() as sC:
            NW = 3
            NWA = 8
            wst = [sb("wst%d" % i, [128, KC, 512], BF16, sC) for i in range(NW)]
            wsa = [sb("wsa%d" % i, [128, KC, 256], BF16, sC) for i in range(NWA)]
            g3b = sb("g3b", [128, D], F32, sC)
            xtC = [sb("xtC%d" % i, [128, 4, D], F32, sC) for i in range(2)]
            utC = sb("utC", [128, KC, 512], BF16, sC)
            hsC = sb("hsC", [128, KC, 512], BF16, sC)
            obC = sb("obC", [128, NH, 512], BF16, sC)
            gatC = sb("gatC", [128, KC, 512], BF16, sC)
            mT = sb("mT", [128, KC, 512], BF16, sC)
            m2T = sb("m2T", [128, KC, 512], BF16, sC)
            hT = sb("hT", [128, HC, 512], BF16, sC)
            xn2 = [sb("xn2_%d" % i, [128, D], BF16, sC) for i in range(2)]
            junk2 = sb("junk2", [128, D], BF16, sC)
            ggt = [sb("ggt%d" % i, [128, 512], BF16, sC) for i in range(2)]
            sg = [sb("sg%d" % i, [128, 512], F32, sC) for i in range(4)]
            rr = [sb("rr%d" % i, [128, 512], BF16, sC) for i in range(2)]
            ssC = sb("ssC", [128, 2, 2, 4], F32, sC)
            rsC = sb("rsC", [128, 2, 2, 4], F32, sC)
            pC = [ps_("pC%d" % i, [128, 512], F32, sC) for i in range(7)]
            pT2 = ps_("pT2", [128, KC, 128], BF16, sC)
            pcc = [0]
            wsc = [0]

            def next_pc():
                k = pcc[0] % 3
                pcc[0] += 1
                return k

            pcc2 = [0]

            def next_pc2():
                k = 3 + pcc2[0] % 4
                pcc2[0] += 1
                return k

            def load_w(src_ap):
                sl = wsc[0] % NW
                wsc[0] += 1
                S.dma("sp", lambda e: [e.dma_start(out=wst[sl][:], in_=src_ap)], reads=["wscr"], writes=[("wst", sl)], sem="d_wst%d" % sl)
                return sl

            wsca = [0]

            def load_wa(scr, c0):
                sl = wsca[0] % NWA
                wsca[0] += 1
                S.dma("sp", lambda e: [e.dma_start(out=wsa[sl][:], in_=scr[c0 // 256])],
                      reads=["wscr"], writes=[("wsa", sl)], sem="d_wsa%d" % sl)
                return sl

            def kcview(scr, r0, c0):
                return scr[r0:r0 + 1024, c0:c0 + 512].rearrange("(kc p) n -> p kc n", p=128)

            S.dma("sp", lambda e: [e.dma_start(out=xtC[1][0:1, 0, :], in_=g3_d)], writes=[("xtC", 1, 0)], sem="d_constC")
            for hf in range(2):
                k = next_pc()
                S.op("pe", lambda e, hf=hf, k=k: e.matmul(pC[k][:], lhsT=onesf[0:1, :], rhs=xtC[1][0:1, 0, hf * 512:(hf + 1) * 512], start=True, stop=True),
                     reads=["onesf", ("xtC", 1, 0)], writes=[("pC", k)])
                S.op("dve", lambda e, hf=hf, k=k: e.tensor_copy(out=g3b[:, hf * 512:(hf + 1) * 512], in_=pC[k][:]), reads=[("pC", k)], writes=["g3b"])

            def rms_stats(xs, slot):
                for tb in range(4):
                    S.op("act", lambda e, tb=tb: e.activation(out=junk2[:], in_=xtC[xs][:, tb, :], func=AF.Square, accum_out=ssC[:, xs, slot, tb:tb + 1]),
                         reads=[("xtC", xs, tb)], writes=["junk2", ("ssC", xs, slot, tb)])
                S.op("dve", lambda e: e.tensor_scalar(out=rsC[:, xs, slot, :], in0=ssC[:, xs, slot, :], scalar1=1.0 / D, scalar2=EPS, op0=ALU.mult, op1=ALU.add),
                     reads=[("ssC", xs, slot, tb) for tb in range(4)], writes=[("rsC", xs, slot)])
                S.op("act", lambda e: e.activation(out=rsC[:, xs, slot, :], in_=rsC[:, xs, slot, :], func=AF.Sqrt), reads=[("rsC", xs, slot)], writes=[("rsC", xs, slot)])
                S.op("dve", lambda e: e.reciprocal(out=rsC[:, xs, slot, :], in_=rsC[:, xs, slot, :]), reads=[("rsC", xs, slot)], writes=[("rsC", xs, slot)])

            def stage1(i):
                xs = i % 2
                X = xtC[xs]
                tsl = slice(i * 512, (i + 1) * 512)
                S.dma("sp", _n(lambda e: [
                    e.dma_start(out=utC[:], in_=ut_s[i]),
                    e.dma_start(out=hsC[:], in_=hs_s[i]),
                ], 2), reads=["ut_s", "hs_s"], writes=["utC", "hsC"], sem="d_actC")
                for oc in range(KC):
                    if oc % 2 == 0:
                        wsl = load_wa(wgl_s, (oc // 2) * 256)
                    k = next_pc()
                    for kc in range(KC):
                        S.op("pe", lambda e, k=k, kc=kc, oc=oc, wsl=wsl: e.matmul(pC[k][:], lhsT=wsa[wsl][:, kc, (oc % 2) * 128:(oc % 2 + 1) * 128], rhs=utC[:, kc, :],
                                                                            start=(kc == 0), stop=(kc == KC - 1)),
                             reads=[("wsa", wsl), "utC"], writes=[("pC", k)], sig=(kc == KC - 1))
                    q = oc % 2
                    S.op("act", lambda e, k=k, q=q: e.activation(out=ggt[q][:], in_=pC[k][:], func=AF.Gelu), reads=[("pC", k)], writes=[("ggt", q)])
                    S.op("pool", lambda e, oc=oc, q=q: e.tensor_tensor(out=gatC[:, oc, :], in0=ggt[q][:], in1=hsC[:, oc, :], op=ALU.mult),
                         reads=[("ggt", q), "hsC"], writes=[("gatC", oc)])
                S.dma("sp", lambda e: [e.dma_start(out=obC[:], in_=ob_s[i])], reads=["ob_s"], writes=["obC"], sem="d_actC2")
                gatR = [("gatC", oc) for oc in range(KC)]
                for oc in range(KC):
                    if oc % 2 == 0:
                        wa_ = load_wa(wa_s, (oc // 2) * 256)
                        wb_ = load_wa(wb_s, (oc // 2) * 256)
                        wga = load_wa(wgg_s, (oc // 2) * 256)
                        wgb = load_wa(wgg_s, 1024 + (oc // 2) * 256)
                    csl = slice((oc % 2) * 128, (oc % 2 + 1) * 128)
                    qa, qb = (2 * oc) % 4, (2 * oc + 1) % 4
                    for (wy, rhs_y, rtok, wg, qq) in ((wa_, gatC, gatR, wga, qa), (wb_, obC, ["obC"], wgb, qb)):
                        ky, kg = next_pc(), next_pc()
                        for kc in range(KC):
                            S.op("pe", lambda e, kc=kc, ky=ky, wy=wy, rhs_y=rhs_y, csl=csl: e.matmul(pC[ky][:], lhsT=wsa[wy][:, kc, csl], rhs=rhs_y[:, kc, :],
                                                                                             start=(kc == 0), stop=(kc == KC - 1)),
                                 reads=list(rtok) + [("wsa", wy)], writes=[("pC", ky)], sig=(kc == KC - 1))
                        for kc in range(KC):
                            S.op("pe", lambda e, kc=kc, kg=kg, wg=wg, csl=csl: e.matmul(pC[kg][:], lhsT=wsa[wg][:, kc, csl], rhs=utC[:, kc, :],
                                                                                  start=(kc == 0), stop=(kc == KC - 1)),
                                 reads=["utC", ("wsa", wg)], writes=[("pC", kg)], sig=(kc == KC - 1))
                        S.op("act", lambda e, kg=kg, qq=qq: e.activation(out=sg[qq][:], in_=pC[kg][:], func=AF.Sigmoid), reads=[("pC", kg)], writes=[("sg", qq)])
                        S.op("dve", lambda e, ky=ky, qq=qq: e.tensor_tensor(out=sg[qq][:], in0=pC[ky][:], in1=sg[qq][:], op=ALU.mult), reads=[("pC", ky), ("sg", qq)], writes=[("sg", qq)])
                    S.op("pool", lambda e, oc=oc, qa=qa, qb=qb: e.tensor_tensor(out=mT[:, oc, :], in0=sg[qa][:], in1=sg[qb][:], op=ALU.add),
                         reads=[("sg", qa), ("sg", qb)], writes=[("mT", oc)])
                S.dma("sp", lambda e: [e.dma_start(out=X[:], in_=xo_d[i * 512:(i + 1) * 512, :].rearrange("(tb p) c -> p tb c", p=128))],
                      writes=[("xtC", xs, tb) for tb in range(4)], sem="d_actC3_%d" % xs)
                mR = [("mT", oc) for oc in range(KC)]
                for qt in range(4):
                    wo_ = load_wa(wo_s, qt * 256)
                    for tb in range(4):
                        k = next_pc()
                        for kc in range(KC):
                            S.op("pe", lambda e, k=k, kc=kc, tb=tb, wo_=wo_: e.matmul(pC[k][:, 0:256], lhsT=mT[:, kc, tb * 128:(tb + 1) * 128], rhs=wsa[wo_][:, kc, :],
                                                                                start=(kc == 0), stop=(kc == KC - 1)),
                                 reads=mR + [("wsa", wo_)], writes=[("pC", k)], sig=(kc == KC - 1))
                        S.op("dve", lambda e, k=k, tb=tb, qt=qt: e.tensor_tensor(out=X[:, tb, qt * 256:(qt + 1) * 256], in0=X[:, tb, qt * 256:(qt + 1) * 256], in1=pC[k][:, 0:256], op=ALU.add),
                             reads=[("pC", k), ("xtC", xs, tb)], writes=[("xtC", xs, tb)])

            def stage2_norm(i):
                xs = i % 2
                X = xtC[xs]
                rms_stats(xs, 0)
                for tb in range(4):
                    q = tb % 2
                    S.op("act", lambda e, tb=tb, q=q: e.mul(out=xn2[q][:], in_=X[:, tb, :], mul=rsC[:, xs, 0, tb:tb + 1]),
                         reads=[("xtC", xs, tb), ("rsC", xs, 0)], writes=[("xn2", q)])
                    for c in range(KC):
                        S.op("pe", lambda e, q=q, c=c: e.transpose(out=pT2[:, c, :], in_=xn2[q][:, c * 128:(c + 1) * 128], identity=identb[:]),
                             reads=[("xn2", q), "identb"], writes=["pT2"], sig=(c == KC - 1))
                    for c in range(KC):
                        S.op("dve", lambda e, tb=tb, c=c: e.tensor_scalar(out=m2T[:, c, tb * 128:(tb + 1) * 128], in0=pT2[:, c, :], scalar1=vecT[:, c, V_G2:V_G2 + 1],
                                                                        scalar2=None, op0=ALU.mult),
                             reads=["pT2", "vecT"], writes=[("m2T", tb, c)])

            def stage2(i):
                xs = i % 2
                X = xtC[xs]
                chunks = [wup_s[c] for c in range(8)]
                chunks += [wdn_s[hf * 4 + g] for hf in range(2) for g in range(4)]
                slot_of = {}

                def use_chunk(n):
                    for m in (n, n + 1, n + 2):
                        if m < len(chunks) and m not in slot_of:
                            slot_of[m] = load_w(chunks[m])
                    return slot_of[n]

                for m in (0, 1):
                    slot_of[m] = load_w(chunks[m])
                m2R = [("m2T", tb, c) for tb in range(4) for c in range(KC)]
                for hc in range(HC):
                    if hc % 4 == 0:
                        wsl = use_chunk(hc // 4)
                    k = next_pc2()
                    for kc in range(KC):
                        S.op("pe", lambda e, k=k, kc=kc, hc=hc, wsl=wsl: e.matmul(pC[k][:], lhsT=wst[wsl][:, kc, (hc % 4) * 128:(hc % 4 + 1) * 128], rhs=m2T[:, kc, :],
                                                                            start=(kc == 0), stop=(kc == KC - 1)),
                             reads=m2R + [("wst", wsl)], writes=[("pC", k)], sig=(kc == KC - 1))
                    q = hc % 2
                    S.op("act", lambda e, k=k, q=q: e.activation(out=rr[q][:], in_=pC[k][:], func=AF.Relu), reads=[("pC", k)], writes=[("rr", q)])
                    S.op("pool", lambda e, hc=hc, q=q: e.tensor_tensor(out=hT[:, hc, :], in0=rr[q][:], in1=rr[q][:], op=ALU.mult), reads=[("rr", q)], writes=[("hT", hc)])
                for hf in range(2):
                    kd = [next_pc2() for _ in range(4)]
                    for g in range(4):
                        wsl = use_chunk(8 + hf * 4 + g)
                        for h8 in range(8):
                            hc = g * 8 + h8
                            for tb in range(4):
                                last = (g == 3 and h8 == 7)
                                S.op("pe", lambda e, hc=hc, h8=h8, tb=tb, wsl=wsl, k=kd[tb], first=(g == 0 and h8 == 0), last=last: e.matmul(
                                    pC[k][:], lhsT=hT[:, hc, tb * 128:(tb + 1) * 128], rhs=wst[wsl][:, h8, :], start=first, stop=last),
                                    reads=[("hT", hc), ("wst", wsl)], writes=[("pC", kd[tb])], sig=last)
                    for tb in range(4):
                        S.op("dve", lambda e, k=kd[tb], tb=tb, hf=hf: e.tensor_tensor(out=X[:, tb, hf * 512:(hf + 1) * 512], in0=X[:, tb, hf * 512:(hf + 1) * 512], in1=pC[k][:], op=ALU.add),
                             reads=[("pC", kd[tb]), ("xtC", xs, tb)], writes=[("xtC", xs, tb)])
                rms_stats(xs, 1)
                for tb in range(4):
                    S.op("dve", lambda e, tb=tb: e.scalar_tensor_tensor(out=X[:, tb, :], in0=X[:, tb, :], scalar=rsC[:, xs, 1, tb:tb + 1], in1=g3b[:], op0=ALU.mult, op1=ALU.mult),
                         reads=[("xtC", xs, tb), ("rsC", xs, 1), "g3b"], writes=[("xtC", xs, tb)])
                S.dma("pool", lambda e: [e.dma_start(out=out_d[i * 512:(i + 1) * 512, :].rearrange("(tb p) c -> p tb c", p=128), in_=X[:])],
                      reads=[("xtC", xs, tb) for tb in range(4)], writes=["out"], sem="d_out%d" % xs)

            n_own = NOWN if stop_after != "Cx" else 1
            def s1_then_norm(i):
                stage1(i)
                stage2_norm(i)

            s1_then_norm(0)
            for i in range(n_own):
                S.interleave(S.capture(stage2, i), S.capture(s1_then_norm, i + 1) if i + 1 < n_own else [], spans=[1.0, 0.85])
            S.run("C")
    return nc


_NC_CACHE = {}


def _prep_inputs(inp, ncores=8):
    f = lambda a: np.ascontiguousarray(np.asarray(a, dtype=np.float32))
    x = f(inp["x"])
    vecs = np.zeros((16, D), np.float32)
    vecs[0:4] = f(inp["conv_w"])
    vecs[4] = f(inp["conv_b"])
    vecs[5] = f(inp["lru_ba"])
    vecs[6] = f(inp["lru_bx"])
    vecs[7] = f(inp["lru_lambda"])
    vecs[8] = f(inp["norm_mix_g"])
    vecs[9] = f(inp["norm_mlp_g"])
    vecs[10] = f(inp["norm_final_g"])
    ident = np.eye(128, dtype=np.float32)
    tri = np.triu(np.ones((128, 128), np.float32))
    common = {
        "w_in": f(inp["w_in"]), "vecs": vecs, "g3row": f(inp["norm_final_g"])[None, :],
        "lru_wa": f(inp["lru_wa"]), "lru_wx": f(inp["lru_wx"]), "forget_b": f(inp["forget_b"])[None, :],
        "w_branch_a": f(inp["w_branch_a"]), "w_branch_b": f(inp["w_branch_b"]), "w_out": f(inp["w_out"]),
        "w_up": f(inp["w_up"]), "w_down": f(inp["w_down"]), "ident": ident, "tri": tri,
    }
    maps = []
    for c in range(ncores):
        b, j = c // 2, c % 2
        xb = x[b]
        xo = np.ascontiguousarray(xb.reshape(NT, 512, D)[j::2].reshape(OWN, D))
        sel = np.zeros((128, 2), np.float32)
        sel[:, j] = 1.0
        kpos = np.arange(8)[:, None, None] * 128 + np.arange(128)[None, :, None]
        qpos = j * 512 + np.arange(512)[None, None, :]
        masks = np.where(kpos <= qpos, 0.0, MASKV).astype(np.float32)
        csel = np.zeros((16, NH, 128), np.float32)
        for h in range(NH):
            csel[j * 8 + h, h, :] = 1.0
        m = dict(common)
        m.update({"x": np.ascontiguousarray(xb), "xo": xo, "sel": sel, "masks": masks, "csel": csel})
        maps.append(m)
    return maps


def kernel(**inputs):
    if "nc" not in _NC_CACHE:
        _NC_CACHE["nc"] = build_nc()
    nc = _NC_CACHE["nc"]
    maps = _prep_inputs(inputs, 8)
    res = run_bass_kernel_spmd(nc, maps, core_ids=list(range(8)))
    B = inputs["x"].shape[0]
    out = np.empty((B, SEQ, D), np.float32)
    ov = out.reshape(B, NT, 512, D)
    for c in range(8):
        b, j = c // 2, c % 2
        ov[b, j::2] = np.asarray(res.results[c]["out"], dtype=np.float32).reshape(NOWN, 512, D)
    return out
```

```python
import math
from contextlib import ExitStack

import numpy as np
import concourse.bass as bass
import concourse.mybir as mybir
from concourse.bass_utils import run_bass_kernel_spmd

F32 = mybir.dt.float32
BF16 = mybir.dt.bfloat16
AF = mybir.ActivationFunctionType
ALU = mybir.AluOpType
AX = mybir.AxisListType


class _Op:
    __slots__ = ("eng", "fn", "reads", "writes", "sig", "dsem", "key", "cnt", "waits", "clock")

    def __init__(self, eng, fn, reads, writes, sig, dsem):
        self.eng, self.fn, self.reads, self.writes, self.sig, self.dsem = eng, fn, reads, writes, sig, dsem


class Sched:
    ENGS = ("pe", "act", "dve", "pool", "sp")
    SAME_ENGINE_SYNC = True

    def __init__(self, nc):
        self.nc = nc
        self.ops = []
        self.sems = {}
        self.count = {}
        self.last_w = {}
        self.readers = {}
        self.known = {e: {} for e in self.ENGS}
        self.stack = ExitStack()
        self.n_wait = 0
        self.n_ops = 0

    def __enter__(self):
        self.stack.__enter__()
        return self

    def __exit__(self, *a):
        return self.stack.__exit__(*a)

    def _sem(self, key):
        if key not in self.sems:
            self.sems[key] = self.stack.enter_context(self.nc.semaphore("s_" + str(key)))
            self.count[key] = 0
        return self.sems[key]

    def op(self, eng, fn, reads=(), writes=(), sig=True):
        if eng != "pe":
            sig = True
        self.ops.append(_Op(eng, fn, tuple(reads), tuple(writes), sig, None))

    def dma(self, eng, fn, reads=(), writes=(), sem=None):
        assert sem is not None
        self.ops.append(_Op(eng, fn, tuple(reads), tuple(writes), True, sem))

    def capture(self, fn, *a):
        saved, self.ops = self.ops, []
        fn(*a)
        out, self.ops = self.ops, saved
        return out

    def interleave(self, *lists, spans=None):
        if spans is None:
            spans = [1.0] * len(lists)
        pairs = [(l, sp) for l, sp in zip(lists, spans) if l]
        lists = [p[0] for p in pairs]
        spans = [p[1] for p in pairs]
        pos = [0] * len(lists)
        for _ in range(sum(len(l) for l in lists)):
            j = min((spans[j] * pos[j] / len(lists[j]), j) for j in range(len(lists)) if pos[j] < len(lists[j]))[1]
            self.ops.append(lists[j][pos[j]])
            pos[j] += 1

    def run(self, name):
        ops, self.ops = self.ops, []
        self.last_w = {}
        self.readers = {}
        last = {}
        for o in ops:
            if o.dsem is None:
                last[o.eng] = o
        for o in last.values():
            o.sig = True

        def producers(o, last_w, readers):
            ps = []
            for t in o.reads:
                if t in last_w:
                    ps.append(last_w[t])
            for t in o.writes:
                if t in last_w:
                    ps.append(last_w[t])
                ps.extend(readers.get(t, ()))
            return ps

        def update(o, last_w, readers):
            for t in o.writes:
                last_w[t] = o
                readers[t] = []
            for t in o.reads:
                if t not in o.writes:
                    readers.setdefault(t, []).append(o)

        lw, rd = {}, {}
        for o in ops:
            for p in producers(o, lw, rd):
                if p.eng == "pe" and p.dsem is None and not p.sig and not (o.eng == "pe" and o.dsem is None):
                    p.sig = True
            update(o, lw, rd)
        pending = {e: [] for e in self.ENGS}
        for o in ops:
            if o.dsem is not None:
                self._sem(o.dsem)
                o.key = o.dsem
                self.count[o.dsem] += 16 * getattr(o.fn, "n", 1)
                o.cnt = self.count[o.dsem]
            else:
                self._sem(o.eng)
                o.key = o.eng
                pending[o.eng].append(o)
                if o.sig:
                    self.count[o.eng] += 1
                    for p in pending[o.eng]:
                        p.cnt = self.count[o.eng]
                    pending[o.eng] = []
        per_eng = {e: [] for e in self.ENGS}
        for o in ops:
            kn = self.known[o.eng]
            need = {}
            for p in producers(o, self.last_w, self.readers):
                k, c = p.key, p.cnt
                if k == "pe" and o.eng == "pe" and o.dsem is None:
                    continue
                if (not self.SAME_ENGINE_SYNC) and k == o.eng and o.dsem is None and p.dsem is None:
                    continue
                if kn.get(k, 0) >= c:
                    continue
                if need.get(k, (0, None))[0] < c:
                    need[k] = (c, p.clock)
            o.waits = []
            for k, (c, clk) in need.items():
                if kn.get(k, 0) >= c:
                    continue
                o.waits.append((k, c))
                for kk, cc in clk.items():
                    if kn.get(kk, 0) < cc:
                        kn[kk] = cc
            clock = dict(kn)
            clock[o.key] = max(clock.get(o.key, 0), o.cnt)
            o.clock = clock
            update(o, self.last_w, self.readers)
            per_eng[o.eng].append(o)
            self.n_wait += len(o.waits)
        self.n_ops += len(ops)
        final = dict(self.count)

        def emit(engname, e):
            for o in per_eng[engname]:
                for k, c in o.waits:
                    e.wait_ge(self.sems[k], c)
                r = o.fn(e)
                if o.dsem is not None:
                    assert len(r) == getattr(o.fn, "n", 1), (len(r), getattr(o.fn, "n", 1))
                    for ins in r:
                        ins.then_inc(self.sems[o.dsem], 16)
                elif o.sig:
                    r.then_inc(self.sems[o.eng], 1)
            kn = self.known[engname]
            for k, c in final.items():
                if c > 0 and kn.get(k, 0) < c:
                    e.wait_ge(self.sems[k], c)

        with self.nc.Block() as blk:
            blk.tensor(lambda e: emit("pe", e))
            blk.scalar(lambda e: emit("act", e))
            blk.vector(lambda e: emit("dve", e))
            blk.gpsimd(lambda e: emit("pool", e))
            blk.sync(lambda e: emit("sp", e))
        for e in self.ENGS:
            for k, c in final.items():
                if self.known[e].get(k, 0) < c:
                    self.known[e][k] = c


def _n(fn, n):
    fn.n = n
    return fn


SEQ = 8192
D = 1024
KC = 8
NB = SEQ // 128
NT = SEQ // 512
NOWN = NT // 2
OWN = NOWN * 512
DFF = 4096
HC = DFF // 128
NH = 8
D_IN = 7176
C_XL, C_GL, C_Q, C_K, C_V, C_GA, C_GB, C_F = 0, 1024, 2048, 3072, 4096, 5120, 6144, 7168
QSCALE = 1.0 / math.sqrt(128.0)
EPS = 1e-6
MASKV = -30000.0
V_CW, V_CB, V_BA, V_BX, V_LAM, V_G1, V_G2, V_G3 = 0, 4, 5, 6, 7, 8, 9, 10


def build_nc(debug=False, stop_after="C"):
    nc = bass.Bass("TRN2", target_bir_lowering=False)
    dk = "ExternalOutput" if debug else "Internal"

    def din(name, shape, dt=F32):
        return nc.dram_tensor(name, list(shape), dt, kind="ExternalInput").ap()

    def dscr(name, shape, dt=BF16):
        return nc.dram_tensor(name, list(shape), dt, kind=dk).ap()

    x_d = din("x", [SEQ, D])
    xo_d = din("xo", [OWN, D])
    win_d = din("w_in", [D, D_IN])
    vecs_d = din("vecs", [16, D])
    g3_d = din("g3row", [1, D])
    lwa_d = din("lru_wa", [16, 64, 64])
    lwx_d = din("lru_wx", [16, 64, 64])
    fb_d = din("forget_b", [1, NH])
    wa_d = din("w_branch_a", [D, D])
    wb_d = din("w_branch_b", [D, D])
    wo_d = din("w_out", [D, D])
    wup_d = din("w_up", [D, DFF])
    wdn_d = din("w_down", [DFF, D])
    ident_d = din("ident", [128, 128])
    tri_d = din("tri", [128, 128])
    sel_d = din("sel", [128, 2])
    masks_d = din("masks", [8, 128, 512])
    csel_d = din("csel", [16, NH, 128])
    out_d = nc.dram_tensor("out", [OWN, D], F32, kind="ExternalOutput").ap()

    kt_s = dscr("kt_s", [128, NH, SEQ])
    v_s = dscr("v_s", [NH, 128, NB, 128])
    q_s = dscr("q_s", [128, NH, OWN])
    ut_s = dscr("ut_s", [NOWN, 128, KC, 512])
    hs_s = dscr("hs_s", [NOWN, 128, KC, 512])
    ob_s = dscr("ob_s", [NOWN, 128, NH, 512])
    wq_s = dscr("wq_s", [D, D])
    wgl_s = dscr("wgl_s", [4, 128, KC, 256])
    wgg_s = dscr("wgg_s", [8, 128, KC, 256])
    wa_s = dscr("wa_s", [4, 128, KC, 256])
    wb_s = dscr("wb_s", [4, 128, KC, 256])
    wo_s = dscr("wo_s", [4, 128, KC, 256])
    wup_s = dscr("wup_s", [8, 128, KC, 512])
    wdn_s = dscr("wdn_s", [8, 128, KC, 512])
    dbg_d = nc.dram_tensor("dbg", [128, 4096], F32, kind="ExternalOutput").ap() if debug else None

    S = Sched(nc)
    es = ExitStack()

    def sb(name, shape, dt, stack=None):
        return (stack or es).enter_context(nc.sbuf_tensor(name, list(shape), dt))

    def ps_(name, shape, dt, stack=None):
        return (stack or es).enter_context(nc.psum_tensor(name, list(shape), dt))

    with S, es:
        identf = sb("identf", [128, 128], F32)
        identb = sb("identb", [128, 128], BF16)
        onesf = sb("onesf", [128, 128], F32)
        onesb = sb("onesb", [128, 128], BF16)
        trif = sb("trif", [128, 128], F32)
        selt = sb("selt", [128, 2], F32)
        selB = [sb("selB%d" % i, [128, 512], F32) for i in range(2)]
        vecT = sb("vecT", [128, KC, 16], F32)
        dv = sb("dv", [128, KC, 4], F32)
        fbb = sb("fbb", [128, NH], F32)
        E = sb("E", [128, NB, NH], F32)

        with ExitStack() as sA:
            Wx = sb("Wx", [128, KC, D], BF16, sA)
            Wk = sb("Wk", [128, KC, D], BF16, sA)
            Wv = sb("Wv", [128, KC, D], BF16, sA)
            Wf = sb("Wf", [128, KC, NH], BF16, sA)
            diag = sb("diag", [128, 32, 128], BF16, sA)
            BDa = sb("BDa", [128, KC, 128], BF16, sA)
            BDx = sb("BDx", [128, KC, 128], BF16, sA)
            with ExitStack() as s0:
                vecs16 = sb("vecs16", [16, D], F32, s0)
                fbrow = sb("fbrow", [1, NH], F32, s0)
                stg = [sb("stg%d" % i, [128, KC, 512], F32, s0) for i in range(2)]
                gB1 = sb("gB1", [128, KC, 512], F32, s0)
                stgf = sb("stgf", [128, KC, NH], F32, s0)
                tmp8 = sb("tmp8", [128, KC], F32, s0)
                psA = ps_("p0a", [128, 512], F32, s0)
                psB = ps_("p0b", [128, 512], F32, s0)

                S.dma("sp", _n(lambda e: [
                    e.dma_start(out=identf[:], in_=ident_d),
                    e.dma_start(out=trif[:], in_=tri_d),
                    e.dma_start(out=selt[:], in_=sel_d),
                    e.dma_start(out=vecs16[:], in_=vecs_d),
                    e.dma_start(out=fbrow[:], in_=fb_d),
                ], 5), writes=["identf", "trif", "selt", "vecs16", "fbrow"], sem="d_const")
                S.op("pool", lambda e: e.memset(onesf[:], 1.0), writes=["onesf"])
                S.op("pool", lambda e: e.memset(onesb[:], 1.0), writes=["onesb"])
                S.op("pool", lambda e: e.memset(BDa[:], 0.0), writes=["BDa"])
                S.op("pool", lambda e: e.memset(BDx[:], 0.0), writes=["BDx"])
                S.dma("pool", _n(lambda e: [
                    e.dma_start(out=BD[half * 64:(half + 1) * 64, :, half * 64:(half + 1) * 64],
                                in_=src.rearrange("(oc two) c d -> two c oc d", two=2)[half])
                    for BD, src in ((BDa, lwa_d), (BDx, lwx_d)) for half in range(2)
                ], 4), reads=[], writes=["BDa", "BDx"], sem="d_const2")
                S.op("dve", lambda e: e.tensor_copy(out=identb[:], in_=identf[:]), reads=["identf"], writes=["identb"])
                for c in range(KC):
                    S.op("pe", lambda e, c=c: e.transpose(out=psA[:, c * 16:(c + 1) * 16], in_=vecs16[:, c * 128:(c + 1) * 128],
                                                           identity=identf[0:16, 0:16]),
                         reads=["vecs16", "identf"], writes=["psA"], sig=(c == KC - 1))
                S.op("dve", lambda e: e.tensor_copy(out=vecT[:].rearrange("p c r -> p (c r)"), in_=psA[:, 0:128]),
                     reads=["psA"], writes=["vecT"])
                S.op("dve", lambda e: e.tensor_scalar(out=dv[:, :, 0], in0=vecT[:, :, V_BA], scalar1=0.5, scalar2=None, op0=ALU.mult),
                     reads=["vecT"], writes=["dv0"])
                S.op("dve", lambda e: e.tensor_scalar(out=dv[:, :, 1], in0=vecT[:, :, V_BX], scalar1=0.5, scalar2=None, op0=ALU.mult),
                     reads=["vecT"], writes=["dv1"])
                S.op("act", lambda e: e.activation(out=tmp8[:], in_=vecT[:, :, V_LAM], func=AF.Exp, scale=-1.0), reads=["vecT"], writes=["tmp8"])
                S.op("act", lambda e: e.activation(out=tmp8[:], in_=tmp8[:], func=AF.Ln, bias=1.0, scale=1.0), reads=["tmp8"], writes=["tmp8"])
                S.op("dve", lambda e: e.tensor_scalar(out=dv[:, :, 2], in0=tmp8[:], scalar1=-4.0, scalar2=None, op0=ALU.mult),
                     reads=["tmp8"], writes=["dv2"])
                S.op("pe", lambda e: e.matmul(psB[:, 0:NH], lhsT=onesf[0:1, :], rhs=fbrow[0:1, :], start=True, stop=True),
                     reads=["onesf", "fbrow"], writes=["psB"])
                S.op("dve", lambda e: e.tensor_copy(out=fbb[:], in_=psB[:, 0:NH]), reads=["psB"], writes=["fbb"])
                for oc in range(KC):
                    for k in range(4):
                        S.op("dve", lambda e, oc=oc, k=k: e.tensor_scalar(out=diag[:, oc * 4 + k, :], in0=identf[:], scalar1=vecT[:, oc, V_CW + k:V_CW + k + 1],
                                                                       scalar2=None, op0=ALU.mult),
                             reads=["identf", "vecT"], writes=["diag"])

                S.op("pool", lambda e: e.memset(gB1[:], 1.0), writes=["gB1"])
                for kc in range(KC):
                    S.op("dve", lambda e, kc=kc: e.tensor_scalar(out=gB1[:, kc, :], in0=gB1[:, kc, :], scalar1=vecT[:, kc, V_G1:V_G1 + 1], scalar2=None, op0=ALU.mult),
                         reads=["gB1", "vecT"], writes=["gB1"])
                for j in range(2):
                    S.op("pool", lambda e, j=j: e.memset(selB[j][:], 1.0), writes=[("selB", j)])
                    S.op("dve", lambda e, j=j: e.tensor_scalar(out=selB[j][:], in0=selB[j][:], scalar1=selt[:, j:j + 1], scalar2=None, op0=ALU.mult),
                         reads=[("selB", j), "selt"], writes=[("selB", j)])
                fwc = [0]

                def fold_weight(dst, src_d, c0, ncols, gidx, tag, q="sp"):
                    for hf in range((ncols + 511) // 512):
                        w = min(512, ncols - hf * 512)
                        sl = fwc[0] % 2
                        fwc[0] += 1
                        S.dma(q, lambda e, sl=sl, hf=hf, w=w: [e.dma_start(
                            out=stg[sl][:, :, 0:w], in_=src_d[:, c0 + hf * 512:c0 + hf * 512 + w].rearrange("(kc p) n -> p kc n", p=128))],
                            writes=[("stg", sl)], sem="d_stg%d" % sl)
                        eng = "dve" if sl == 0 else "pool"
                        S.op(eng, lambda e, sl=sl, hf=hf, w=w: e.tensor_tensor(out=dst[:, :, hf * 512:hf * 512 + w], in0=stg[sl][:, :, 0:w], in1=gB1[:, :, 0:w], op=ALU.mult),
                             reads=[("stg", sl), "gB1"], writes=[tag])

                fold_weight(Wk, win_d, C_K, D, V_G1, "Wk")
                fold_weight(Wv, win_d, C_V, D, V_G1, "Wv")
                fold_weight(Wx, win_d, C_XL, D, V_G1, "Wx")
                fold_weight(Wf, win_d, C_F, NH, V_G1, "Wf")
                S.run("P0")

            with ExitStack() as s1:
                xt = [sb("xt%d" % i, [128, D], F32, s1) for i in range(4)]
                xn = [sb("xn%d" % i, [128, D], BF16, s1) for i in range(4)]
                junk = sb("junk", [128, D], BF16, s1)
                ss = sb("ss", [128, 2, 4], F32, s1)
                rstd = sb("rstd", [128, 2, 4], F32, s1)
                uT = [sb("uT%d" % i, [128, KC, 512], BF16, s1) for i in range(2)]
                kst = sb("kst", [128, NH, 512], BF16, s1)
                vst = sb("vst", [128, 4, D], BF16, s1)
                xl = [sb("xl%d" % i, [128, KC, 515], BF16, s1) for i in range(2)]
                xa = [sb("xa%d" % i, [128, 512], BF16, s1) for i in range(KC)]
                Abuf = sb("Abuf", [128, KC, 512], F32, s1)
                A2buf = sb("A2buf", [128, KC, 512], F32, s1)
                Tbuf = sb("Tbuf", [128, KC, 512], BF16, s1)
                TH = [sb("TH%d" % i, [128, 512], F32, s1) for i in range(2)]
                Hf = [sb("Hf%d" % i, [128, 512], F32, s1) for i in range(2)]
                hst = sb("hst", [128, KC], F32, s1)
                Hsel = sb("Hsel", [128, KC, 512], BF16, s1)
                zt = sb("zt", [128, NH], F32, s1)
                pT = [ps_("pT%d" % i, [128, KC, 128], BF16, s1) for i in range(2)]
                pM = [ps_("pM%d" % i, [128, 512], F32, s1) for i in range(5)]
                pF = ps_("pF", [128, NH], F32, s1)
                pmc = [0, 0]

                def next_pm():
                    k = pmc[0] % 3
                    pmc[0] += 1
                    return k

                def next_pl():
                    k = 3 + pmc[1] % 2
                    pmc[1] += 1
                    return k

                S.op("pool", lambda e: e.memset(xl[0][:, :, 0:3], 0.0), writes=[("xl", 0, oc) for oc in range(KC)])

                def front(t):
                    p = t % 2
                    for tb in range(4):
                        blk = 4 * t + tb
                        S.dma("sp", lambda e, tb=tb, blk=blk: [e.dma_start(out=xt[tb][:], in_=x_d[blk * 128:(blk + 1) * 128, :])],
                              writes=[("xt", tb)], sem="d_xt%d" % tb)
                        S.op("act", lambda e, tb=tb, p=p: e.activation(out=junk[:], in_=xt[tb][:], func=AF.Square, accum_out=ss[:, p, tb:tb + 1]),
                             reads=[("xt", tb)], writes=["junk", ("ss", p, tb)])
                    S.op("dve", lambda e, p=p: e.tensor_scalar(out=rstd[:, p, :], in0=ss[:, p, :], scalar1=1.0 / D, scalar2=EPS, op0=ALU.mult, op1=ALU.add),
                         reads=[("ss", p, tb) for tb in range(4)], writes=[("rstd", p)])
                    S.op("act", lambda e, p=p: e.activation(out=rstd[:, p, :], in_=rstd[:, p, :], func=AF.Sqrt), reads=[("rstd", p)], writes=[("rstd", p)])
                    S.op("dve", lambda e, p=p: e.reciprocal(out=rstd[:, p, :], in_=rstd[:, p, :]), reads=[("rstd", p)], writes=[("rstd", p)])
                    for tb in range(4):
                        S.op("act", lambda e, tb=tb, p=p: e.mul(out=xn[tb][:], in_=xt[tb][:], mul=rstd[:, p, tb:tb + 1]),
                             reads=[("xt", tb), ("rstd", p)], writes=[("xn", tb)])
                    for tb in range(4):
                        q = tb % 2
                        for c in range(KC):
                            S.op("pe", lambda e, q=q, c=c, tb=tb: e.transpose(out=pT[q][:, c, :], in_=xn[tb][:, c * 128:(c + 1) * 128], identity=identb[:]),
                                 reads=[("xn", tb), "identb"], writes=[("pT", q)], sig=(c == KC - 1))
                        S.op("dve", lambda e, q=q, p=p, tb=tb: e.tensor_copy(out=uT[p][:, :, tb * 128:(tb + 1) * 128], in_=pT[q][:]),
                             reads=[("pT", q)], writes=[("uT", p, tb)])

                def mid(t):
                    p = t % 2
                    uTr = [("uT", p, tb) for tb in range(4)]
                    for h in range(NH):
                        k = next_pm()
                        for kc in range(KC):
                            S.op("pe", lambda e, k=k, kc=kc, h=h, p=p: e.matmul(pM[k][:], lhsT=Wk[:, kc, h * 128:(h + 1) * 128], rhs=uT[p][:, kc, :],
                                                                           start=(kc == 0), stop=(kc == KC - 1)),
                                 reads=uTr + ["Wk"], writes=[("pM", k)], sig=(kc == KC - 1))
                        if h % 2 == 0:
                            S.op("act", lambda e, k=k, h=h: e.copy(out=kst[:, h, :], in_=pM[k][:]), reads=[("pM", k)], writes=[("kst", h)])
                        else:
                            S.op("dve", lambda e, k=k, h=h: e.tensor_copy(out=kst[:, h, :], in_=pM[k][:]), reads=[("pM", k)], writes=[("kst", h)])
                    S.dma("sp", lambda e, t=t: [e.dma_start(out=kt_s[:, :, t * 512:(t + 1) * 512], in_=kst[:])],
                          reads=[("kst", h) for h in range(NH)], writes=["kt_s"], sem="d_kst")
                    for tb in range(4):
                        for hf in range(2):
                            k = next_pm()
                            for kc in range(KC):
                                S.op("pe", lambda e, k=k, kc=kc, tb=tb, hf=hf, p=p: e.matmul(pM[k][:], lhsT=uT[p][:, kc, tb * 128:(tb + 1) * 128],
                                                                                        rhs=Wv[:, kc, hf * 512:(hf + 1) * 512], start=(kc == 0), stop=(kc == KC - 1)),
                                     reads=[("uT", p, tb), "Wv"], writes=[("pM", k)], sig=(kc == KC - 1))
                            if hf == 0:
                                S.op("dve", lambda e, k=k, tb=tb, hf=hf: e.tensor_copy(out=vst[:, tb, hf * 512:(hf + 1) * 512], in_=pM[k][:]),
                                     reads=[("pM", k)], writes=[("vst", tb)])
                            else:
                                S.op("act", lambda e, k=k, tb=tb, hf=hf: e.copy(out=vst[:, tb, hf * 512:(hf + 1) * 512], in_=pM[k][:]),
                                     reads=[("pM", k)], writes=[("vst", tb)])
                        for kc in range(KC):
                            S.op("pe", lambda e, kc=kc, tb=tb, p=p: e.matmul(pF[:], lhsT=uT[p][:, kc, tb * 128:(tb + 1) * 128], rhs=Wf[:, kc, :],
                                                                        start=(kc == 0), stop=(kc == KC - 1)),
                                 reads=[("uT", p, tb), "Wf"], writes=["pF"], sig=(kc == KC - 1))
                        S.op("dve", lambda e: e.tensor_tensor(out=zt[:], in0=pF[:], in1=fbb[:], op=ALU.add), reads=["pF", "fbb"], writes=["zt"])
                        S.op("act", lambda e, blk=4 * t + tb: e.activation(out=E[:, blk, :], in_=zt[:], func=AF.Exp, scale=-1.0), reads=["zt"], writes=["E"])
                    S.dma("sp", _n(lambda e, t=t: [e.dma_start(out=v_s[:, :, 4 * t + tb, :].rearrange("h p d -> p h d"),
                                                               in_=vst[:, tb, :].rearrange("p (h d) -> p h d", h=NH)) for tb in range(4)], 4),
                          reads=[("vst", tb) for tb in range(4)], writes=["v_s"], sem="d_vst")
                    for oc in range(KC):
                        k = next_pm()
                        for kc in range(KC):
                            S.op("pe", lambda e, k=k, kc=kc, oc=oc, p=p: e.matmul(pM[k][:], lhsT=Wx[:, kc, oc * 128:(oc + 1) * 128], rhs=uT[p][:, kc, :],
                                                                            start=(kc == 0), stop=(kc == KC - 1)),
                                 reads=uTr + ["Wx"], writes=[("pM", k)], sig=(kc == KC - 1))
                        eng = "act" if oc % 2 == 0 else "dve"
                        if eng == "act":
                            S.op("act", lambda e, k=k, oc=oc, p=p: e.copy(out=xl[p][:, oc, 3:515], in_=pM[k][:]), reads=[("pM", k)], writes=[("xl", p, oc)])
                        else:
                            S.op("dve", lambda e, k=k, oc=oc, p=p: e.tensor_copy(out=xl[p][:, oc, 3:515], in_=pM[k][:]), reads=[("pM", k)], writes=[("xl", p, oc)])

                def lru1(t):
                    p = t % 2
                    for oc in range(KC):
                        k = next_pl()
                        for j in range(4):
                            S.op("pe", lambda e, k=k, j=j, oc=oc, p=p: e.matmul(pM[k][:], lhsT=diag[:, oc * 4 + j, :], rhs=xl[p][:, oc, j:j + 512], start=(j == 0), stop=(j == 3)),
                                 reads=[("xl", p, oc), "diag"], writes=[("pM", k)], sig=(j == 3))
                        S.op("act", lambda e, k=k, oc=oc: e.activation(out=xa[oc][:], in_=pM[k][:], func=AF.Identity, bias=vecT[:, oc, V_CB:V_CB + 1], scale=1.0),
                             reads=[("pM", k), "vecT"], writes=[("xa", oc)])
                        S.op("pool", lambda e, oc=oc, p=p: e.tensor_copy(out=xl[1 - p][:, oc, 0:3], in_=xl[p][:, oc, 512:515]), reads=[("xl", p, oc)], writes=[("xl", 1 - p, oc)])
                    for oc in range(KC):
                        q = oc % 2
                        kr = next_pl()
                        S.op("pe", lambda e, kr=kr, oc=oc: e.matmul(pM[kr][:], lhsT=BDa[:, oc, :], rhs=xa[oc][:], start=True, stop=True),
                             reads=[("xa", oc), "BDa"], writes=[("pM", kr)])
                        ki = next_pl()
                        S.op("pe", lambda e, ki=ki, oc=oc: e.matmul(pM[ki][:], lhsT=BDx[:, oc, :], rhs=xa[oc][:], start=True, stop=True),
                             reads=[("xa", oc), "BDx"], writes=[("pM", ki)])
                        S.op("act", lambda e, kr=kr, oc=oc: e.activation(out=Abuf[:, oc, :], in_=pM[kr][:], func=AF.Tanh, bias=dv[:, oc, 0:1], scale=0.5),
                             reads=[("pM", kr), "dv0"], writes=[("A", oc)])
                        S.op("act", lambda e, ki=ki, oc=oc, q=q: e.activation(out=TH[q][:], in_=pM[ki][:], func=AF.Tanh, bias=dv[:, oc, 1:2], scale=0.5),
                             reads=[("pM", ki), "dv1"], writes=[("TH", q)])
                        S.op("act", lambda e, oc=oc: e.activation(out=Abuf[:, oc, :], in_=Abuf[:, oc, :], func=AF.Exp, bias=dv[:, oc, 2:3], scale=dv[:, oc, 2:3]),
                             reads=[("A", oc), "dv2"], writes=[("A", oc)])
                        S.op("pool", lambda e, oc=oc: e.tensor_tensor(out=A2buf[:, oc, :], in0=Abuf[:, oc, :], in1=Abuf[:, oc, :], op=ALU.mult),
                             reads=[("A", oc)], writes=[("A2", oc)])
                        S.op("dve", lambda e, oc=oc, q=q: e.scalar_tensor_tensor(out=Tbuf[:, oc, :], in0=TH[q][:], scalar=1.0, in1=xa[oc][:], op0=ALU.add, op1=ALU.mult),
                             reads=[("TH", q), ("xa", oc)], writes=[("T", oc)])

                def lru2(t):
                    p = t % 2
                    i = t // 2
                    for oc in range(KC):
                        S.op("act", lambda e, oc=oc: e.activation(out=A2buf[:, oc, :], in_=A2buf[:, oc, :], func=AF.Sqrt, bias=0.25, scale=-0.25),
                             reads=[("A2", oc)], writes=[("A2", oc)])
                    for oc in range(KC):
                        S.op("dve", lambda e, oc=oc: e.tensor_tensor(out=A2buf[:, oc, :], in0=A2buf[:, oc, :], in1=Tbuf[:, oc, :], op=ALU.mult),
                             reads=[("A2", oc), ("T", oc)], writes=[("A2", oc)])
                    for oc in range(KC):
                        q = oc % 2
                        init = 0.0 if t == 0 else hst[:, oc:oc + 1]
                        S.op("dve", lambda e, oc=oc, q=q, init=init: e.tensor_tensor_scan(out=Hf[q][:], data0=Abuf[:, oc, :], data1=A2buf[:, oc, :], initial=init,
                                                                                      op0=ALU.mult, op1=ALU.add),
                             reads=[("A", oc), ("A2", oc), ("hst", oc)], writes=[("Hf", q)])
                        S.op("act", lambda e, oc=oc, q=q: e.copy(out=hst[:, oc:oc + 1], in_=Hf[q][:, 511:512]), reads=[("Hf", q)], writes=[("hst", oc)])
                        if p == 0:
                            S.op("pool", lambda e, oc=oc, q=q: e.tensor_tensor(out=Hsel[:, oc, :], in0=Hf[q][:], in1=selB[0][:], op=ALU.mult),
                                 reads=[("Hf", q), ("selB", 0)], writes=[("Hsel", oc)])
                        else:
                            S.op("pool", lambda e, oc=oc, q=q: e.tensor_tensor(out=Hf[q][:], in0=Hf[q][:], in1=selB[1][:], op=ALU.mult),
                                 reads=[("Hf", q), ("selB", 1)], writes=[("Hf", q)])
                            S.op("pool", lambda e, oc=oc, q=q: e.tensor_tensor(out=Hsel[:, oc, :], in0=Hsel[:, oc, :], in1=Hf[q][:], op=ALU.add),
                                 reads=[("Hf", q), ("Hsel", oc)], writes=[("Hsel", oc)])
                    if p == 1:
                        S.dma("sp", lambda e, i=i: [e.dma_start(out=hs_s[i], in_=Hsel[:])],
                              reads=[("Hsel", oc) for oc in range(KC)], writes=["hs_s"], sem="d_hsel")

                nt_run = NT if stop_after != "A1x" else 2
                def lru(t):
                    lru1(t)
                    lru2(t)

                front(0)
                mid(0)
                front(1)
                for t in range(1, nt_run):
                    S.interleave(S.capture(mid, t), S.capture(front, t + 1) if t + 1 < nt_run else [], S.capture(lru, t - 1), spans=[1.0, 0.55, 1.0])
                lru(nt_run - 1)
                S.run("A1")
                if stop_after in ("A1", "A1x"):
                    S.dma("sp", lambda e: [e.dma_start(out=dbg_d[:, 0:512], in_=E[:].rearrange("p b h -> p (b h)"))], reads=["E"], writes=["dbg"], sem="d_dbg")
                    S.run("dbgA1")
                    return nc

            with ExitStack() as s2:
                Wq = sb("Wq", [128, KC, D], BF16, s2)
                ut2 = [sb("ut2_%d" % i, [128, KC, 512], BF16, s2) for i in range(2)]
                qst = [sb("qst%d" % i, [128, NH, 512], BF16, s2) for i in range(2)]
                pQ = [ps_("pQ%d" % i, [128, 512], F32, s2) for i in range(4)]
                S.dma("pool", _n(lambda e: [e.dma_start(out=Wq[:, :, hf * 512:(hf + 1) * 512],
                                                        in_=win_d[:, C_Q + hf * 512:C_Q + (hf + 1) * 512].rearrange("(kc p) n -> p kc n", p=128)) for hf in range(2)], 2),
                      writes=["Wq"], sem="d_wq")
                xt2 = sb("xtA2", [128, 4, D], F32, s2)
                xnA = [sb("xnA%d" % i, [128, D], BF16, s2) for i in range(2)]
                junkA = sb("junkA", [128, D], BF16, s2)
                ssA = sb("ssA", [128, 4], F32, s2)
                rsA = sb("rsA", [128, 4], F32, s2)
                pTA = [ps_("pTA%d" % i, [128, KC, 128], BF16, s2) for i in range(2)]
                cqc = [0]

                def a2_front(i):
                    sl = i % 2
                    S.dma("sp", lambda e, i=i: [e.dma_start(out=xt2[:], in_=xo_d[i * 512:(i + 1) * 512, :].rearrange("(tb p) c -> p tb c", p=128))],
                          writes=[("xt2", tb) for tb in range(4)], sem="d_xt2")
                    for tb in range(4):
                        S.op("act", lambda e, tb=tb: e.activation(out=junkA[:], in_=xt2[:, tb, :], func=AF.Square, accum_out=ssA[:, tb:tb + 1]),
                             reads=[("xt2", tb)], writes=["junkA", ("ssA", tb)])
                    S.op("dve", lambda e: e.tensor_scalar(out=rsA[:], in0=ssA[:], scalar1=1.0 / D, scalar2=EPS, op0=ALU.mult, op1=ALU.add),
                         reads=[("ssA", tb) for tb in range(4)], writes=["rsA"])
                    S.op("act", lambda e: e.activation(out=rsA[:], in_=rsA[:], func=AF.Sqrt), reads=["rsA"], writes=["rsA"])
                    S.op("dve", lambda e: e.reciprocal(out=rsA[:], in_=rsA[:]), reads=["rsA"], writes=["rsA"])
                    for tb in range(4):
                        q = tb % 2
                        S.op("act", lambda e, tb=tb, q=q: e.mul(out=xnA[q][:], in_=xt2[:, tb, :], mul=rsA[:, tb:tb + 1]),
                             reads=[("xt2", tb), "rsA"], writes=[("xnA", q)])
                        for c in range(KC):
                            S.op("pe", lambda e, q=q, c=c: e.transpose(out=pTA[q][:, c, :], in_=xnA[q][:, c * 128:(c + 1) * 128], identity=identb[:]),
                                 reads=[("xnA", q), "identb"], writes=[("pTA", q)], sig=(c == KC - 1))
                        for c in range(KC):
                            S.op("dve", lambda e, q=q, sl=sl, tb=tb, c=c: e.tensor_scalar(out=ut2[sl][:, c, tb * 128:(tb + 1) * 128], in0=pTA[q][:, c, :],
                                                                                       scalar1=vecT[:, c, V_G1:V_G1 + 1], scalar2=None, op0=ALU.mult),
                                 reads=[("pTA", q), "vecT"], writes=[("ut2", sl, tb, c)])
                    S.dma("sp", lambda e, i=i, sl=sl: [e.dma_start(out=ut_s[i], in_=ut2[sl][:])],
                          reads=[("ut2", sl, tb, c) for tb in range(4) for c in range(KC)], writes=["ut_s"], sem="d_uts%d" % sl)

                def a2_q(i):
                    sl = i % 2
                    for h in range(NH):
                        k = cqc[0] % 4
                        cqc[0] += 1
                        for kc in range(KC):
                            S.op("pe", lambda e, k=k, kc=kc, h=h, sl=sl: e.matmul(pQ[k][:], lhsT=Wq[:, kc, h * 128:(h + 1) * 128], rhs=ut2[sl][:, kc, :],
                                                                            start=(kc == 0), stop=(kc == KC - 1)),
                                 reads=[("ut2", sl, tb, kc) for tb in range(4)] + ["Wq"], writes=[("pQ", k)], sig=(kc == KC - 1))
                        if h % 2 == 0:
                            S.op("act", lambda e, k=k, h=h, sl=sl: e.mul(out=qst[sl][:, h, :], in_=pQ[k][:], mul=QSCALE), reads=[("pQ", k)], writes=[("qst", sl, h)])
                        else:
                            S.op("dve", lambda e, k=k, h=h, sl=sl: e.tensor_scalar(out=qst[sl][:, h, :], in0=pQ[k][:], scalar1=QSCALE, scalar2=None, op0=ALU.mult),
                                 reads=[("pQ", k)], writes=[("qst", sl, h)])
                    S.dma("sp", lambda e, i=i, sl=sl: [e.dma_start(out=q_s[:, :, i * 512:(i + 1) * 512], in_=qst[sl][:])],
                          reads=[("qst", sl, h) for h in range(NH)], writes=["q_s"], sem="d_qst%d" % sl)

                a2_front(0)
                for i in range(NOWN):
                    S.interleave(S.capture(a2_q, i), S.capture(a2_front, i + 1) if i + 1 < NOWN else [])
                S.run("A2")
                if stop_after == "A2":
                    return nc

        with ExitStack() as sB:
            negF = sb("negF", [128, NB, NH], F32, sB)
            Fb = sb("Fb", [128, NOWN, 4, 2, NH], BF16, sB)
            cT = sb("cT", [128, NOWN, 512], BF16, sB)
            maskb = sb("maskb", [128, 8, 512], BF16, sB)
            cselb = sb("cselb", [128, NH, 128], BF16, sB)
            with ExitStack() as sb0:
                tot = sb("tot", [128, NB, NH], F32, sb0)
                inc = sb("inc", [128, NB, NH], F32, sb0)
                ones64 = sb("ones64", [128, NB], F32, sb0)
                pc1 = ps_("pc1", [128, 512], F32, sb0)
                pc2 = ps_("pc2", [128, 512], F32, sb0)
                pX = ps_("pX", [128, 512], BF16, sb0)
                Efl = E[:].rearrange("p b h -> p (b h)")
                S.op("pool", lambda e: e.memset(cT[:], 0.0), writes=["cT"])
                S.op("pool", lambda e: e.memset(cselb[:], 0.0), writes=["cselb"])
                S.op("pool", lambda e: e.memset(ones64[:], 1.0), writes=["ones64"])
                S.dma("pool", _n(lambda e: [
                    e.dma_start(out=maskb[:], in_=masks_d.rearrange("m p q -> p m q")),
                    e.dma_start(out=cselb[0:16, :, :], in_=csel_d),
                ], 2), reads=[], writes=["maskb", "cselb"], sem="d_constB")
                S.op("act", lambda e: e.activation(out=Efl, in_=Efl, func=AF.Ln, bias=1.0, scale=1.0), reads=["E"], writes=["E"])
                S.op("pe", lambda e: e.matmul(pc1[:], lhsT=trif[:], rhs=Efl, start=True, stop=True), reads=["E", "trif"], writes=["pc1"])
                S.op("pe", lambda e: e.matmul(pc2[:], lhsT=onesf[:], rhs=Efl, start=True, stop=True), reads=["E", "onesf"], writes=["pc2"])
                S.op("dve", lambda e: e.tensor_copy(out=tot[:].rearrange("p b h -> p (b h)"), in_=pc2[:]), reads=["pc2"], writes=["tot"])
                for h in range(NH):
                    S.op("dve", lambda e, h=h: e.tensor_tensor_scan(out=inc[:, :, h], data0=ones64[:], data1=tot[:, :, h], initial=0.0, op0=ALU.mult, op1=ALU.add),
                         reads=["tot", "ones64"], writes=[("inc", h)])
                S.op("dve", lambda e: e.tensor_tensor(out=inc[:], in0=inc[:], in1=tot[:], op=ALU.subtract),
                     reads=[("inc", h) for h in range(NH)] + ["tot"], writes=["incx"])
                S.op("dve", lambda e: e.tensor_tensor(out=negF[:].rearrange("p b h -> p (b h)"), in0=pc1[:], in1=inc[:].rearrange("p b h -> p (b h)"), op=ALU.add),
                     reads=["pc1", "incx"], writes=["negF"])
                for i in range(NOWN):
                    S.op("dve", lambda e, i=i: e.tensor_scalar(out=Fb[:, i], in0=negF[:, 8 * i:8 * i + 8, :].rearrange("p (p2 tb) h -> p tb p2 h", p2=2),
                                                            scalar1=-1.0, scalar2=None, op0=ALU.mult),
                         reads=["negF"], writes=[("Fb", i)])
                    for tb in range(4):
                        S.op("pe", lambda e, i=i, tb=tb: e.transpose(out=pX[0:16, tb * 128:(tb + 1) * 128], in_=Fb[:, i, tb].rearrange("p a h -> p (a h)"), identity=identb[:]),
                             reads=[("Fb", i), "identb"], writes=["pX"], sig=(tb == 3))
                    S.op("dve", lambda e, i=i: e.tensor_copy(out=cT[0:16, i, :], in_=pX[0:16, :]), reads=["pX"], writes=["cT"])
                if debug:
                    S.dma("sp", _n(lambda e: [e.dma_start(out=dbg_d[:, 512:1024], in_=negF[:].rearrange("p b h -> p (b h)")),
                                              e.dma_start(out=dbg_d[0:16, 1024:1536], in_=cT[0:16, 0, :].bitcast(F32) if False else negF[0:16, :, :].rearrange("p b h -> p (b h)"))], 2),
                          reads=["negF", "cT"], writes=["dbg"], sem="d_dbg")
                S.run("B0")

            with ExitStack() as sb1:
                KT = [sb("KT%d" % i, [128, SEQ], BF16, sb1) for i in range(2)]
                Vh = [sb("Vh%d" % i, [128, NB, 128], BF16, sb1) for i in range(2)]
                Qh = [sb("Qh%d" % i, [128, OWN], BF16, sb1) for i in range(2)]
                pt = [sb("pt%d" % i, [128, 512], BF16, sb1) for i in range(4)]
                rl = [sb("rl%d" % i, [128, 512], F32, sb1) for i in range(2)]
                acc = [sb("acc%d" % i, [128, 512], F32, sb1) for i in range(2)]
                obst = [sb("obst%d" % i, [128, 512], BF16, sb1) for i in range(2)]
                pS = [ps_("pS%d" % i, [128, 512], F32, sb1) for i in range(4)]
                pO = [ps_("pO%d" % i, [128, 512], F32, sb1) for i in range(2)]
                pL = [ps_("pL%d" % i, [128, 512], F32, sb1) for i in range(2)]

                heads = range(NH) if stop_after != "Bx" else range(2)
                blocks = []
                for h in heads:
                    for i in range(NOWN):
                        for kb in range(8 * i + 8):
                            blocks.append((h, i, kb))
                LA = 3

                def load_head(h):
                    sl = h % 2
                    S.dma("sp", lambda e, h=h, sl=sl: [e.dma_start(out=KT[sl][:], in_=kt_s[:, h, :])], reads=["kt_s"], writes=[("KT", sl)], sem="d_kt%d" % sl)
                    S.dma("sp", lambda e, h=h, sl=sl: [e.dma_start(out=Vh[sl][:], in_=v_s[h])], reads=["v_s"], writes=[("Vh", sl)], sem="d_vh%d" % sl)
                    S.dma("sp", lambda e, h=h, sl=sl: [e.dma_start(out=Qh[sl][:], in_=q_s[:, h, :])], reads=["q_s"], writes=[("Qh", sl)], sem="d_qh%d" % sl)

                NPT = 4

                def col0(i, kb):
                    m = kb - (8 * i + 4)
                    return 128 * m if m > 0 else 0

                def emit_S(n):
                    h, i, kb = blocks[n]
                    sl = h % 2
                    s_ = n % 4
                    pslot = n % NPT
                    diag_blk = kb >= 8 * i
                    c0 = col0(i, kb)
                    S.op("pe", lambda e: e.matmul(pS[s_][:, c0:512], lhsT=KT[sl][:, kb * 128:(kb + 1) * 128], rhs=Qh[sl][:, i * 512 + c0:(i + 1) * 512], start=True, stop=False),
                         reads=[("KT", sl), ("Qh", sl)], writes=[("pS", s_)], sig=False)
                    S.op("pe", lambda e: e.matmul(pS[s_][:, c0:512], lhsT=cselb[:, h, :], rhs=cT[:, i, c0:512], start=False, stop=not diag_blk),
                         reads=["cselb", "cT"], writes=[("pS", s_)], sig=not diag_blk)
                    if diag_blk:
                        S.op("pe", lambda e: e.matmul(pS[s_][:, c0:512], lhsT=identb[:], rhs=maskb[:, kb - 8 * i, c0:512], start=False, stop=True),
                             reads=["identb", "maskb"], writes=[("pS", s_)], sig=True)
                    S.op("act", lambda e: e.activation(out=pt[pslot][:, c0:512], in_=pS[s_][:, c0:512], func=AF.Exp, bias=negF[:, kb, h:h + 1], scale=1.0),
                         reads=[("pS", s_), "negF"], writes=[("pt", pslot)])

                def emit_PV(n):
                    h, i, kb = blocks[n]
                    sl = h % 2
                    pslot = n % NPT
                    nkb = 8 * i + 8
                    o = (h * NOWN + i) % 2
                    last = kb == nkb - 1
                    c0 = col0(i, kb)
                    S.op("pe", lambda e: e.matmul(pO[o][:, c0:512], lhsT=Vh[sl][:, kb, :], rhs=pt[pslot][:, c0:512], start=(kb == 0), stop=last),
                         reads=[("Vh", sl), ("pt", pslot)], writes=[("pO", o)], sig=True)
                    if kb == 0:
                        S.op("dve", lambda e: e.tensor_copy(out=acc[o][:], in_=pt[pslot][:]), reads=[("pt", pslot)], writes=[("acc", o)])
                    elif kb % 4 != 3:
                        S.op("dve", lambda e: e.tensor_tensor(out=acc[o][:, c0:512], in0=acc[o][:, c0:512], in1=pt[pslot][:, c0:512], op=ALU.add),
                             reads=[("pt", pslot), ("acc", o)], writes=[("acc", o)])
                    else:
                        S.op("pe", lambda e: e.matmul(pL[o][:, c0:512], lhsT=onesb[:], rhs=pt[pslot][:, c0:512], start=(kb == 3), stop=False),
                             reads=["onesb", ("pt", pslot)], writes=[("pL", o)], sig=True)
                    if last:
                        S.op("pe", lambda e: e.matmul(pL[o][:], lhsT=onesf[:], rhs=acc[o][:], start=False, stop=True), reads=["onesf", ("acc", o)], writes=[("pL", o)])
                        S.op("dve", lambda e: e.reciprocal(out=rl[o][:], in_=pL[o][:]), reads=[("pL", o)], writes=[("rl", o)])
                        S.op("dve", lambda e: e.tensor_tensor(out=obst[o][:], in0=pO[o][:], in1=rl[o][:], op=ALU.mult), reads=[("pO", o), ("rl", o)], writes=[("obst", o)])
                        S.dma("sp", lambda e: [e.dma_start(out=ob_s[i, :, h, :], in_=obst[o][:])], reads=[("obst", o)], writes=["ob_s"], sem="d_ob%d" % o)

                cvtC = [sb("cvtC%d" % i, [128, KC, 512], BF16, sb1) for i in range(3)]
                wcnt = [0]

                def prep_weight(src_d, r0, c0, dsts):
                    sl = wcnt[0] % 3
                    wcnt[0] += 1
                    S.dma("pool", lambda e: [e.dma_start(out=cvtC[sl][:], in_=src_d[r0:r0 + 1024, c0:c0 + 512].rearrange("(kc p) n -> p kc n", p=128))],
                          writes=[("cvtC", sl)], sem="d_cvl%d" % sl)
                    S.dma("pool", _n(lambda e: [e.dma_start(out=d_ap, in_=cvtC[sl][:, :, lo:hi]) for (d_ap, lo, hi) in dsts], len(dsts)),
                          reads=[("cvtC", sl)], writes=["wscr"], sem="d_cvs%d" % sl)

                def halves(scr, j):
                    return [(scr[2 * j], 0, 256), (scr[2 * j + 1], 256, 512)]

                if stop_after not in ("B", "Bx"):
                    for hf in range(2):
                        prep_weight(win_d, 0, C_GL + hf * 512, halves(wgl_s, hf))
                    for hf in range(4):
                        prep_weight(win_d, 0, C_GA + hf * 512, halves(wgg_s, hf))
                    for (dst, src) in ((wa_s, wa_d), (wb_s, wb_d), (wo_s, wo_d)):
                        for hf in range(2):
                            prep_weight(src, 0, hf * 512, halves(dst, hf))
                    for hf in range(8):
                        prep_weight(wup_d, 0, hf * 512, [(wup_s[hf], 0, 512)])
                    for g in range(4):
                        for hf in range(2):
                            prep_weight(wdn_d, g * 1024, hf * 512, [(wdn_s[hf * 4 + g], 0, 512)])

                hl = list(heads)
                load_head(hl[0])
                for n in range(len(blocks) + LA):
                    if n < len(blocks):
                        emit_S(n)
                    if n >= LA:
                        emit_PV(n - LA)
                        h, i, kb = blocks[n - LA]
                        if i == 0 and kb == 0 and hl.index(h) + 1 < len(hl):
                            load_head(hl[hl.index(h) + 1])
                S.run("B")
                if stop_after in ("B", "Bx", "Bp"):
                    return nc

        with ExitStack() as sC:
            NW = 3
            NWA = 8
            wst = [sb("wst%d" % i, [128, KC, 512], BF16, sC) for i in range(NW)]
            wsa = [sb("wsa%d" % i, [128, KC, 256], BF16, sC) for i in range(NWA)]
            g3b = sb("g3b", [128, D], F32, sC)
            xtC = [sb("xtC%d" % i, [128, 4, D], F32, sC) for i in range(2)]
            utC = sb("utC", [128, KC, 512], BF16, sC)
            hsC = sb("hsC", [128, KC, 512], BF16, sC)
            obC = sb("obC", [128, NH, 512], BF16, sC)
            gatC = sb("gatC", [128, KC, 512], BF16, sC)
            mT = sb("mT", [128, KC, 512], BF16, sC)
            m2T = sb("m2T", [128, KC, 512], BF16, sC)
            hT = sb("hT", [128, HC, 512], BF16, sC)
            xn2 = [sb("xn2_%d" % i, [128, D], BF16, sC) for i in range(2)]
            junk2 = sb("junk2", [128, D], BF16, sC)
            ggt = [sb("ggt%d" % i, [128, 512], BF16, sC) for i in range(2)]
            sg = [sb("sg%d" % i, [128, 512], F32, sC) for i in range(4)]
            rr = [sb("rr%d" % i, [128, 512], BF16, sC) for i in range(2)]
            ssC = sb("ssC", [128, 2, 2, 4], F32, sC)
            rsC = sb("rsC", [128, 2, 2, 4], F32, sC)
            pC = [ps_("pC%d" % i, [128, 512], F32, sC) for i in range(7)]
            pT2 = ps_("pT2", [128, KC, 128], BF16, sC)
            pcc = [0]
            wsc = [0]

            def next_pc():
                k = pcc[0] % 3
                pcc[0] += 1
                return k

            pcc2 = [0]

            def next_pc2():
                k = 3 + pcc2[0] % 4
                pcc2[0] += 1
                return k

            def load_w(src_ap):
                sl = wsc[0] % NW
                wsc[0] += 1
                S.dma("sp", lambda e: [e.dma_start(out=wst[sl][:], in_=src_ap)], reads=["wscr"], writes=[("wst", sl)], sem="d_wst%d" % sl)
                return sl

            wsca = [0]

            def load_wa(scr, c0):
                sl = wsca[0] % NWA
                wsca[0] += 1
                S.dma("sp", lambda e: [e.dma_start(out=wsa[sl][:], in_=scr[c0 // 256])],
                      reads=["wscr"], writes=[("wsa", sl)], sem="d_wsa%d" % sl)
                return sl

            def kcview(scr, r0, c0):
                return scr[r0:r0 + 1024, c0:c0 + 512].rearrange("(kc p) n -> p kc n", p=128)

            S.dma("sp", lambda e: [e.dma_start(out=xtC[1][0:1, 0, :], in_=g3_d)], writes=[("xtC", 1, 0)], sem="d_constC")
            for hf in range(2):
                k = next_pc()
                S.op("pe", lambda e, hf=hf, k=k: e.matmul(pC[k][:], lhsT=onesf[0:1, :], rhs=xtC[1][0:1, 0, hf * 512:(hf + 1) * 512], start=True, stop=True),
                     reads=["onesf", ("xtC", 1, 0)], writes=[("pC", k)])
                S.op("dve", lambda e, hf=hf, k=k: e.tensor_copy(out=g3b[:, hf * 512:(hf + 1) * 512], in_=pC[k][:]), reads=[("pC", k)], writes=["g3b"])

            def rms_stats(xs, slot):
                for tb in range(4):
                    S.op("act", lambda e, tb=tb: e.activation(out=junk2[:], in_=xtC[xs][:, tb, :], func=AF.Square, accum_out=ssC[:, xs, slot, tb:tb + 1]),
                         reads=[("xtC", xs, tb)], writes=["junk2", ("ssC", xs, slot, tb)])
                S.op("dve", lambda e: e.tensor_scalar(out=rsC[:, xs, slot, :], in0=ssC[:, xs, slot, :], scalar1=1.0 / D, scalar2=EPS, op0=ALU.mult, op1=ALU.add),
                     reads=[("ssC", xs, slot, tb) for tb in range(4)], writes=[("rsC", xs, slot)])
                S.op("act", lambda e: e.activation(out=rsC[:, xs, slot, :], in_=rsC[:, xs, slot, :], func=AF.Sqrt), reads=[("rsC", xs, slot)], writes=[("rsC", xs, slot)])
                S.op("dve", lambda e: e.reciprocal(out=rsC[:, xs, slot, :], in_=rsC[:, xs, slot, :]), reads=[("rsC", xs, slot)], writes=[("rsC", xs, slot)])

            def stage1(i):
                xs = i % 2
                X = xtC[xs]
                tsl = slice(i * 512, (i + 1) * 512)
                S.dma("sp", _n(lambda e: [
                    e.dma_start(out=utC[:], in_=ut_s[i]),
                    e.dma_start(out=hsC[:], in_=hs_s[i]),
                ], 2), reads=["ut_s", "hs_s"], writes=["utC", "hsC"], sem="d_actC")
                for oc in range(KC):
                    if oc % 2 == 0:
                        wsl = load_wa(wgl_s, (oc // 2) * 256)
                    k = next_pc()
                    for kc in range(KC):
                        S.op("pe", lambda e, k=k, kc=kc, oc=oc, wsl=wsl: e.matmul(pC[k][:], lhsT=wsa[wsl][:, kc, (oc % 2) * 128:(oc % 2 + 1) * 128], rhs=utC[:, kc, :],
                                                                            start=(kc == 0), stop=(kc == KC - 1)),
                             reads=[("wsa", wsl), "utC"], writes=[("pC", k)], sig=(kc == KC - 1))
                    q = oc % 2
                    S.op("act", lambda e, k=k, q=q: e.activation(out=ggt[q][:], in_=pC[k][:], func=AF.Gelu), reads=[("pC", k)], writes=[("ggt", q)])
                    S.op("pool", lambda e, oc=oc, q=q: e.tensor_tensor(out=gatC[:, oc, :], in0=ggt[q][:], in1=hsC[:, oc, :], op=ALU.mult),
                         reads=[("ggt", q), "hsC"], writes=[("gatC", oc)])
                S.dma("sp", lambda e: [e.dma_start(out=obC[:], in_=ob_s[i])], reads=["ob_s"], writes=["obC"], sem="d_actC2")
                gatR = [("gatC", oc) for oc in range(KC)]
                for oc in range(KC):
                    if oc % 2 == 0:
                        wa_ = load_wa(wa_s, (oc // 2) * 256)
                        wb_ = load_wa(wb_s, (oc // 2) * 256)
                        wga = load_wa(wgg_s, (oc // 2) * 256)
                        wgb = load_wa(wgg_s, 1024 + (oc // 2) * 256)
                    csl = slice((oc % 2) * 128, (oc % 2 + 1) * 128)
                    qa, qb = (2 * oc) % 4, (2 * oc + 1) % 4
                    for (wy, rhs_y, rtok, wg, qq) in ((wa_, gatC, gatR, wga, qa), (wb_, obC, ["obC"], wgb, qb)):
                        ky, kg = next_pc(), next_pc()
                        for kc in range(KC):
                            S.op("pe", lambda e, kc=kc, ky=ky, wy=wy, rhs_y=rhs_y, csl=csl: e.matmul(pC[ky][:], lhsT=wsa[wy][:, kc, csl], rhs=rhs_y[:, kc, :],
                                                                                             start=(kc == 0), stop=(kc == KC - 1)),
                                 reads=list(rtok) + [("wsa", wy)], writes=[("pC", ky)], sig=(kc == KC - 1))
                        for kc in range(KC):
                            S.op("pe", lambda e, kc=kc, kg=kg, wg=wg, csl=csl: e.matmul(pC[kg][:], lhsT=wsa[wg][:, kc, csl], rhs=utC[:, kc, :],
                                                                                  start=(kc == 0), stop=(kc == KC - 1)),
                                 reads=["utC", ("wsa", wg)], writes=[("pC", kg)], sig=(kc == KC - 1))
                        S.op("act", lambda e, kg=kg, qq=qq: e.activation(out=sg[qq][:], in_=pC[kg][:], func=AF.Sigmoid), reads=[("pC", kg)], writes=[("sg", qq)])
                        S.op("dve", lambda e, ky=ky, qq=qq: e.tensor_tensor(out=sg[qq][:], in0=pC[ky][:], in1=sg[qq][:], op=ALU.mult), reads=[("pC", ky), ("sg", qq)], writes=[("sg", qq)])
                    S.op("pool", lambda e, oc=oc, qa=qa, qb=qb: e.tensor_tensor(out=mT[:, oc, :], in0=sg[qa][:], in1=sg[qb][:], op=ALU.add),
                         reads=[("sg", qa), ("sg", qb)], writes=[("mT", oc)])
                S.dma("sp", lambda e: [e.dma_start(out=X[:], in_=xo_d[i * 512:(i + 1) * 512, :].rearrange("(tb p) c -> p tb c", p=128))],
                      writes=[("xtC", xs, tb) for tb in range(4)], sem="d_actC3_%d" % xs)
                mR = [("mT", oc) for oc in range(KC)]
                for qt in range(4):
                    wo_ = load_wa(wo_s, qt * 256)
                    for tb in range(4):
                        k = next_pc()
                        for kc in range(KC):
                            S.op("pe", lambda e, k=k, kc=kc, tb=tb, wo_=wo_: e.matmul(pC[k][:, 0:256], lhsT=mT[:, kc, tb * 128:(tb + 1) * 128], rhs=wsa[wo_][:, kc, :],
                                                                                start=(kc == 0), stop=(kc == KC - 1)),
                                 reads=mR + [("wsa", wo_)], writes=[("pC", k)], sig=(kc == KC - 1))
                        S.op("dve", lambda e, k=k, tb=tb, qt=qt: e.tensor_tensor(out=X[:, tb, qt * 256:(qt + 1) * 256], in0=X[:, tb, qt * 256:(qt + 1) * 256], in1=pC[k][:, 0:256], op=ALU.add),
                             reads=[("pC", k), ("xtC", xs, tb)], writes=[("xtC", xs, tb)])

            def stage2_norm(i):
                xs = i % 2
                X = xtC[xs]
                rms_stats(xs, 0)
                for tb in range(4):
                    q = tb % 2
                    S.op("act", lambda e, tb=tb, q=q: e.mul(out=xn2[q][:], in_=X[:, tb, :], mul=rsC[:, xs, 0, tb:tb + 1]),
                         reads=[("xtC", xs, tb), ("rsC", xs, 0)], writes=[("xn2", q)])
                    for c in range(KC):
                        S.op("pe", lambda e, q=q, c=c: e.transpose(out=pT2[:, c, :], in_=xn2[q][:, c * 128:(c + 1) * 128], identity=identb[:]),
                             reads=[("xn2", q), "identb"], writes=["pT2"], sig=(c == KC - 1))
                    for c in range(KC):
                        S.op("dve", lambda e, tb=tb, c=c: e.tensor_scalar(out=m2T[:, c, tb * 128:(tb + 1) * 128], in0=pT2[:, c, :], scalar1=vecT[:, c, V_G2:V_G2 + 1],
                                                                        scalar2=None, op0=ALU.mult),
                             reads=["pT2", "vecT"], writes=[("m2T", tb, c)])

            def stage2(i):
                xs = i % 2
                X = xtC[xs]
                chunks = [wup_s[c] for c in range(8)]
                chunks += [wdn_s[hf * 4 + g] for hf in range(2) for g in range(4)]
                slot_of = {}

                def use_chunk(n):
                    for m in (n, n + 1, n + 2):
                        if m < len(chunks) and m not in slot_of:
                            slot_of[m] = load_w(chunks[m])
                    return slot_of[n]

                for m in (0, 1):
                    slot_of[m] = load_w(chunks[m])
                m2R = [("m2T", tb, c) for tb in range(4) for c in range(KC)]
                for hc in range(HC):
                    if hc % 4 == 0:
                        wsl = use_chunk(hc // 4)
                    k = next_pc2()
                    for kc in range(KC):
                        S.op("pe", lambda e, k=k, kc=kc, hc=hc, wsl=wsl: e.matmul(pC[k][:], lhsT=wst[wsl][:, kc, (hc % 4) * 128:(hc % 4 + 1) * 128], rhs=m2T[:, kc, :],
                                                                            start=(kc == 0), stop=(kc == KC - 1)),
                             reads=m2R + [("wst", wsl)], writes=[("pC", k)], sig=(kc == KC - 1))
                    q = hc % 2
                    S.op("act", lambda e, k=k, q=q: e.activation(out=rr[q][:], in_=pC[k][:], func=AF.Relu), reads=[("pC", k)], writes=[("rr", q)])
                    S.op("pool", lambda e, hc=hc, q=q: e.tensor_tensor(out=hT[:, hc, :], in0=rr[q][:], in1=rr[q][:], op=ALU.mult), reads=[("rr", q)], writes=[("hT", hc)])
                for hf in range(2):
                    kd = [next_pc2() for _ in range(4)]
                    for g in range(4):
                        wsl = use_chunk(8 + hf * 4 + g)
                        for h8 in range(8):
                            hc = g * 8 + h8
                            for tb in range(4):
                                last = (g == 3 and h8 == 7)
                                S.op("pe", lambda e, hc=hc, h8=h8, tb=tb, wsl=wsl, k=kd[tb], first=(g == 0 and h8 == 0), last=last: e.matmul(
                                    pC[k][:], lhsT=hT[:, hc, tb * 128:(tb + 1) * 128], rhs=wst[wsl][:, h8, :], start=first, stop=last),
                                    reads=[("hT", hc), ("wst", wsl)], writes=[("pC", kd[tb])], sig=last)
                    for tb in range(4):
                        S.op("dve", lambda e, k=kd[tb], tb=tb, hf=hf: e.tensor_tensor(out=X[:, tb, hf * 512:(hf + 1) * 512], in0=X[:, tb, hf * 512:(hf + 1) * 512], in1=pC[k][:], op=ALU.add),
                             reads=[("pC", kd[tb]), ("xtC", xs, tb)], writes=[("xtC", xs, tb)])
                rms_stats(xs, 1)
                for tb in range(4):
                    S.op("dve", lambda e, tb=tb: e.scalar_tensor_tensor(out=X[:, tb, :], in0=X[:, tb, :], scalar=rsC[:, xs, 1, tb:tb + 1], in1=g3b[:], op0=ALU.mult, op1=ALU.mult),
                         reads=[("xtC", xs, tb), ("rsC", xs, 1), "g3b"], writes=[("xtC", xs, tb)])
                S.dma("pool", lambda e: [e.dma_start(out=out_d[i * 512:(i + 1) * 512, :].rearrange("(tb p) c -> p tb c", p=128), in_=X[:])],
                      reads=[("xtC", xs, tb) for tb in range(4)], writes=["out"], sem="d_out%d" % xs)

            n_own = NOWN if stop_after != "Cx" else 1
            def s1_then_norm(i):
                stage1(i)
                stage2_norm(i)

            s1_then_norm(0)
            for i in range(n_own):
                S.interleave(S.capture(stage2, i), S.capture(s1_then_norm, i + 1) if i + 1 < n_own else [], spans=[1.0, 0.85])
            S.run("C")
    return nc


_NC_CACHE = {}


def _prep_inputs(inp, ncores=8):
    f = lambda a: np.ascontiguousarray(np.asarray(a, dtype=np.float32))
    x = f(inp["x"])
    vecs = np.zeros((16, D), np.float32)
    vecs[0:4] = f(inp["conv_w"])
    vecs[4] = f(inp["conv_b"])
    vecs[5] = f(inp["lru_ba"])
    vecs[6] = f(inp["lru_bx"])
    vecs[7] = f(inp["lru_lambda"])
    vecs[8] = f(inp["norm_mix_g"])
    vecs[9] = f(inp["norm_mlp_g"])
    vecs[10] = f(inp["norm_final_g"])
    ident = np.eye(128, dtype=np.float32)
    tri = np.triu(np.ones((128, 128), np.float32))
    common = {
        "w_in": f(inp["w_in"]), "vecs": vecs, "g3row": f(inp["norm_final_g"])[None, :],
        "lru_wa": f(inp["lru_wa"]), "lru_wx": f(inp["lru_wx"]), "forget_b": f(inp["forget_b"])[None, :],
        "w_branch_a": f(inp["w_branch_a"]), "w_branch_b": f(inp["w_branch_b"]), "w_out": f(inp["w_out"]),
        "w_up": f(inp["w_up"]), "w_down": f(inp["w_down"]), "ident": ident, "tri": tri,
    }
    maps = []
    for c in range(ncores):
        b, j = c // 2, c % 2
        xb = x[b]
        xo = np.ascontiguousarray(xb.reshape(NT, 512, D)[j::2].reshape(OWN, D))
        sel = np.zeros((128, 2), np.float32)
        sel[:, j] = 1.0
        kpos = np.arange(8)[:, None, None] * 128 + np.arange(128)[None, :, None]
        qpos = j * 512 + np.arange(512)[None, None, :]
        masks = np.where(kpos <= qpos, 0.0, MASKV).astype(np.float32)
        csel = np.zeros((16, NH, 128), np.float32)
        for h in range(NH):
            csel[j * 8 + h, h, :] = 1.0
        m = dict(common)
        m.update({"x": np.ascontiguousarray(xb), "xo": xo, "sel": sel, "masks": masks, "csel": csel})
        maps.append(m)
    return maps


def kernel(**inputs):
    if "nc" not in _NC_CACHE:
        _NC_CACHE["nc"] = build_nc()
    nc = _NC_CACHE["nc"]
    maps = _prep_inputs(inputs, 8)
    res = run_bass_kernel_spmd(nc, maps, core_ids=list(range(8)))
    B = inputs["x"].shape[0]
    out = np.empty((B, SEQ, D), np.float32)
    ov = out.reshape(B, NT, 512, D)
    for c in range(8):
        b, j = c // 2, c % 2
        ov[b, j::2] = np.asarray(res.results[c]["out"], dtype=np.float32).reshape(NOWN, 512, D)
    return out
```
